# Optimizing a Trainium2 kernel written in Bass

```python
import math
import jax, jax.numpy as jnp
from jax import lax
import numpy as np

D_MODEL = 1024
BATCH = 8
SEQ = 4096
DEPTH = 1

MLA_HEADS = 8
MLA_NOPE = 128
MLA_ROPE = 64
MLA_V = 128
Q_LORA = 384
KV_LORA = 256
ROPE_THETA = 10000.0
SWA_HEADS = 8
SWA_KV_HEADS = 2
SWA_HEAD_DIM = 128
WINDOW = 128
BLOCK = 128
N_BUCKETS = 32
MAX_DISTANCE = 128
D_FF = 4 * D_MODEL
N_BRANCHES = 2
N_MOD = 6
EPS = 1e-6
NEG_INF = -1e30

SPLITS = (Q_LORA, KV_LORA, MLA_ROPE,
          SWA_HEADS * SWA_HEAD_DIM, SWA_KV_HEADS * SWA_HEAD_DIM, SWA_KV_HEADS * SWA_HEAD_DIM,
          N_BRANCHES * D_MODEL)
D_IN = sum(SPLITS)

kernel_name = "hybrid_mla_swa_gated_encoder_block"


def rmsnorm(x, g):
    xf = x.astype(jnp.float32)
    y = xf * lax.rsqrt(jnp.mean(xf * xf, axis=-1, keepdims=True) + EPS) * g.astype(jnp.float32)
    return y.astype(x.dtype)


def rope_angles(pos, dim):
    inv = ROPE_THETA ** (-jnp.arange(0, dim, 2, dtype=jnp.float32) / dim)
    ang = pos.astype(jnp.float32)[..., None] * inv
    return jnp.cos(ang), jnp.sin(ang)


def apply_rope(x, cos, sin):
    half = x.shape[-1] // 2
    x1 = x[..., :half].astype(jnp.float32)
    x2 = x[..., half:].astype(jnp.float32)
    out = jnp.concatenate([x1 * cos - x2 * sin, x2 * cos + x1 * sin], axis=-1)
    return out.astype(x.dtype)


def t5_bucket(rel):
    half = N_BUCKETS // 2
    max_exact = half // 2
    ret = jnp.where(rel > 0, half, 0)
    n = jnp.abs(rel)
    nf = jnp.maximum(n, 1).astype(jnp.float32)
    large = max_exact + (jnp.log(nf / max_exact) / math.log(MAX_DISTANCE / max_exact)
                         * (half - max_exact)).astype(jnp.int32)
    large = jnp.minimum(large, half - 1)
    return ret + jnp.where(n < max_exact, n, large)


def mla_branch(cq, ckv, kr, pos, q_norm, w_uq, kv_norm, w_ukv):
    B, S, _ = cq.shape
    nb = S // BLOCK
    q = (rmsnorm(cq, q_norm) @ w_uq).reshape(B, S, MLA_HEADS, MLA_NOPE + MLA_ROPE)
    q_nope, q_rope = q[..., :MLA_NOPE], q[..., MLA_NOPE:]
    kv = (rmsnorm(ckv, kv_norm) @ w_ukv).reshape(B, S, MLA_HEADS, MLA_NOPE + MLA_V)
    k_nope, v = kv[..., :MLA_NOPE], kv[..., MLA_NOPE:]
    cos, sin = rope_angles(pos, MLA_ROPE)
    q_rope = apply_rope(q_rope, cos[:, :, None], sin[:, :, None])
    k_rope = apply_rope(kr, cos, sin)
    scale = (MLA_NOPE + MLA_ROPE) ** -0.5
    qn_b = q_nope.reshape(B, nb, BLOCK, MLA_HEADS, MLA_NOPE).transpose(1, 0, 2, 3, 4)
    qr_b = q_rope.reshape(B, nb, BLOCK, MLA_HEADS, MLA_ROPE).transpose(1, 0, 2, 3, 4)

    def attend(args):
        qn, qr = args
        s = (jnp.einsum('bqhd,bkhd->bhqk', qn, k_nope)
             + jnp.einsum('bqhr,bkr->bhqk', qr, k_rope))
        p = jax.nn.softmax(s.astype(jnp.float32) * scale, axis=-1).astype(v.dtype)
        return jnp.einsum('bhqk,bkhd->bqhd', p, v)

    o = lax.map(attend, (qn_b, qr_b))
    return o.transpose(1, 0, 2, 3, 4).reshape(B, S, MLA_HEADS * MLA_V)


def swa_branch(q, k, v, pos, rel_bias, sink):
    B, S, _ = q.shape
    nb = S // BLOCK
    G = SWA_HEADS // SWA_KV_HEADS
    span = BLOCK + 2 * WINDOW
    q = q.reshape(B, S, SWA_KV_HEADS, G, SWA_HEAD_DIM) * (SWA_HEAD_DIM ** -0.5)
    k = k.reshape(B, S, SWA_KV_HEADS, SWA_HEAD_DIM)
    v = v.reshape(B, S, SWA_KV_HEADS, SWA_HEAD_DIM)
    pad = ((0, 0), (WINDOW, WINDOW), (0, 0), (0, 0))
    kp = jnp.pad(k, pad)
    vp = jnp.pad(v, pad)
    posp = jnp.pad(pos, ((0, 0), (WINDOW, WINDOW)))
    validp = jnp.pad(jnp.ones((S,), dtype=bool), (WINDOW, WINDOW))
    q_b = q.reshape(B, nb, BLOCK, SWA_KV_HEADS, G, SWA_HEAD_DIM).transpose(1, 0, 2, 3, 4, 5)
    pq_b = pos.reshape(B, nb, BLOCK).transpose(1, 0, 2)
    starts = jnp.arange(nb, dtype=jnp.int32) * BLOCK
    bias_tab = rel_bias.astype(jnp.float32)
    sink_f = sink.astype(jnp.float32).reshape(1, SWA_KV_HEADS, G, 1, 1)

    def attend(args):
        qb, pqb, start = args
        kb = lax.dynamic_slice_in_dim(kp, start, span, axis=1)
        vb = lax.dynamic_slice_in_dim(vp, start, span, axis=1)
        pk = lax.dynamic_slice_in_dim(posp, start, span, axis=1)
        ok = lax.dynamic_slice_in_dim(validp, start, span, axis=0)
        rel = pk[:, None, :] - pqb[:, :, None]
        mask = ok[None, None, :] & (jnp.abs(rel) <= WINDOW)
        bias = bias_tab[t5_bucket(rel)]
        bias = bias.transpose(0, 3, 1, 2).reshape(B, SWA_KV_HEADS, G, BLOCK, span)
        s = jnp.einsum('bqngd,bknd->bngqk', qb, kb).astype(jnp.float32) + bias
        s = jnp.where(mask[:, None, None], s, NEG_INF)
        sk = jnp.broadcast_to(sink_f, (B, SWA_KV_HEADS, G, BLOCK, 1))
        p = jax.nn.softmax(jnp.concatenate([s, sk], axis=-1), axis=-1)[..., :-1]
        return jnp.einsum('bngqk,bknd->bqngd', p.astype(vb.dtype), vb)

    o = lax.map(attend, (q_b, pq_b, starts))
    return o.transpose(1, 0, 2, 3, 4, 5).reshape(B, S, SWA_HEADS * SWA_HEAD_DIM)


def setup_inputs(seed: int = 0) -> dict:
    key = jax.random.key(seed)
    ks = jax.random.split(key, 24)
    D = D_MODEL
    f32 = jnp.float32

    def w(k, shape, fan_in, gain=1.0):
        return jax.random.normal(k, shape, f32) * (gain * fan_in ** -0.5)

    def gain(k, shape):
        return 1.0 + 0.01 * jax.random.normal(k, shape, f32)

    x = jax.random.normal(ks[0], (BATCH, SEQ, D), f32)
    c = jax.random.normal(ks[1], (BATCH, D), f32)
    offset = jax.random.randint(ks[2], (BATCH, 1), 0, 1024, dtype=jnp.int32)
    positions = offset + jnp.arange(SEQ, dtype=jnp.int32)[None, :]
    return {
        "x": x,
        "c": c,
        "positions": positions,
        "w_ada": w(ks[3], (DEPTH, D, N_MOD * D), D, 0.1),
        "b_ada": 0.01 * jax.random.normal(ks[4], (DEPTH, N_MOD * D), f32),
        "norm_mix": gain(ks[5], (DEPTH, D)),
        "w_in": w(ks[6], (DEPTH, D, D_IN), D),
        "q_norm": gain(ks[7], (DEPTH, Q_LORA)),
        "w_uq": w(ks[8], (DEPTH, Q_LORA, MLA_HEADS * (MLA_NOPE + MLA_ROPE)), Q_LORA),
        "kv_norm": gain(ks[9], (DEPTH, KV_LORA)),
        "w_ukv": w(ks[10], (DEPTH, KV_LORA, MLA_HEADS * (MLA_NOPE + MLA_V)), KV_LORA),
        "rel_bias": 0.5 * jax.random.normal(ks[11], (N_BUCKETS, SWA_HEADS), f32),
        "sink": 0.5 * jax.random.normal(ks[12], (DEPTH, SWA_HEADS), f32),
        "w_o_mla": w(ks[13], (DEPTH, MLA_HEADS * MLA_V, D), MLA_HEADS * MLA_V),
        "w_o_swa": w(ks[14], (DEPTH, SWA_HEADS * SWA_HEAD_DIM, D), SWA_HEADS * SWA_HEAD_DIM),
        "w_out": w(ks[15], (DEPTH, D, D), D),
        "norm_mlp": gain(ks[16], (DEPTH, D)),
        "w_ff1": w(ks[17], (DEPTH, D, D_FF), D),
        "w_ff2": w(ks[18], (DEPTH, D_FF, D), D_FF),
        "norm_final": gain(ks[19], (D,)),
    }


def reference(x, c, positions, w_ada, b_ada, norm_mix, w_in, q_norm, w_uq, kv_norm, w_ukv,
              rel_bias, sink, w_o_mla, w_o_swa, w_out, norm_mlp, w_ff1, w_ff2, norm_final):
    B, S, D = x.shape
    split_idx = [int(i) for i in np.cumsum(SPLITS)[:-1]]
    c_act = jax.nn.silu(c)
    for l in range(DEPTH):
        mod = c_act @ w_ada[l] + b_ada[l]
        sh1, sc1, g1, sh2, sc2, g2 = jnp.split(mod, N_MOD, axis=-1)

        h = rmsnorm(x, norm_mix[l]) * (1.0 + sc1[:, None, :]) + sh1[:, None, :]
        proj = h @ w_in[l]
        cq, ckv, kr, qs, ks_, vs, gates = jnp.split(proj, split_idx, axis=-1)
        y_a = mla_branch(cq, ckv, kr, positions, q_norm[l], w_uq[l], kv_norm[l], w_ukv[l]) @ w_o_mla[l]
        y_b = swa_branch(qs, ks_, vs, positions, rel_bias, sink[l]) @ w_o_swa[l]
        gates = jax.nn.sigmoid(gates.astype(jnp.float32)).astype(x.dtype).reshape(B, S, N_BRANCHES, D)
        merged = gates[:, :, 0] * y_a + gates[:, :, 1] * y_b
        x = x + g1[:, None, :] * (merged @ w_out[l])

        h = rmsnorm(x, norm_mlp[l]) * (1.0 + sc2[:, None, :]) + sh2[:, None, :]
        ff = jnp.square(jax.nn.relu(h @ w_ff1[l])) @ w_ff2[l]
        x = x + g2[:, None, :] * ff
    return rmsnorm(x, norm_final)
```

```python
import math
import contextlib
import numpy as np
import concourse.bass as bass
import concourse.mybir as mybir
from concourse.bass_utils import run_bass_kernel_spmd

F32 = mybir.dt.float32
BF16 = mybir.dt.bfloat16
I32 = mybir.dt.int32
AF = mybir.ActivationFunctionType
ALU = mybir.AluOpType

D = 1024
SEQ = 4096
NT = SEQ // 128
NB = SEQ // 512
D_IN = 4288
EPS = 1e-6
MLA_SCALE = 192 ** -0.5
SWA_SCALE = 128 ** -0.5
NEG = -30000.0
TWO_PI = 2.0 * math.pi
C1 = 6.28125
C2 = TWO_PI - C1
PI_SAFE = 3.1415925
N_CORES = 8

DEBUG = {}


class Buf:
    __slots__ = ("name", "w", "r")

    def __init__(self, name=""):
        self.name = name
        self.w = None
        self.r = {}


class Sched:
    ENGS = ("pe", "act", "dve", "pool", "sp")

    def __init__(self, nc, n_dma_sems=48):
        self.nc = nc
        self.ops = {e: [] for e in self.ENGS}
        self.cnt = {e: 0 for e in self.ENGS}
        self.known = {e: {} for e in self.ENGS}
        self.n_dma = n_dma_sems
        self.dma_cnt = [0] * n_dma_sems
        half = n_dma_sems // 2
        self.dma_pools = {"sp": list(range(0, half)), "pool": list(range(half, n_dma_sems))}
        self.dma_rr = {"sp": 0, "pool": 0}
        self.sems = {}

    def _need(self, X, waits, key, val):
        if self.known[X].get(key, 0) >= val:
            return
        if waits.get(key, 0) < val:
            waits[key] = val

    @staticmethod
    def _flat(bs):
        out = []
        for b in bs:
            if isinstance(b, (list, tuple)):
                out.extend(Sched._flat(b))
            else:
                out.append(b)
        return out

    def op(self, eng, fn, reads=(), writes=(), inc=True, dma=False):
        X = eng
        reads = self._flat(reads)
        writes = self._flat(writes)
        waits = {}
        for b in reads:
            if b.w is not None:
                k, v, e = b.w
                if e == X and k == X and X == "pe":
                    continue
                self._need(X, waits, k, v)
        for b in writes:
            if b.w is not None:
                k, v, e = b.w
                if not (e == X and k == X and X == "pe"):
                    self._need(X, waits, k, v)
            for k, (v, e) in b.r.items():
                if e == X and k == X:
                    continue
                self._need(X, waits, k, v)
        if dma:
            pl = "pool" if X == "pool" else "sp"
            lst = self.dma_pools[pl]
            i = lst[self.dma_rr[pl]]
            self.dma_rr[pl] = (self.dma_rr[pl] + 1) % len(lst)
            key = "d%d" % i
            if self.dma_cnt[i] > 0:
                self._need(X, waits, key, 16 * self.dma_cnt[i])
            self.dma_cnt[i] += 1
            tok = (key, 16 * self.dma_cnt[i], X)
            incspec = (key, 16)
        else:
            if inc:
                self.cnt[X] += 1
                tok = (X, self.cnt[X], X)
                incspec = (X, 1)
            else:
                tok = (X, self.cnt[X] + 1, X)
                incspec = None
        for k, v in waits.items():
            self.known[X][k] = v
        self.ops[X].append((tuple(waits.items()), fn, incspec))
        k, v, e = tok
        for b in reads:
            old = b.r.get(k)
            if old is None or old[0] < v:
                b.r[k] = (v, e)
        for b in writes:
            b.w = tok
            b.r = {}
        return tok

    def alias(self, news, olds):
        acc = {}
        for o in olds:
            for k, (v, e) in o.r.items():
                if k not in acc or acc[k][0] < v:
                    acc[k] = (v, "?")
            if o.w is not None:
                k, v, e = o.w
                if k not in acc or acc[k][0] < v:
                    acc[k] = (v, "?")
        for n in news:
            for k, (v, e) in acc.items():
                old = n.r.get(k)
                if old is None or old[0] < v:
                    n.r[k] = (v, e)

    def barrier(self):
        for X in self.ENGS:
            waits = {}
            for i in range(self.n_dma):
                if self.dma_cnt[i] > 0:
                    self._need(X, waits, "d%d" % i, 16 * self.dma_cnt[i])
            for e in self.ENGS:
                if e != X and self.cnt[e] > 0:
                    self._need(X, waits, e, self.cnt[e])
            for k, v in waits.items():
                self.known[X][k] = v
            self.ops[X].append((tuple(waits.items()), None, None))

    def emit(self):
        nc = self.nc
        with contextlib.ExitStack() as st:
            keys = list(self.ENGS) + ["d%d" % i for i in range(self.n_dma)]
            for k in keys:
                self.sems[k] = st.enter_context(nc.semaphore("s_" + k))
            block = st.enter_context(nc.Block())
            sems = self.sems

            def run(e, engobj):
                for waits, fn, incspec in self.ops[e]:
                    for k, v in waits:
                        engobj.wait_ge(sems[k], v)
                    if fn is None:
                        continue
                    ins = fn(engobj)
                    if incspec is not None:
                        ins.then_inc(sems[incspec[0]], incspec[1])

            @block.tensor
            def _(t):
                run("pe", t)

            @block.scalar
            def _(s):
                run("act", s)

            @block.vector
            def _(v):
                run("dve", v)

            @block.gpsimd
            def _(g):
                run("pool", g)

            @block.sync
            def _(s):
                run("sp", s)


def _t5_bucket_np(rel):
    rel = np.asarray(rel, dtype=np.int32)
    half, max_exact = 16, 8
    ret = np.where(rel > 0, half, 0)
    n = np.abs(rel)
    nf = np.maximum(n, 1).astype(np.float32)
    large = max_exact + (np.log(nf / np.float32(max_exact)) / np.float32(math.log(128 / max_exact))
                         * np.float32(half - max_exact)).astype(np.int32)
    large = np.minimum(large, half - 1)
    return ret + np.where(n < max_exact, n, large)


_BUCKET_FIX = {16: 10, 32: 12, 64: 14, 128: 15}


def _host_consts():
    ident = np.eye(128, dtype=np.float32)
    J = np.ascontiguousarray(ident[::-1])
    inv = (10000.0 ** (-np.arange(0, 64, 2, dtype=np.float32) / np.float32(64))).astype(np.float32)
    invt = np.ascontiguousarray(np.broadcast_to(inv[None, :], (128, 32))).astype(np.float32)
    oht = np.zeros((33, 512), dtype=np.float32)
    for m in range(512):
        rel = m - 255
        if abs(rel) <= 128:
            n = abs(rel)
            bk = int(_t5_bucket_np(rel))
            if n in _BUCKET_FIX:
                bk = _BUCKET_FIX[n] + (16 if rel > 0 else 0)
            oht[bk, m] = 1.0
        else:
            oht[32, m] = 1.0
    return {"ident": ident, "jrev": J, "invt": invt, "oht": oht}


ARENA_BYTES = 211968
EXTRA = 207872


def build_program():
    nc = bass.Bass("TRN2", target_bir_lowering=False)
    S = Sched(nc, n_dma_sems=48)

    def dram(name, shape, dt, kind="ExternalInput"):
        return nc.dram_tensor(name, shape, dt, kind=kind)

    x_d = dram("x", [SEQ, D], F32).ap()
    c_d = dram("c_l", [128, 8], F32).ap()
    pos_d = dram("pos_l", [128, NT], I32).ap()
    wada_d = dram("w_ada", [D, 6 * D], F32).ap()
    bada_d = dram("b_ada_l", [128, 48], F32).ap()
    nmix_d = dram("norm_mix_l", [128, 8], F32).ap()
    win_d = dram("w_in", [D, D_IN], F32).ap()
    qn_d = dram("q_norm_l", [128, 3], F32).ap()
    wuq_d = dram("w_uq", [384, 1536], F32).ap()
    kvn_d = dram("kv_norm_l", [128, 2], F32).ap()
    wukv_d = dram("w_ukv", [256, 2048], F32).ap()
    rb_d = dram("rel_bias", [32, 8], F32).ap()
    sink_d = dram("sink", [8], F32).ap()
    womla_d = dram("w_o_mla", [D, D], F32).ap()
    woswa_d = dram("w_o_swa", [D, D], F32).ap()
    wout_d = dram("w_out", [D, D], F32).ap()
    nmlp_d = dram("norm_mlp_l", [128, 8], F32).ap()
    wff1_d = dram("w_ff1", [D, 4 * D], F32).ap()
    wff2_d = dram("w_ff2", [4 * D, D], F32).ap()
    nfin_d = dram("norm_final_l", [128, 8], F32).ap()
    ident_d = dram("ident", [128, 128], F32).ap()
    jrev_d = dram("jrev", [128, 128], F32).ap()
    invt_d = dram("invt", [128, 32], F32).ap()
    oht_d = dram("oht", [33, 512], F32).ap()
    tbl_t = dram("tbl_scratch", [8, 512], F32, kind="Internal")
    NPIECE = 74
    wsc = dram("wsc", [NPIECE, 128, 2048], BF16, kind="Internal").ap()
    out_d = dram("out", [SEQ, D], F32, kind="ExternalOutput").ap()
    dbg_d = {}
    for name, shape in DEBUG.items():
        dbg_d[name] = dram("dbg_" + name, list(shape), F32, kind="ExternalOutput").ap()

    st = contextlib.ExitStack()
    with st:
        arena = st.enter_context(nc.sbuf_tensor("arena", [128, ARENA_BYTES // 2], BF16))
        banks = [st.enter_context(nc.psum_tensor("bank%d" % i, [128, 512], F32))[:, :] for i in range(8)]
        bank_b = [Buf("bank%d" % i) for i in range(8)]

        def V(off, dt, shape, p0=0, p1=128):
            n = int(np.prod(shape))
            esz = 2 if dt == BF16 else 4
            assert off % 4 == 0 and off + n * esz <= ARENA_BYTES, (off, n, esz)
            v = arena[p0:p1, off // 2:(off + n * esz) // 2]
            if dt != BF16:
                v = v.bitcast(dt)
            if len(shape) == 2:
                v = v.rearrange("p (a b) -> p a b", a=shape[0])
            elif len(shape) == 3:
                v = v.rearrange("p (a b c) -> p a b c", a=shape[0], b=shape[1])
            elif len(shape) == 4:
                v = v.rearrange("p (a b c d) -> p a b c d", a=shape[0], b=shape[1], c=shape[2])
            return v

        def mm(out, lhsT, rhs, start, stop, R, W, inc):
            S.op("pe", lambda e: e.matmul(out, lhsT=lhsT, rhs=rhs, start=start, stop=stop),
                 reads=R, writes=W, inc=inc)

        def tp(out, in_, R, W, inc):
            S.op("pe", lambda e: e.transpose(out=out, in_=in_, identity=ident_f), reads=R + [b_const],
                 writes=W, inc=inc)

        def act(out, in_, func, R, W, bias=None, scale=None, accum=None):
            kw = {}
            if bias is not None:
                kw["bias"] = bias
            if scale is not None:
                kw["scale"] = scale
            if accum is not None:
                kw["accum_out"] = accum
            S.op("act", lambda e: e.activation(out=out, in_=in_, func=func, **kw), reads=R, writes=W)

        def ts(eng, out, in0, s1, s2, op0, op1, R, W):
            if op1 is None:
                S.op(eng, lambda e: e.tensor_scalar(out=out, in0=in0, scalar1=s1, scalar2=None, op0=op0),
                     reads=R, writes=W)
            else:
                S.op(eng, lambda e: e.tensor_scalar(out=out, in0=in0, scalar1=s1, scalar2=s2, op0=op0, op1=op1),
                     reads=R, writes=W)

        def tt(eng, out, in0, in1, op, R, W):
            S.op(eng, lambda e: e.tensor_tensor(out=out, in0=in0, in1=in1, op=op), reads=R, writes=W)

        def stt(eng, out, in0, scalar, in1, op0, op1, R, W):
            S.op(eng, lambda e: e.scalar_tensor_tensor(out=out, in0=in0, scalar=scalar, in1=in1, op0=op0, op1=op1),
                 reads=R, writes=W)

        def cp(eng, out, in_, R, W):
            if eng == "act":
                S.op("act", lambda e: e.copy(out=out, in_=in_), reads=R, writes=W)
            else:
                S.op(eng, lambda e: e.tensor_copy(out=out, in_=in_), reads=R, writes=W)

        def recip(out, in_, R, W):
            S.op("dve", lambda e: e.reciprocal(out=out, in_=in_), reads=R, writes=W)

        def dma(eng, out, in_, R, W):
            S.op(eng, lambda e: e.dma_start(out=out, in_=in_), reads=R, writes=W, dma=True)

        def memset(eng, ap, val, W):
            S.op(eng, lambda e: e.memset(ap, val), writes=W)

        bank_rr = [0]

        reserved = set()

        def nextbank():
            while True:
                i = bank_rr[0]
                bank_rr[0] = (i + 1) % 8
                if i not in reserved:
                    return banks[i], bank_b[i]

        def dump(name, ap, R):
            if name in dbg_d:
                dma("sp", dbg_d[name], ap, R, [])

        ident_f = V(0, F32, [128])
        ones_f = V(512, F32, [128])
        ones_bf = V(1024, BF16, [128])
        jrev_f = V(1280, F32, [128])
        b_const = Buf("const")
        SV = 2048

        def sv(i, n):
            return V(SV + 4 * i, F32, [n])

        cT = sv(0, 8); cexp = sv(8, 8); cact2 = V(SV + 64, F32, [8, 2])
        modT = sv(32, 48); badaT = sv(80, 48)
        nmix = sv(128, 8); nmlp = sv(136, 8); nfin = sv(144, 8)
        a1 = sv(152, 8); a2 = sv(160, 8)
        qn = sv(168, 3); kvn = sv(172, 2)
        sinkexp = sv(176, 8)
        pos_f = sv(184, 32)
        pos_i = V(SV + 4 * 216, I32, [32])
        stat = sv(248, 16)
        invt = sv(264, 32)
        cos_t = V(4096, F32, [32, 32])
        sin_t = V(8192, F32, [32, 32])
        rb_aug = V(3328, F32, [8], 0, 33)
        tbl_sb = V(12288, F32, [512], 0, 8)
        oht = V(14336, F32, [512], 0, 33)
        b_small = Buf("small")
        b_trig = Buf("trig")
        b_mod = Buf("mod")

        OT_OFF = 16384
        OT = V(OT_OFF, BF16, [8, SEQ])
        b_OT = [[Buf("OT%d_%d" % (h, q)) for q in range(NB)] for h in range(8)]
        P1O = 81920
        cqnT = V(P1O, BF16, [3, SEQ])
        ckvnT = V(P1O + 24576, BF16, [2, SEQ])
        KrT = V(P1O + 40960, BF16, [SEQ])
        b_cqn = [Buf("cqn%d" % t) for t in range(NT)]
        b_ckvn = [Buf("ckvn%d" % t) for t in range(NT)]
        b_kr = [Buf("kr%d" % t) for t in range(NT)]
        TR = 131072

        dma("sp", ident_f, ident_d, [], [b_const])
        dma("sp", jrev_f, jrev_d, [], [b_const])
        dma("sp", invt, invt_d, [], [b_small])
        dma("sp", oht, oht_d, [], [b_small])
        dma("sp", cT, c_d, [], [b_small])
        dma("sp", badaT, bada_d, [], [b_small])
        dma("sp", nmix, nmix_d, [], [b_small])
        dma("sp", nmlp, nmlp_d, [], [b_small])
        dma("sp", nfin, nfin_d, [], [b_small])
        dma("sp", qn, qn_d, [], [b_small])
        dma("sp", kvn, kvn_d, [], [b_small])
        dma("sp", pos_i, pos_d, [], [b_small])
        dma("sp", sinkexp, sink_d.partition_broadcast(128), [], [b_small])
        dma("sp", rb_aug[0:32, :], rb_d, [], [b_small])
        memset("dve", rb_aug[32:33, :], NEG, [b_small])
        memset("dve", ones_f, 1.0, [b_const])
        memset("dve", ones_bf, 1.0, [b_const])

        act(cexp, cT, AF.Exp, [b_small], [b_small], scale=-1.0)
        ts("dve", cexp, cexp, 1.0, None, ALU.add, None, [b_small], [b_small])
        recip(cexp, cexp, [b_small], [b_small])
        tt("dve", cact2[:, :, 0], cT, cexp, ALU.mult, [b_small], [b_small])
        tt("dve", cact2[:, :, 1], cT, cexp, ALU.mult, [b_small], [b_small])
        act(sinkexp, sinkexp, AF.Exp, [b_small], [b_small])

        tg = [V(TR + 32768 + 4096 * i, F32, [32, 32]) for i in range(4)]
        tgi = V(TR + 32768 + 4096 * 4, I32, [32, 32])
        b_tg = Buf("tg")
        cp("dve", pos_f, pos_i, [b_small], [b_small])
        ang, nf, rr, mk = tg
        tt("dve", ang, pos_f.unsqueeze(2).broadcast_to([128, 32, 32]),
           invt.unsqueeze(1).broadcast_to([128, 32, 32]), ALU.mult, [b_small], [b_tg])
        ts("dve", tgi, ang, 1.0 / TWO_PI, None, ALU.mult, None, [b_tg], [b_tg])
        cp("dve", nf, tgi, [b_tg], [b_tg])
        stt("dve", rr, nf, -C1, ang, ALU.mult, ALU.add, [b_tg], [b_tg])
        stt("dve", rr, nf, -C2, rr, ALU.mult, ALU.add, [b_tg], [b_tg])

        def wrap(r):
            ts("dve", mk, r, math.pi, TWO_PI, ALU.is_gt, ALU.mult, [b_tg], [b_tg])
            tt("dve", r, r, mk, ALU.subtract, [b_tg], [b_tg])
            ts("dve", mk, r, -math.pi, TWO_PI, ALU.is_lt, ALU.mult, [b_tg], [b_tg])
            tt("dve", r, r, mk, ALU.add, [b_tg], [b_tg])
            ts("dve", r, r, -PI_SAFE, PI_SAFE, ALU.max, ALU.min, [b_tg], [b_tg])

        wrap(rr)
        act(sin_t, rr, AF.Sin, [b_tg], [b_trig])
        ts("dve", rr, rr, math.pi / 2, None, ALU.add, None, [b_tg], [b_tg])
        wrap(rr)
        act(cos_t, rr, AF.Sin, [b_tg], [b_trig])

        stg = [V(TR + 16384 * i, F32, [8, 512]) for i in range(2)]
        b_stg = [Buf("stg0"), Buf("stg1")]
        psM, b_psM = nextbank()
        psMv = psM[:, 0:96].rearrange("p (a b) -> p a b", b=2)
        for pc in range(12):
            sl = pc % 2
            dma("sp", stg[sl], wada_d[:, pc * 512:(pc + 1) * 512].rearrange("(k p) n -> p k n", p=128),
                [], [b_stg[sl]])
            for j in range(4):
                cc = pc * 4 + j
                for k in range(8):
                    mm(psMv[:, cc, :], stg[sl][:, k, j * 128:(j + 1) * 128], cact2[:, k, :], k == 0, k == 7,
                       [b_stg[sl], b_small], [b_psM], inc=(k == 7))
        tt("dve", modT, psMv[:, :, 0], badaT, ALU.add, [b_psM, b_small], [b_mod])
        stt("dve", a1, modT[:, 8:16], 1.0, nmix, ALU.add, ALU.mult, [b_mod, b_small], [b_mod])
        stt("dve", a2, modT[:, 32:40], 1.0, nmlp, ALU.add, ALU.mult, [b_mod, b_small], [b_mod])
        sh1 = modT[:, 0:8]; g1v = modT[:, 16:24]; sh2 = modT[:, 24:32]; g2v = modT[:, 40:48]
        S.barrier()

        stat_rr = [0]

        def rstd_of(ss_ap, n, R):
            act(ss_ap, ss_ap, AF.Ln, R, R, bias=EPS, scale=1.0 / n)
            act(ss_ap, ss_ap, AF.Exp, R, R, scale=-0.5)
            return ss_ap

        stat_bufs = [Buf("stat%d" % i) for i in range(16)]

        def new_stat():
            i = stat_rr[0]
            stat_rr[0] = (i + 1) % 16
            return stat[:, i:i + 1], stat_bufs[i]

        w704 = V(TR, BF16, [8, 704]); b_w704 = Buf("w704")
        xt = [V(TR + 11264 + 4096 * i, F32, [1024]) for i in range(2)]; b_xt = [Buf(), Buf()]
        xn = [V(TR + 19456 + 4096 * i, F32, [1024]) for i in range(2)]; b_xn = [Buf(), Buf()]
        hT = V(TR + 27648, BF16, [8, 512]); b_hT = [Buf() for _ in range(4)]
        junk = V(TR + 35840, BF16, [1024]); b_junk = Buf("junk")
        cqkv = [V(TR + 37888 + 2560 * i, F32, [640]) for i in range(2)]; b_cqkv = [Buf(), Buf()]
        krr = [V(TR + 43008 + 512 * i, F32, [128]) for i in range(2)]; b_krr = [Buf(), Buf()]
        rtmp = [V(TR + 44032 + 128 * i, F32, [32]) for i in range(4)]; b_rtmp = Buf("rtmp")

        dma("pool", w704, win_d[:, 0:704].rearrange("(k p) n -> p k n", p=128), [], [b_w704])

        pspec = {}
        plist = []

        def addp(key, src2d, kch, ncols):
            pspec[key] = (len(plist), kch, ncols)
            plist.append((key, src2d, kch, ncols))

        addp(("ks",), win_d[:, 1728:1984], 8, 256)
        addp(("vs",), win_d[:, 1984:2240], 8, 256)
        for pc in range(4):
            addp(("qs", pc), win_d[:, 704 + pc * 256:704 + (pc + 1) * 256], 8, 256)
        for c in range(8):
            addp(("g0", c), win_d[:, 2240 + c * 128:2240 + (c + 1) * 128], 8, 128)
            addp(("g1", c), win_d[:, 3264 + c * 128:3264 + (c + 1) * 128], 8, 128)
            addp(("oa", c), womla_d[:, c * 128:(c + 1) * 128], 8, 128)
            addp(("ob", c), woswa_d[:, c * 128:(c + 1) * 128], 8, 128)
        for cq in range(4):
            addp(("wo", cq), wout_d[:, cq * 256:(cq + 1) * 256], 8, 256)
        for pc in range(16):
            addp(("f1", pc), wff1_d[:, pc * 256:(pc + 1) * 256], 8, 256)
        for hf in range(2):
            for g in range(8):
                addp(("f2", hf, g), wff2_d[g * 512:(g + 1) * 512, hf * 512:(hf + 1) * 512], 4, 512)
        assert len(plist) == NPIECE
        b_wsc = [Buf("wsc%d" % i) for i in range(NPIECE)]
        for i, (key, src2d, kch, ncols) in enumerate(plist):
            dma("pool", wsc[i][:, 0:kch * ncols].rearrange("p (k n) -> p k n", k=kch),
                src2d.rearrange("(k p) n -> p k n", p=128), [], [b_wsc[i]])

        def make_nt(ring, b_ring, fixed_banks=None):
            rr = [0]

            def front(src, bsrc, load=None):
                i = rr[0]
                rr[0] = (i + 1) % len(ring)
                if load is not None:
                    dma("sp", ring[i], load, [], [b_ring[i]])
                    src, bsrc = ring[i], b_ring[i]
                ss, bss = new_stat()
                act(junk, src, AF.Square, [bsrc], [b_junk, bss], accum=ss)
                rstd_of(ss, D, [bss])
                ts("dve", ring[i], src, ss, None, ALU.mult, None, [bsrc, bss], [b_ring[i]])
                return i

            def back(i, avec, shvec, dst_fn, bdst):
                for hf in range(2):
                    if fixed_banks is None:
                        ps, bps = nextbank()
                    else:
                        ps, bps = banks[fixed_banks[hf]], bank_b[fixed_banks[hf]]
                    for j in range(4):
                        k = hf * 4 + j
                        tp(ps[:, j * 128:(j + 1) * 128], ring[i][:, k * 128:(k + 1) * 128], [b_ring[i]], [bps],
                           inc=(j == 3))
                    for j in range(4):
                        k = hf * 4 + j
                        if j % 2 == 0:
                            act(dst_fn(k), ps[:, j * 128:(j + 1) * 128], AF.Identity, [bps, b_mod], [bdst],
                                bias=shvec[:, k:k + 1], scale=avec[:, k:k + 1])
                        else:
                            ts("dve", dst_fn(k), ps[:, j * 128:(j + 1) * 128], avec[:, k:k + 1], shvec[:, k:k + 1],
                               ALU.mult, ALU.add, [bps, b_mod], [bdst])

            return front, back

        p1_front, p1_back = make_nt([xt[0], xt[1], xn[0], xn[1]], [b_xt[0], b_xt[1], b_xn[0], b_xn[1]], fixed_banks=(0, 1))
        p1_slot = {}
        p1_ps = {}

        def p1_A(t):
            p1_slot[t] = p1_front(None, None, load=x_d[t * 128:(t + 1) * 128, :])

        def p1_B(t):
            tl = t % 4
            p1_back(p1_slot.pop(t), a1, sh1, lambda k, tl=tl: hT[:, k, tl * 128:(tl + 1) * 128], b_hT[tl])
            ia = 2 + 2 * (t % 2)
            psA, bA, psB, bB = banks[ia], bank_b[ia], banks[ia + 1], bank_b[ia + 1]
            for k in range(8):
                mm(psA[:, 0:384], hT[:, k, tl * 128:(tl + 1) * 128], w704[:, k, 0:384], k == 0, k == 7,
                   [b_hT[tl], b_w704], [bA], inc=(k == 7))
            for k in range(8):
                mm(psB[:, 0:320], hT[:, k, tl * 128:(tl + 1) * 128], w704[:, k, 384:704], k == 0, k == 7,
                   [b_hT[tl], b_w704], [bB], inc=(k == 7))
            p1_ps[t] = (psA, bA, psB, bB)

        def p1_C(t):
            sl = t % 2
            psA, bA, psB, bB = p1_ps.pop(t)
            ssq, bq = new_stat()
            act(junk[:, 0:384], psA[:, 0:384], AF.Square, [bA], [b_junk, bq], accum=ssq)
            rstd_of(ssq, 384, [bq])
            sskv, bkv = new_stat()
            act(junk[:, 0:256], psB[:, 0:256], AF.Square, [bB], [b_junk, bkv], accum=sskv)
            rstd_of(sskv, 256, [bkv])
            ts("dve", cqkv[sl][:, 0:384], psA[:, 0:384], ssq, None, ALU.mult, None, [bA, bq], [b_cqkv[sl]])
            ts("dve", cqkv[sl][:, 384:640], psB[:, 0:256], sskv, None, ALU.mult, None, [bB, bkv], [b_cqkv[sl]])
            x1_ = psB[:, 256:288]; x2_ = psB[:, 288:320]
            ct = cos_t[:, t, :]; sn = sin_t[:, t, :]
            tt("dve", rtmp[0], x1_, ct, ALU.mult, [bB, b_trig], [b_rtmp])
            tt("dve", rtmp[1], x2_, sn, ALU.mult, [bB, b_trig], [b_rtmp])
            tt("dve", rtmp[2], x2_, ct, ALU.mult, [bB, b_trig], [b_rtmp])
            tt("dve", rtmp[3], x1_, sn, ALU.mult, [bB, b_trig], [b_rtmp])
            tt("dve", krr[sl][:, 0:32], rtmp[0], rtmp[1], ALU.subtract, [b_rtmp], [b_krr[sl]])
            tt("dve", krr[sl][:, 32:64], rtmp[2], rtmp[3], ALU.add, [b_rtmp], [b_krr[sl]])
            cp("dve", krr[sl][:, 64:128], krr[sl][:, 0:64], [b_krr[sl]], [b_krr[sl]])
            psT, bT = banks[6], bank_b[6]
            for j in range(3):
                tp(psT[:, j * 128:(j + 1) * 128], cqkv[sl][:, j * 128:(j + 1) * 128], [b_cqkv[sl]], [bT], inc=(j == 2))
            psU, bU = banks[7], bank_b[7]
            for j in range(2):
                tp(psU[:, j * 128:(j + 1) * 128], cqkv[sl][:, 384 + j * 128:384 + (j + 1) * 128], [b_cqkv[sl]], [bU],
                   inc=False)
            tp(psU[:, 256:384], krr[sl], [b_krr[sl]], [bU], inc=True)
            tok = slice(t * 128, (t + 1) * 128)
            for j in range(3):
                ts("dve", cqnT[:, j, tok], psT[:, j * 128:(j + 1) * 128], qn[:, j:j + 1], None, ALU.mult, None,
                   [bT, b_small], [b_cqn[t]])
            for j in range(2):
                act(ckvnT[:, j, tok], psU[:, j * 128:(j + 1) * 128], AF.Identity, [bU, b_small], [b_ckvn[t]],
                    scale=kvn[:, j:j + 1])
            cp("act", KrT[:, tok], psU[:, 256:384], [bU], [b_kr[t]])

        p1_A(0)
        for n in range(1, NT + 2):
            if 0 <= n - 1 < NT:
                p1_B(n - 1)
            if 0 <= n - 2 < NT:
                p1_C(n - 2)
            if n < NT:
                p1_A(n)
        dump("cqnT", cqnT[:, :, 0:512], b_cqn[0:4])
        dump("ckvnT", ckvnT[:, :, 0:512], b_ckvn[0:4])
        dump("KrT", KrT[:, 0:512], b_kr[0:4])
        S.barrier()

        KT = [V(TR + 8192 * i, BF16, [SEQ]) for i in range(2)]
        Vt = [V(TR + 16384 + 8192 * i, BF16, [NT, 128]) for i in range(2)]
        QTn = [V(TR + 32768 + 8192 * i, BF16, [SEQ]) for i in range(2)]
        QTr = V(TR + 49152, BF16, [SEQ])
        ropeT = [V(TR + 57344 + 2048 * i, F32, [4, 2, 32]) for i in range(4)]
        b_KT = [[Buf() for _ in range(NB)] for _ in range(2)]
        b_V = [[Buf() for _ in range(NB)] for _ in range(2)]
        b_QTn = [[Buf() for _ in range(NB)] for _ in range(2)]
        b_QTr = [Buf() for _ in range(NB)]
        b_ropeT = Buf("ropeT")
        wqn = [V(TR + 65536 + 1792 * i, BF16, [3, 128]) for i in range(2)]
        wk = [V(TR + 65536 + 1792 * i + 768, BF16, [2, 128]) for i in range(2)]
        wv = [V(TR + 65536 + 1792 * i + 1280, BF16, [2, 128]) for i in range(2)]
        b_wh = [Buf(), Buf()]
        wqr = V(TR + 69120, BF16, [3, 2, 64]); b_wqr = Buf("wqr")
        qrot = V(TR + 69120 + 768, F32, [4, 2, 2, 32])
        PT = [V(TR + 70656 + 1024 * i, BF16, [512]) for i in range(4)] + [V(TR + 57344 + 7168, BF16, [512])]
        b_PT = [Buf() for _ in range(5)]
        recs = [V(TR + 74752, F32, [512])] * 2
        b_recs = [Buf("rec")] * 2
        sc_rr = [0]
        ropeT = [V(TR + 57344 + 1024 * i, F32, [4, 2, 32]) for i in range(3)]
        qrot = V(TR + 57344 + 3072, F32, [4, 2, 2, 32])
        b_qrot = Buf("qrot")
        QTrz = [V(TR + 57344 + 5120 + 1024 * i, BF16, [512]) for i in range(2)]
        b_QTrz = [Buf(), Buf()]
        pt_rr = [0]
        ev_rr = [0]
        dacc = [V(EXTRA + 2048 * i, F32, [512]) for i in range(2)]; b_dacc = [Buf(), Buf()]
        dacc_rr = [0]

        def evac(out, in_, R, W):
            ev_rr[0] += 1
            cp("dve", out, in_, R, W)

        for hp in range(4):
            for e in range(2):
                h = 2 * hp + e
                dma("pool", wqr[:, :, e, :],
                    wuq_d[:, h * 192 + 128:h * 192 + 192].rearrange("(k p) n -> p k n", p=128), [], [b_wqr])
            for g in range(NB):
                ps, bps = nextbank()
                for tl in range(4):
                    t = 4 * g + tl
                    for kc in range(3):
                        mm(ps[:, tl * 128:(tl + 1) * 128], cqnT[:, kc, t * 128:(t + 1) * 128],
                           wqr[:, kc, :, :], kc == 0, kc == 2, [b_cqn[t], b_wqr], [bps], inc=(kc == 2 and tl == 3))
                psv = ps.rearrange("p (t h f i) -> p t h f i", t=4, h=2, f=2)
                cb = cos_t[:, 4 * g:4 * g + 4, :].unsqueeze(2).broadcast_to([128, 4, 2, 32])
                sb_ = sin_t[:, 4 * g:4 * g + 4, :].unsqueeze(2).broadcast_to([128, 4, 2, 32])
                x1 = psv[:, :, :, 0, :]; x2 = psv[:, :, :, 1, :]
                tt("dve", ropeT[0], x1, cb, ALU.mult, [bps, b_trig], [b_ropeT])
                tt("dve", ropeT[1], x2, sb_, ALU.mult, [bps, b_trig], [b_ropeT])
                tt("dve", qrot[:, :, :, 0, :], ropeT[0], ropeT[1], ALU.subtract, [b_ropeT], [b_qrot])
                tt("dve", ropeT[0], x2, cb, ALU.mult, [bps, b_trig], [b_ropeT])
                tt("dve", ropeT[1], x1, sb_, ALU.mult, [bps, b_trig], [b_ropeT])
                tt("dve", qrot[:, :, :, 1, :], ropeT[0], ropeT[1], ALU.add, [b_ropeT], [b_qrot])
                ps2, bps2 = nextbank()
                qflat = qrot.rearrange("p t h f i -> p (t h f i)")
                for tl in range(4):
                    tp(ps2[:, tl * 128:(tl + 1) * 128], qflat[:, tl * 128:(tl + 1) * 128], [b_qrot], [bps2], inc=(tl == 3))
                evac(QTr[:, g * 512:(g + 1) * 512], ps2, [bps2], [b_QTr[g]])
            for e in range(2):
                h = 2 * hp + e
                sl = e
                dma("pool", wqn[sl], wuq_d[:, h * 192:h * 192 + 128].rearrange("(k p) n -> p k n", p=128), [], [b_wh[sl]])
                dma("pool", wk[sl], wukv_d[:, h * 256:h * 256 + 128].rearrange("(k p) n -> p k n", p=128), [], [b_wh[sl]])
                dma("pool", wv[sl], wukv_d[:, h * 256 + 128:h * 256 + 256].rearrange("(k p) n -> p k n", p=128), [],
                    [b_wh[sl]])
                for g in range(NB):
                    cols = slice(g * 512, (g + 1) * 512)
                    ps, bps = nextbank()
                    for kc in range(3):
                        mm(ps, wqn[sl][:, kc, :], cqnT[:, kc, cols], kc == 0, kc == 2,
                           [b_wh[sl]] + b_cqn[4 * g:4 * g + 4], [bps], inc=(kc == 2))
                    evac(QTn[sl][:, cols], ps, [bps], [b_QTn[sl][g]])
                    ps, bps = nextbank()
                    for kc in range(2):
                        mm(ps, wk[sl][:, kc, :], ckvnT[:, kc, cols], kc == 0, kc == 1,
                           [b_wh[sl]] + b_ckvn[4 * g:4 * g + 4], [bps], inc=(kc == 1))
                    evac(KT[sl][:, cols], ps, [bps], [b_KT[sl][g]])
                    ps, bps = nextbank()
                    for tl in range(4):
                        t = 4 * g + tl
                        for kc in range(2):
                            mm(ps[:, tl * 128:(tl + 1) * 128], ckvnT[:, kc, t * 128:(t + 1) * 128], wv[sl][:, kc, :],
                               kc == 0, kc == 1, [b_wh[sl], b_ckvn[t]], [bps], inc=(kc == 1 and tl == 3))
                    evac(Vt[sl][:, 4 * g:4 * g + 4, :].rearrange("p t d -> p (t d)"), ps, [bps], [b_V[sl][g]])
                for zi in range(2):
                    memset("pool", QTrz[zi][(1 - e) * 64:(2 - e) * 64, :], 0.0, [b_QTrz[zi]])
                LA = 3
                items = [(qb, kc) for qb in range(NB) for kc in range(NT)]
                accs = {}
                pend = {}
                dst = {}
                deferred = []

                def acc_of(qb):
                    if qb not in accs:
                        a = (qb % 2) * 2
                        accs[qb] = (banks[a], bank_b[a], banks[a + 1], bank_b[a + 1])
                    return accs[qb]

                def scores(qb, kc, sl=sl, e=e):
                    qc = slice(qb * 512, (qb + 1) * 512)
                    zi = qb % 2
                    if kc == 0:
                        cp("pool", QTrz[zi][e * 64:(e + 1) * 64, :], QTr[e * 64:(e + 1) * 64, qc], [b_QTr[qb]],
                           [b_QTrz[zi]])
                    si = 4 + sc_rr[0]
                    sc_rr[0] = (sc_rr[0] + 1) % 4
                    ps, bps = banks[si], bank_b[si]
                    kcs = slice(kc * 128, (kc + 1) * 128)
                    mm(ps, KT[sl][:, kcs], QTn[sl][:, qc], True, False,
                       [b_KT[sl][kc // 4], b_QTn[sl][qb]], [bps], inc=False)
                    mm(ps, KrT[:, kcs], QTrz[zi], False, True, [b_kr[kc], b_QTrz[zi]], [bps], inc=True)
                    i = pt_rr[0]
                    pt_rr[0] = (i + 1) % len(PT)
                    act(PT[i], ps, AF.Exp, [bps], [b_PT[i]], scale=MLA_SCALE)
                    pend[(qb, kc)] = i

                def pv(qb, kc, sl=sl):
                    accO, bO, accD, bD = acc_of(qb)
                    i = pend.pop((qb, kc))
                    on_pe = False
                    mm(accO, Vt[sl][:, kc, :], PT[i], kc == 0, kc == NT - 1, [b_V[sl][kc // 4], b_PT[i]], [bO],
                       inc=not on_pe)
                    if on_pe:
                        mm(accD, ones_bf, PT[i], kc == 7, False, [b_const, b_PT[i]], [bD], inc=True)
                    else:
                        d = dst.setdefault(qb, {"n": 0, "used": [False, False]})
                        j = d["n"] % 2
                        d["n"] += 1
                        if not d["used"][j]:
                            d["used"][j] = True
                            cp("dve", dacc[j], PT[i], [b_PT[i]], [b_dacc[j]])
                        else:
                            tt("dve", dacc[j], dacc[j], PT[i], ALU.add, [b_dacc[j], b_PT[i]], [b_dacc[j]])

                def epi_pe(qb):
                    accO, bO, accD, bD = acc_of(qb)
                    ri = qb % 2
                    mm(accD, ones_f, recs[ri], True, True, [b_const, b_recs[ri]], [bD], inc=True)

                def epi_dve(qb, h=h):
                    accO, bO, accD, bD = acc_of(qb)
                    ri = qb % 2
                    qc = slice(qb * 512, (qb + 1) * 512)
                    act(recs[ri], accD, AF.Ln, [bD], [b_recs[ri]])
                    act(recs[ri], recs[ri], AF.Exp, [b_recs[ri]], [b_recs[ri]], scale=-1.0)
                    tt("dve", OT[:, h, qc], accO, recs[ri], ALU.mult, [bO, b_recs[ri]], [b_OT[h][qb]])

                for n in range(min(LA, len(items))):
                    scores(*items[n])
                for n, (qb, kc) in enumerate(items):
                    pv(qb, kc)
                    if n + LA < len(items):
                        scores(*items[n + LA])
                    if kc == NT - 1:
                        ri = qb % 2
                        tt("dve", recs[ri], dacc[0], dacc[1], ALU.add, [b_dacc[0], b_dacc[1]], [b_recs[ri]])
                        deferred.append(qb)
                    if kc == 3 and deferred:
                        epi_pe(deferred[0])
                    if kc == 5 and deferred:
                        epi_dve(deferred.pop(0))
                for qb in deferred:
                    epi_pe(qb)
                    epi_dve(qb)
        dump("OT", OT[:, :, 0:512].bitcast(BF16) if False else OT[:, :, 0:512], [b_OT[h][0] for h in range(8)])
        S.barrier()

        P3 = P1O
        G1b = V(P3, F32, [1024]); G2b = V(P3 + 4096, F32, [1024]); gFb = V(P3 + 8192, F32, [1024])
        Bm = V(P3 + 12288, F32, [3, 8, 128])
        b_G = Buf("G"); b_Bm = Buf("Bm")
        hTe = V(P3 + 24576, BF16, [8, 768]); b_hTe = [Buf() for _ in range(6)]
        U = P3 + 36864
        ksT = V(U, BF16, [2, 768]); b_ks = Buf("ks")
        vs = V(U + 3072, BF16, [6, 256]); b_vs = [Buf() for _ in range(6)]
        qsT = V(U + 6144, BF16, [8, 512]); b_qs = [Buf() for _ in range(8)]
        swaOT = V(U + 14336, BF16, [8, 512]); b_swaO = [[Buf() for _ in range(4)] for _ in range(2)]
        mergedT = V(U + 22528, BF16, [8, 512]); b_mg = [Buf() for _ in range(8)]
        uT = V(U, BF16, [32, 512]); b_uT = [Buf() for _ in range(32)]
        u_old = [b_ks] + b_vs + b_qs + b_swaO[0] + b_swaO[1] + b_mg
        XB = U + 32768
        xb = [V(XB + 4096 * i, F32, [1024]) for i in range(2)]; b_xb = [Buf(), Buf()]
        x1 = V(XB + 8192, F32, [4, 1024]); b_x1 = [Buf() for _ in range(4)]
        PT3 = [V(XB + 24576 + 1024 * i, BF16, [4, 128]) for i in range(3)]; b_PT3 = [Buf() for _ in range(3)]
        WR = XB + 27648
        b_wrh = [Buf() for _ in range(8)]
        b_wr = [[b_wrh[0], b_wrh[1]]]
        xn3 = V(WR + 16384, F32, [1024]); b_xn3 = Buf("xn3")
        sfp = [V(WR + 20480 + 2048 * i, F32, [512]) for i in range(3)]; b_sfp = [Buf() for _ in range(3)]
        junk3 = V(WR + 26624, BF16, [1024])
        dtmp = xn3[:, 0:512]; b_dtmp = b_xn3
        for i_ in range(4):
            PT3.append(V(EXTRA + 1024 * i_, BF16, [4, 128])); b_PT3.append(Buf())
        swb_rr = [0]; sfp_rr = [0]; pt3_rr = [0]
        assert WR + 26624 + 2048 <= ARENA_BYTES
        junk = junk3
        wr_rr = [0]

        def wpiece(*key):
            idx, kch, ncols = pspec[key]
            i = wr_rr[0]
            nh = 1 if kch * ncols <= 1024 else 2
            if nh == 2 and i % 2 == 1:
                i += 1
            i %= 8
            wr_rr[0] = (i + nh) % 8
            bw = b_wrh[i:i + nh]
            v = V(WR + 2048 * i, BF16, [kch, ncols])
            dma("pool", v, wsc[idx][:, 0:kch * ncols].rearrange("p (k n) -> p k n", k=kch), [b_wsc[idx]], bw)
            return v, bw

        dg = [V(WR + 20480 + 2048 * i, F32, [128]) for i in range(2)]
        for gi, (gvec, Gb, bsrc) in enumerate(((g1v, G1b, b_mod), (g2v, G2b, b_mod), (nfin, gFb, b_small))):
            for hf in range(2):
                ps, bps = nextbank()
                for j in range(4):
                    c = hf * 4 + j
                    d = dg[c % 2]
                    bd = b_sfp[c % 2]
                    ts("dve", d, ident_f, gvec[:, c:c + 1], None, ALU.mult, None, [b_const, bsrc], [bd])
                    mm(ps[:, j * 128:(j + 1) * 128], ones_f, d, True, True, [b_const, bd], [bps], inc=True)
                cp("dve", Gb[:, hf * 512:(hf + 1) * 512], ps, [bps], [b_G])
        ps, bps = nextbank()
        mm(ps[0:8, :], rb_aug, oht, True, True, [b_small], [bps], inc=True)
        b_tbl = Buf("tbl")
        cp("dve", tbl_sb, ps[0:8, :], [bps], [b_tbl])
        b_tbld = Buf("tbld")
        dma("sp", tbl_t.ap(), tbl_sb, [b_tbl], [b_tbld])
        hank = V(WR, F32, [8, 128])
        for dl in range(3):
            src = bass.AP(tensor=tbl_t, offset=dl * 128, ap=[[1, 128], [512, 8], [1, 128]])
            dma("sp", hank, src, [b_tbld], [b_wr[0]])
            for hh in range(2):
                ps, bps = nextbank()
                for j in range(4):
                    h = hh * 4 + j
                    mm(ps[:, j * 128:(j + 1) * 128], hank[:, h, :], jrev_f, True, True, [b_wr[0], b_const], [bps],
                       inc=(j == 3))
                cp("dve", Bm[:, dl, hh * 4:(hh + 1) * 4, :].rearrange("p h q -> p (h q)"), ps, [bps], [b_Bm])
        dump("Bm", Bm.rearrange("p a h q -> p (a h q)"), [b_Bm])
        dump("G1b", G1b, [b_G])

        nt_front, nt_back = make_nt([xb[0], xb[1], xn3], [b_xb[0], b_xb[1], b_xn3])

        def hext_steps(b):
            valid = [j for j in range(6) if 0 <= 4 * b - 1 + j < NT]
            slot = {}
            steps = []

            def mk_front(j):
                def f():
                    te = 4 * b - 1 + j
                    slot[j] = nt_front(None, None, load=x_d[te * 128:(te + 1) * 128, :])
                return f

            def mk_back(j):
                def f():
                    nt_back(slot[j], a1, sh1, lambda k, j=j: hTe[:, k, j * 128:(j + 1) * 128], b_hTe[j])
                return f

            for n, j in enumerate(valid):
                steps.append(mk_front(j))
                if n >= 1:
                    steps.append(mk_back(valid[n - 1]))
            steps.append(mk_back(valid[-1]))
            return steps

        def act_recip(buf, src, R, W):
            act(buf, src, AF.Ln, R, W)
            act(buf, buf, AF.Exp, W, W, scale=-1.0)

        for st_ in hext_steps(0):
            st_()
        for b in range(NB):
            S.alias(u_old, b_uT)
            valid = [j for j in range(6) if 0 <= 4 * b - 1 + j < NT]
            own = slice(128, 640)
            b_own = b_hTe[1:5]
            for tl in range(4):
                te = 4 * b + tl
                dma("sp", x1[:, tl, :], x_d[te * 128:(te + 1) * 128, :], [], [b_x1[tl]])
            wp, bwp = wpiece("ks")
            for kv in range(2):
                for (j0, j1) in ((0, 4), (4, 6)):
                    js = [j for j in valid if j0 <= j < j1]
                    if not js:
                        continue
                    cs = slice(js[0] * 128, (js[-1] + 1) * 128)
                    n = (js[-1] + 1 - js[0]) * 128
                    ps, bps = nextbank()
                    for k in range(8):
                        mm(ps[:, 0:n], wp[:, k, kv * 128:(kv + 1) * 128], hTe[:, k, cs], k == 0, k == 7,
                           [bwp] + [b_hTe[j] for j in js], [bps], inc=(k == 7))
                    cp("act", ksT[:, kv, cs], ps[:, 0:n], [bps], [b_ks])
            wp, bwp = wpiece("vs")
            for j in valid:
                ps, bps = nextbank()
                for k in range(8):
                    mm(ps[:, 0:256], hTe[:, k, j * 128:(j + 1) * 128], wp[:, k, :], k == 0, k == 7,
                       [bwp, b_hTe[j]], [bps], inc=(k == 7))
                cp("act", vs[:, j, :], ps[:, 0:256], [bps], [b_vs[j]])
            for pc in range(4):
                wp, bwp = wpiece("qs", pc)
                for hh in range(2):
                    h = pc * 2 + hh
                    ps, bps = nextbank()
                    for k in range(8):
                        mm(ps, wp[:, k, hh * 128:(hh + 1) * 128], hTe[:, k, own], k == 0, k == 7,
                           [bwp] + b_own, [bps], inc=(k == 7))
                    cp("act", qsT[:, h, :], ps, [bps], [b_qs[h]])
            units = [(qt, kv) for qt in range(4) for kv in range(2)]
            sw = {}

            def swa_scores(u):
                qt, kv = units[u]
                j = qt + 1
                hs = slice(kv * 4, (kv + 1) * 4)
                dls = [dl for dl in (-1, 0, 1) if (j + dl) in valid]
                pts = []
                for dl in dls:
                    jk = j + dl
                    bi = 4 + swb_rr[0]
                    swb_rr[0] = (swb_rr[0] + 1) % 4
                    ps, bps = banks[bi], bank_b[bi]
                    psv = ps.rearrange("p (h q) -> p h q", h=4)
                    mm(psv, ksT[:, kv, jk * 128:(jk + 1) * 128], qsT[:, hs, qt * 128:(qt + 1) * 128], True, True,
                       [b_ks] + b_qs[kv * 4:(kv + 1) * 4], [bps], inc=True)
                    si = sfp_rr[0]
                    sfp_rr[0] = (si + 1) % 3
                    pi = pt3_rr[0]
                    pt3_rr[0] = (pi + 1) % len(PT3)
                    sv_ = sfp[si].rearrange("p (h q) -> p h q", h=4)
                    stt("dve", sv_, psv, SWA_SCALE, Bm[:, dl + 1, hs, :], ALU.mult, ALU.add, [bps, b_Bm],
                        [b_sfp[si]])
                    act(PT3[pi], sv_, AF.Exp, [b_sfp[si]], [b_PT3[pi]])
                    pts.append((jk, pi))
                sw[u] = pts

            def swa_pv(u):
                qt, kv = units[u]
                hs = slice(kv * 4, (kv + 1) * 4)
                a = (u % 2) * 2
                accO, bO, accD, bD = banks[a], bank_b[a], banks[a + 1], bank_b[a + 1]
                pts = sw.pop(u)
                for n_i, (jk, pi) in enumerate(pts):
                    first = (n_i == 0)
                    last = (n_i == len(pts) - 1)
                    ptf = PT3[pi].rearrange("p h q -> p (h q)")
                    mm(accO, vs[:, jk, kv * 128:(kv + 1) * 128], ptf, first, last, [b_vs[jk], b_PT3[pi]], [bO],
                       inc=False)
                    mm(accD, ones_bf, ptf, first, last, [b_const, b_PT3[pi]], [bD], inc=True)
                dv = dtmp.rearrange("p (h q) -> p h q", h=4)
                tt("dve", dv, accD.rearrange("p (h q) -> p h q", h=4),
                   sinkexp[:, hs].unsqueeze(2).broadcast_to([128, 4, 128]), ALU.add, [bD, b_small], [b_dtmp])
                act_recip(dtmp, dtmp, [b_dtmp], [b_dtmp])
                tt("dve", swaOT[:, hs, qt * 128:(qt + 1) * 128], accO.rearrange("p (h q) -> p h q", h=4), dv,
                   ALU.mult, [bO, b_dtmp], [b_swaO[kv][qt]])

            reserved.update(range(4))
            swa_scores(0)
            for u in range(len(units)):
                if u + 1 < len(units):
                    swa_scores(u + 1)
                swa_pv(u)
            reserved.clear()
            if b == 0:
                dump("swaOT", swaOT, b_swaO[0] + b_swaO[1])
            for c in range(8):
                if True:
                    pg0, bpg0 = wpiece("g0", c)
                    pg1, bpg1 = wpiece("g1", c)
                    pa, bpa = wpiece("oa", c)
                    pb, bpb = wpiece("ob", c)
                    ccs = slice(0, 128)
                    ps_g0, bg0 = nextbank()
                    for k in range(8):
                        mm(ps_g0, pg0[:, k, ccs], hTe[:, k, own], k == 0, k == 7, [bpg0] + b_own, [bg0], inc=(k == 7))
                    ps_g1, bg1 = nextbank()
                    for k in range(8):
                        mm(ps_g1, pg1[:, k, ccs], hTe[:, k, own], k == 0, k == 7, [bpg1] + b_own, [bg1], inc=(k == 7))
                    ps_a, ba = nextbank()
                    for h in range(8):
                        mm(ps_a, pa[:, h, ccs], OT[:, h, b * 512:(b + 1) * 512], h == 0, h == 7, [bpa, b_OT[h][b]],
                           [ba], inc=(h == 7))
                    ps_b, bb = nextbank()
                    for h in range(8):
                        mm(ps_b, pb[:, h, ccs], swaOT[:, h, :], h == 0, h == 7,
                           [bpb] + [b_swaO[h // 4][q] for q in range(4)], [bb], inc=(h == 7))
                    act(sfp[0], ps_g0, AF.Sigmoid, [bg0], [b_sfp[0]])
                    act(sfp[1], ps_g1, AF.Sigmoid, [bg1], [b_sfp[1]])
                    tt("dve", sfp[0], sfp[0], ps_a, ALU.mult, [b_sfp[0], ba], [b_sfp[0]])
                    tt("dve", sfp[1], sfp[1], ps_b, ALU.mult, [b_sfp[1], bb], [b_sfp[1]])
                    tt("dve", mergedT[:, c, :], sfp[0], sfp[1], ALU.add, [b_sfp[0], b_sfp[1]], [b_mg[c]])
            if b == 0:
                dump("mergedT", mergedT, b_mg)
            slot2 = {}
            for tp2 in range(2):
                for cq in range(4):
                    po, bpo = wpiece("wo", cq)
                    cs = slice(cq * 256, (cq + 1) * 256)
                    ps, bps = nextbank()
                    for tl2 in range(2):
                        tl = tp2 * 2 + tl2
                        for k in range(8):
                            mm(ps[:, tl2 * 256:(tl2 + 1) * 256], mergedT[:, k, tl * 128:(tl + 1) * 128], po[:, k, :],
                               k == 0, k == 7, [bpo, b_mg[k]], [bps], inc=(k == 7 and tl2 == 1))
                    for tl2 in range(2):
                        tl = tp2 * 2 + tl2
                        si = tl2
                        tt("dve", sfp[si][:, 0:256], ps[:, tl2 * 256:(tl2 + 1) * 256], G1b[:, cs], ALU.mult,
                           [bps, b_G], [b_sfp[si]])
                        tt("dve", x1[:, tl, cs], sfp[si][:, 0:256], x1[:, tl, cs], ALU.add, [b_sfp[si], b_x1[tl]],
                           [b_x1[tl]])
                if tp2 == 1:
                    for tl in (0, 1):
                        nt_back(slot2[tl], a2, sh2, lambda k, tl=tl: hTe[:, k, tl * 128:(tl + 1) * 128], b_hTe[tl])
                for tl in (tp2 * 2, tp2 * 2 + 1):
                    slot2[tl] = nt_front(x1[:, tl, :], b_x1[tl])
            for tl in (2, 3):
                nt_back(slot2[tl], a2, sh2, lambda k, tl=tl: hTe[:, k, tl * 128:(tl + 1) * 128], b_hTe[tl])
            if b == 0:
                dump("x1", x1.rearrange("p t d -> p (t d)"), b_x1)
            S.alias(b_uT, u_old)
            h2 = slice(0, 512)
            b_h2 = b_hTe[0:4]
            for pc in range(16):
                p1, bp1 = wpiece("f1", pc)
                for cc in range(2):
                    ch = pc * 2 + cc
                    ps, bps = nextbank()
                    for k in range(8):
                        mm(ps, p1[:, k, cc * 128:(cc + 1) * 128], hTe[:, k, h2], k == 0, k == 7, [bp1] + b_h2, [bps],
                           inc=(k == 7))
                    si = ch % 3
                    act(sfp[si], ps, AF.Relu, [bps], [b_sfp[si]])
                    tt("dve", uT[:, ch, :], sfp[si], sfp[si], ALU.mult, [b_sfp[si]], [b_uT[ch]])
            nxt = hext_steps(b + 1) if b + 1 < NB else []
            for hf in range(2):
                cs = slice(hf * 512, (hf + 1) * 512)
                accs = [(banks[i], bank_b[i]) for i in range(4)]
                reserved.update(range(4))
                for g in range(8):
                    p2, bp2 = wpiece("f2", hf, g)
                    for kk in range(4):
                        kc = g * 4 + kk
                        for tl in range(4):
                            mm(accs[tl][0], uT[:, kc, tl * 128:(tl + 1) * 128], p2[:, kk, :], kc == 0, kc == 31,
                               [bp2, b_uT[kc]], [accs[tl][1]], inc=(kc == 31 or (kk == 3 and tl == 3)))
                    if nxt and not (hf == 1 and g == 7):
                        nxt.pop(0)()
                reserved.clear()
                for tl in range(4):
                    si = tl % 3
                    tt("dve", sfp[si], accs[tl][0], G2b[:, cs], ALU.mult, [accs[tl][1], b_G], [b_sfp[si]])
                    tt("dve", x1[:, tl, cs], sfp[si], x1[:, tl, cs], ALU.add, [b_sfp[si], b_x1[tl]], [b_x1[tl]])
            for tl in range(4):
                te = 4 * b + tl
                ss, bss = new_stat()
                act(junk, x1[:, tl, :], AF.Square, [b_x1[tl]], [b_junk, bss], accum=ss)
                rstd_of(ss, D, [bss])
                stt("dve", x1[:, tl, :], x1[:, tl, :], ss, gFb, ALU.mult, ALU.mult, [b_x1[tl], bss, b_G], [b_x1[tl]])
                dma("sp", out_d[te * 128:(te + 1) * 128, :], x1[:, tl, :], [b_x1[tl]], [])
            while nxt:
                nxt.pop(0)()
        S.barrier()
        S.emit()
    return nc


_PROGRAM = None


def _lay(v, k):
    return np.ascontiguousarray(np.asarray(v, dtype=np.float32).reshape(k, 128).T)


def kernel(x, c, positions, w_ada, b_ada, norm_mix, w_in, q_norm, w_uq, kv_norm, w_ukv,
           rel_bias, sink, w_o_mla, w_o_swa, w_out, norm_mlp, w_ff1, w_ff2, norm_final):
    global _PROGRAM
    if _PROGRAM is None:
        _PROGRAM = build_program()
    nc = _PROGRAM
    f = lambda a: np.ascontiguousarray(np.asarray(a, dtype=np.float32))
    consts = _host_consts()
    shared = {
        "w_ada": f(w_ada[0]), "b_ada_l": _lay(b_ada[0], 48), "norm_mix_l": _lay(norm_mix[0], 8),
        "w_in": f(w_in[0]), "q_norm_l": _lay(q_norm[0], 3), "w_uq": f(w_uq[0]),
        "kv_norm_l": _lay(kv_norm[0], 2), "w_ukv": f(w_ukv[0]), "rel_bias": f(rel_bias),
        "sink": f(sink[0]), "w_o_mla": f(w_o_mla[0]), "w_o_swa": f(w_o_swa[0]), "w_out": f(w_out[0]),
        "norm_mlp_l": _lay(norm_mlp[0], 8), "w_ff1": f(w_ff1[0]), "w_ff2": f(w_ff2[0]),
        "norm_final_l": _lay(norm_final, 8),
    }
    shared.update(consts)
    x = np.asarray(x, dtype=np.float32)
    c = np.asarray(c, dtype=np.float32)
    positions = np.asarray(positions, dtype=np.int32)
    in_maps = []
    for b in range(N_CORES):
        m = dict(shared)
        m["x"] = np.ascontiguousarray(x[b])
        m["c_l"] = _lay(c[b], 8)
        m["pos_l"] = np.ascontiguousarray(positions[b].reshape(NT, 128).T)
        in_maps.append(m)
    res = run_bass_kernel_spmd(nc, in_maps, core_ids=list(range(N_CORES)))
    kernel.last_results = res
    return np.stack([np.asarray(r["out"], dtype=np.float32) for r in res.results], axis=0)
```

```python
import math
import contextlib
import numpy as np
import concourse.bass as bass
import concourse.mybir as mybir
from concourse.bass_utils import run_bass_kernel_spmd

F32 = mybir.dt.float32
BF16 = mybir.dt.bfloat16
I32 = mybir.dt.int32
AF = mybir.ActivationFunctionType
ALU = mybir.AluOpType

D = 1024
SEQ = 4096
NT = SEQ // 128
NB = SEQ // 512
D_IN = 4288
EPS = 1e-6
MLA_SCALE = 192 ** -0.5
SWA_SCALE = 128 ** -0.5
NEG = -30000.0
TWO_PI = 2.0 * math.pi
C1 = 6.28125
C2 = TWO_PI - C1
PI_SAFE = 3.1415925
N_CORES = 8

DEBUG = {}


class Buf:
    __slots__ = ("name", "w", "r", "psum")

    def __init__(self, name="", psum=False):
        self.name = name
        self.w = None
        self.r = {}
        self.psum = psum


class Sched:
    ENGS = ("pe", "act", "dve", "pool", "sp")

    def __init__(self, nc, n_dma_sems=48):
        self.nc = nc
        self.ops = {e: [] for e in self.ENGS}
        self.cnt = {e: 0 for e in self.ENGS}
        self.known = {e: {} for e in self.ENGS}
        self.n_dma = n_dma_sems
        self.dma_cnt = [0] * n_dma_sems
        half = n_dma_sems // 2
        self.dma_pools = {"sp": list(range(0, half)), "pool": list(range(half, n_dma_sems))}
        self.dma_rr = {"sp": 0, "pool": 0}
        self.sems = {}

    def _need(self, X, waits, key, val):
        if self.known[X].get(key, 0) >= val:
            return
        if waits.get(key, 0) < val:
            waits[key] = val

    @staticmethod
    def _flat(bs):
        out = []
        for b in bs:
            if isinstance(b, (list, tuple)):
                out.extend(Sched._flat(b))
            else:
                out.append(b)
        return out

    def op(self, eng, fn, reads=(), writes=(), inc=True, dma=False):
        X = eng
        reads = self._flat(reads)
        writes = self._flat(writes)
        waits = {}
        for b in reads:
            if b.psum:
                for k, (v, e) in b.r.items():
                    if e != X:
                        self._need(X, waits, k, v)
            if b.w is not None:
                k, v, e = b.w
                if e == X and k == X and X == "pe":
                    continue
                self._need(X, waits, k, v)
        for b in writes:
            if b.w is not None:
                k, v, e = b.w
                if not (e == X and k == X):
                    self._need(X, waits, k, v)
            for k, (v, e) in b.r.items():
                if e == X and k == X:
                    continue
                self._need(X, waits, k, v)
        if dma:
            pl = "pool" if X == "pool" else "sp"
            lst = self.dma_pools[pl]
            i = lst[self.dma_rr[pl]]
            self.dma_rr[pl] = (self.dma_rr[pl] + 1) % len(lst)
            key = "d%d" % i
            if self.dma_cnt[i] > 0:
                self._need(X, waits, key, 16 * self.dma_cnt[i])
            self.dma_cnt[i] += 1
            tok = (key, 16 * self.dma_cnt[i], X)
            incspec = (key, 16)
        else:
            if inc:
                self.cnt[X] += 1
                tok = (X, self.cnt[X], X)
                incspec = (X, 1)
            else:
                tok = (X, self.cnt[X] + 1, X)
                incspec = None
        for k, v in waits.items():
            self.known[X][k] = v
        self.ops[X].append((tuple(waits.items()), fn, incspec))
        k, v, e = tok
        for b in reads:
            old = b.r.get(k)
            if old is None or old[0] < v:
                b.r[k] = (v, e)
        for b in writes:
            b.w = tok
            b.r = {}
        return tok

    def alias(self, news, olds):
        acc = {}
        for o in olds:
            for k, (v, e) in o.r.items():
                if k not in acc or acc[k][0] < v:
                    acc[k] = (v, "?")
            if o.w is not None:
                k, v, e = o.w
                if k not in acc or acc[k][0] < v:
                    acc[k] = (v, "?")
        for n in news:
            for k, (v, e) in acc.items():
                old = n.r.get(k)
                if old is None or old[0] < v:
                    n.r[k] = (v, e)

    def barrier(self):
        for X in self.ENGS:
            waits = {}
            for i in range(self.n_dma):
                if self.dma_cnt[i] > 0:
                    self._need(X, waits, "d%d" % i, 16 * self.dma_cnt[i])
            for e in self.ENGS:
                if e != X and self.cnt[e] > 0:
                    self._need(X, waits, e, self.cnt[e])
            for k, v in waits.items():
                self.known[X][k] = v
            self.ops[X].append((tuple(waits.items()), None, None))

    def emit(self):
        nc = self.nc
        with contextlib.ExitStack() as st:
            keys = list(self.ENGS) + ["d%d" % i for i in range(self.n_dma)]
            for k in keys:
                self.sems[k] = st.enter_context(nc.semaphore("s_" + k))
            block = st.enter_context(nc.Block())
            sems = self.sems

            def run(e, engobj):
                for waits, fn, incspec in self.ops[e]:
                    for k, v in waits:
                        engobj.wait_ge(sems[k], v)
                    if fn is None:
                        continue
                    ins = fn(engobj)
                    if incspec is not None:
                        ins.then_inc(sems[incspec[0]], incspec[1])

            @block.tensor
            def _(t):
                run("pe", t)

            @block.scalar
            def _(s):
                run("act", s)

            @block.vector
            def _(v):
                run("dve", v)

            @block.gpsimd
            def _(g):
                run("pool", g)

            @block.sync
            def _(s):
                run("sp", s)


def _t5_bucket_np(rel):
    rel = np.asarray(rel, dtype=np.int32)
    half, max_exact = 16, 8
    ret = np.where(rel > 0, half, 0)
    n = np.abs(rel)
    nf = np.maximum(n, 1).astype(np.float32)
    large = max_exact + (np.log(nf / np.float32(max_exact)) / np.float32(math.log(128 / max_exact))
                         * np.float32(half - max_exact)).astype(np.int32)
    large = np.minimum(large, half - 1)
    return ret + np.where(n < max_exact, n, large)


_BUCKET_FIX = {16: 10, 32: 12, 64: 14, 128: 15}


def _host_consts():
    ident = np.eye(128, dtype=np.float32)
    J = np.ascontiguousarray(ident[::-1])
    inv = (10000.0 ** (-np.arange(0, 64, 2, dtype=np.float32) / np.float32(64))).astype(np.float32)
    invt = np.ascontiguousarray(np.broadcast_to(inv[None, :], (128, 32))).astype(np.float32)
    oht = np.zeros((33, 512), dtype=np.float32)
    for m in range(512):
        rel = m - 255
        if abs(rel) <= 128:
            n = abs(rel)
            bk = int(_t5_bucket_np(rel))
            if n in _BUCKET_FIX:
                bk = _BUCKET_FIX[n] + (16 if rel > 0 else 0)
            oht[bk, m] = 1.0
        else:
            oht[32, m] = 1.0
    return {"ident": ident, "jrev": J, "invt": invt, "oht": oht}


ARENA_BYTES = 211968
EXTRA = 207872


def build_program():
    nc = bass.Bass("TRN2", target_bir_lowering=False)
    S = Sched(nc, n_dma_sems=48)

    def dram(name, shape, dt, kind="ExternalInput"):
        return nc.dram_tensor(name, shape, dt, kind=kind)

    x_d = dram("x", [SEQ, D], F32).ap()
    c_d = dram("c_l", [128, 8], F32).ap()
    pos_d = dram("pos_l", [128, NT], I32).ap()
    wada_d = dram("w_ada", [D, 6 * D], F32).ap()
    bada_d = dram("b_ada_l", [128, 48], F32).ap()
    nmix_d = dram("norm_mix_l", [128, 8], F32).ap()
    win_d = dram("w_in", [D, D_IN], F32).ap()
    qn_d = dram("q_norm_l", [128, 3], F32).ap()
    wuq_d = dram("w_uq", [384, 1536], F32).ap()
    kvn_d = dram("kv_norm_l", [128, 2], F32).ap()
    wukv_d = dram("w_ukv", [256, 2048], F32).ap()
    rb_d = dram("rel_bias", [32, 8], F32).ap()
    sink_d = dram("sink", [8], F32).ap()
    womla_d = dram("w_o_mla", [D, D], F32).ap()
    woswa_d = dram("w_o_swa", [D, D], F32).ap()
    wout_d = dram("w_out", [D, D], F32).ap()
    nmlp_d = dram("norm_mlp_l", [128, 8], F32).ap()
    wff1_d = dram("w_ff1", [D, 4 * D], F32).ap()
    wff2_d = dram("w_ff2", [4 * D, D], F32).ap()
    nfin_d = dram("norm_final_l", [128, 8], F32).ap()
    ident_d = dram("ident", [128, 128], F32).ap()
    jrev_d = dram("jrev", [128, 128], F32).ap()
    invt_d = dram("invt", [128, 32], F32).ap()
    oht_d = dram("oht", [33, 512], F32).ap()
    tbl_t = dram("tbl_scratch", [8, 512], F32, kind="Internal")
    NPIECE = 74
    wsc = dram("wsc", [NPIECE, 128, 2048], BF16, kind="Internal").ap()
    out_d = dram("out", [SEQ, D], F32, kind="ExternalOutput").ap()
    dbg_d = {}
    for name, shape in DEBUG.items():
        dbg_d[name] = dram("dbg_" + name, list(shape), F32, kind="ExternalOutput").ap()

    st = contextlib.ExitStack()
    with st:
        arena = st.enter_context(nc.sbuf_tensor("arena", [128, ARENA_BYTES // 2], BF16))
        banks = [st.enter_context(nc.psum_tensor("bank%d" % i, [128, 512], F32))[:, :] for i in range(8)]
        bank_b = [Buf("bank%d" % i, psum=True) for i in range(8)]

        def V(off, dt, shape, p0=0, p1=128):
            n = int(np.prod(shape))
            esz = 2 if dt == BF16 else 4
            assert off % 4 == 0 and off + n * esz <= ARENA_BYTES, (off, n, esz)
            v = arena[p0:p1, off // 2:(off + n * esz) // 2]
            if dt != BF16:
                v = v.bitcast(dt)
            if len(shape) == 2:
                v = v.rearrange("p (a b) -> p a b", a=shape[0])
            elif len(shape) == 3:
                v = v.rearrange("p (a b c) -> p a b c", a=shape[0], b=shape[1])
            elif len(shape) == 4:
                v = v.rearrange("p (a b c d) -> p a b c d", a=shape[0], b=shape[1], c=shape[2])
            return v

        def mm(out, lhsT, rhs, start, stop, R, W, inc):
            S.op("pe", lambda e: e.matmul(out, lhsT=lhsT, rhs=rhs, start=start, stop=stop),
                 reads=R, writes=W, inc=inc)

        def tp(out, in_, R, W, inc):
            S.op("pe", lambda e: e.transpose(out=out, in_=in_, identity=ident_f), reads=R + [b_const],
                 writes=W, inc=inc)

        def act(out, in_, func, R, W, bias=None, scale=None, accum=None):
            kw = {}
            if bias is not None:
                kw["bias"] = bias
            if scale is not None:
                kw["scale"] = scale
            if accum is not None:
                kw["accum_out"] = accum
            S.op("act", lambda e: e.activation(out=out, in_=in_, func=func, **kw), reads=R, writes=W)

        def ts(eng, out, in0, s1, s2, op0, op1, R, W):
            if op1 is None:
                S.op(eng, lambda e: e.tensor_scalar(out=out, in0=in0, scalar1=s1, scalar2=None, op0=op0),
                     reads=R, writes=W)
            else:
                S.op(eng, lambda e: e.tensor_scalar(out=out, in0=in0, scalar1=s1, scalar2=s2, op0=op0, op1=op1),
                     reads=R, writes=W)

        def tt(eng, out, in0, in1, op, R, W):
            S.op(eng, lambda e: e.tensor_tensor(out=out, in0=in0, in1=in1, op=op), reads=R, writes=W)

        def stt(eng, out, in0, scalar, in1, op0, op1, R, W):
            S.op(eng, lambda e: e.scalar_tensor_tensor(out=out, in0=in0, scalar=scalar, in1=in1, op0=op0, op1=op1),
                 reads=R, writes=W)

        def cp(eng, out, in_, R, W):
            if eng == "act":
                S.op("act", lambda e: e.copy(out=out, in_=in_), reads=R, writes=W)
            else:
                S.op(eng, lambda e: e.tensor_copy(out=out, in_=in_), reads=R, writes=W)

        def recip(out, in_, R, W):
            S.op("dve", lambda e: e.reciprocal(out=out, in_=in_), reads=R, writes=W)

        def dma(eng, out, in_, R, W):
            S.op(eng, lambda e: e.dma_start(out=out, in_=in_), reads=R, writes=W, dma=True)

        def memset(eng, ap, val, W):
            S.op(eng, lambda e: e.memset(ap, val), writes=W)

        bank_rr = [0]

        reserved = set()

        def nextbank():
            while True:
                i = bank_rr[0]
                bank_rr[0] = (i + 1) % 8
                if i not in reserved:
                    return banks[i], bank_b[i]

        def dump(name, ap, R):
            if name in dbg_d:
                dma("sp", dbg_d[name], ap, R, [])

        ident_f = V(0, F32, [128])
        ones_f = V(512, F32, [128])
        ones_bf = V(1024, BF16, [128])
        jrev_f = V(1280, F32, [128])
        b_const = Buf("const")
        SV = 2048

        def sv(i, n):
            return V(SV + 4 * i, F32, [n])

        cT = sv(0, 8); cexp = sv(8, 8); cact2 = V(SV + 64, F32, [8, 2])
        modT = sv(32, 48); badaT = sv(80, 48)
        nmix = sv(128, 8); nmlp = sv(136, 8); nfin = sv(144, 8)
        a1 = sv(152, 8); a2 = sv(160, 8)
        qn = sv(168, 3); kvn = sv(172, 2)
        sinkexp = sv(176, 8)
        pos_f = sv(184, 32)
        pos_i = V(SV + 4 * 216, I32, [32])
        stat = sv(248, 16)
        invt = sv(264, 32)
        cos_t = V(4096, F32, [32, 32])
        sin_t = V(8192, F32, [32, 32])
        rb_aug = V(3328, F32, [8], 0, 33)
        tbl_sb = V(12288, F32, [512], 0, 8)
        oht = V(14336, F32, [512], 0, 33)
        b_small = Buf("small")
        b_trig = Buf("trig")
        b_mod = Buf("mod")

        OT_OFF = 16384
        OT = V(OT_OFF, BF16, [8, SEQ])
        b_OT = [[Buf("OT%d_%d" % (h, q)) for q in range(NB)] for h in range(8)]
        P1O = 81920
        cqnT = V(P1O, BF16, [3, SEQ])
        ckvnT = V(P1O + 24576, BF16, [2, SEQ])
        KrT = V(P1O + 40960, BF16, [SEQ])
        b_cqn = [Buf("cqn%d" % t) for t in range(NT)]
        b_ckvn = [Buf("ckvn%d" % t) for t in range(NT)]
        b_kr = [Buf("kr%d" % t) for t in range(NT)]
        TR = 131072

        dma("sp", ident_f, ident_d, [], [b_const])
        dma("sp", jrev_f, jrev_d, [], [b_const])
        dma("sp", invt, invt_d, [], [b_small])
        dma("sp", oht, oht_d, [], [b_small])
        dma("sp", cT, c_d, [], [b_small])
        dma("sp", badaT, bada_d, [], [b_small])
        dma("sp", nmix, nmix_d, [], [b_small])
        dma("sp", nmlp, nmlp_d, [], [b_small])
        dma("sp", nfin, nfin_d, [], [b_small])
        dma("sp", qn, qn_d, [], [b_small])
        dma("sp", kvn, kvn_d, [], [b_small])
        dma("sp", pos_i, pos_d, [], [b_small])
        dma("sp", sinkexp, sink_d.partition_broadcast(128), [], [b_small])
        dma("sp", rb_aug[0:32, :], rb_d, [], [b_small])
        memset("dve", rb_aug[32:33, :], NEG, [b_small])
        memset("dve", ones_f, 1.0, [b_const])
        memset("dve", ones_bf, 1.0, [b_const])

        act(cexp, cT, AF.Exp, [b_small], [b_small], scale=-1.0)
        ts("dve", cexp, cexp, 1.0, None, ALU.add, None, [b_small], [b_small])
        recip(cexp, cexp, [b_small], [b_small])
        tt("dve", cact2[:, :, 0], cT, cexp, ALU.mult, [b_small], [b_small])
        tt("dve", cact2[:, :, 1], cT, cexp, ALU.mult, [b_small], [b_small])
        act(sinkexp, sinkexp, AF.Exp, [b_small], [b_small])

        tg = [V(TR + 32768 + 4096 * i, F32, [32, 32]) for i in range(4)]
        tgi = V(TR + 32768 + 4096 * 4, I32, [32, 32])
        b_tg = Buf("tg")
        cp("dve", pos_f, pos_i, [b_small], [b_small])
        ang, nf, rr, mk = tg
        tt("dve", ang, pos_f.unsqueeze(2).broadcast_to([128, 32, 32]),
           invt.unsqueeze(1).broadcast_to([128, 32, 32]), ALU.mult, [b_small], [b_tg])
        ts("dve", tgi, ang, 1.0 / TWO_PI, None, ALU.mult, None, [b_tg], [b_tg])
        cp("dve", nf, tgi, [b_tg], [b_tg])
        stt("dve", rr, nf, -C1, ang, ALU.mult, ALU.add, [b_tg], [b_tg])
        stt("dve", rr, nf, -C2, rr, ALU.mult, ALU.add, [b_tg], [b_tg])

        def wrap(r):
            ts("dve", mk, r, math.pi, TWO_PI, ALU.is_gt, ALU.mult, [b_tg], [b_tg])
            tt("dve", r, r, mk, ALU.subtract, [b_tg], [b_tg])
            ts("dve", mk, r, -math.pi, TWO_PI, ALU.is_lt, ALU.mult, [b_tg], [b_tg])
            tt("dve", r, r, mk, ALU.add, [b_tg], [b_tg])
            ts("dve", r, r, -PI_SAFE, PI_SAFE, ALU.max, ALU.min, [b_tg], [b_tg])

        wrap(rr)
        act(sin_t, rr, AF.Sin, [b_tg], [b_trig])
        ts("dve", rr, rr, math.pi / 2, None, ALU.add, None, [b_tg], [b_tg])
        wrap(rr)
        act(cos_t, rr, AF.Sin, [b_tg], [b_trig])

        stg = [V(TR + 16384 * i, F32, [8, 512]) for i in range(2)]
        b_stg = [Buf("stg0"), Buf("stg1")]
        psM, b_psM = nextbank()
        psMv = psM[:, 0:96].rearrange("p (a b) -> p a b", b=2)
        for pc in range(12):
            sl = pc % 2
            dma("sp", stg[sl], wada_d[:, pc * 512:(pc + 1) * 512].rearrange("(k p) n -> p k n", p=128),
                [], [b_stg[sl]])
            for j in range(4):
                cc = pc * 4 + j
                for k in range(8):
                    mm(psMv[:, cc, :], stg[sl][:, k, j * 128:(j + 1) * 128], cact2[:, k, :], k == 0, k == 7,
                       [b_stg[sl], b_small], [b_psM], inc=(k == 7))
        tt("dve", modT, psMv[:, :, 0], badaT, ALU.add, [b_psM, b_small], [b_mod])
        stt("dve", a1, modT[:, 8:16], 1.0, nmix, ALU.add, ALU.mult, [b_mod, b_small], [b_mod])
        stt("dve", a2, modT[:, 32:40], 1.0, nmlp, ALU.add, ALU.mult, [b_mod, b_small], [b_mod])
        sh1 = modT[:, 0:8]; g1v = modT[:, 16:24]; sh2 = modT[:, 24:32]; g2v = modT[:, 40:48]
        S.barrier()

        stat_rr = [0]

        def rstd_of(ss_ap, n, R):
            act(ss_ap, ss_ap, AF.Ln, R, R, bias=EPS, scale=1.0 / n)
            act(ss_ap, ss_ap, AF.Exp, R, R, scale=-0.5)
            return ss_ap

        stat_bufs = [Buf("stat%d" % i) for i in range(16)]

        def new_stat():
            i = stat_rr[0]
            stat_rr[0] = (i + 1) % 16
            return stat[:, i:i + 1], stat_bufs[i]

        w704 = V(TR, BF16, [8, 704]); b_w704 = Buf("w704")
        xt = [V(TR + 11264 + 4096 * i, F32, [1024]) for i in range(2)]; b_xt = [Buf(), Buf()]
        xn = [V(TR + 19456 + 4096 * i, F32, [1024]) for i in range(2)]; b_xn = [Buf(), Buf()]
        hT = V(TR + 27648, BF16, [8, 512]); b_hT = [[Buf() for _ in range(8)] for _ in range(4)]
        junk = V(TR + 35840, BF16, [1024]); b_junk = Buf("junk")
        cqkv = [V(TR + 37888 + 2560 * i, F32, [640]) for i in range(2)]; b_cqkv = [Buf(), Buf()]
        krr = [V(TR + 43008 + 512 * i, F32, [128]) for i in range(2)]; b_krr = [Buf(), Buf()]
        rtmp = [V(TR + 44032 + 128 * i, F32, [32]) for i in range(4)]; b_rtmp = Buf("rtmp")

        dma("pool", w704, win_d[:, 0:704].rearrange("(k p) n -> p k n", p=128), [], [b_w704])

        pspec = {}
        plist = []

        def addp(key, src2d, kch, ncols):
            pspec[key] = (len(plist), kch, ncols)
            plist.append((key, src2d, kch, ncols))

        addp(("ks",), win_d[:, 1728:1984], 8, 256)
        addp(("vs",), win_d[:, 1984:2240], 8, 256)
        for pc in range(4):
            addp(("qs", pc), win_d[:, 704 + pc * 256:704 + (pc + 1) * 256], 8, 256)
        for c in range(8):
            addp(("g0", c), win_d[:, 2240 + c * 128:2240 + (c + 1) * 128], 8, 128)
            addp(("g1", c), win_d[:, 3264 + c * 128:3264 + (c + 1) * 128], 8, 128)
            addp(("oa", c), womla_d[:, c * 128:(c + 1) * 128], 8, 128)
            addp(("ob", c), woswa_d[:, c * 128:(c + 1) * 128], 8, 128)
        for cq in range(4):
            addp(("wo", cq), wout_d[:, cq * 256:(cq + 1) * 256], 8, 256)
        for pc in range(16):
            addp(("f1", pc), wff1_d[:, pc * 256:(pc + 1) * 256], 8, 256)
        for hf in range(2):
            for g in range(8):
                addp(("f2", hf, g), wff2_d[g * 512:(g + 1) * 512, hf * 512:(hf + 1) * 512], 4, 512)
        assert len(plist) == NPIECE
        b_wsc = [Buf("wsc%d" % i) for i in range(NPIECE)]
        for i, (key, src2d, kch, ncols) in enumerate(plist):
            dma("pool", wsc[i][:, 0:kch * ncols].rearrange("p (k n) -> p k n", k=kch),
                src2d.rearrange("(k p) n -> p k n", p=128), [], [b_wsc[i]])

        def make_nt(ring, b_ring, fixed_banks=None):
            rr = [0]

            def front(src, bsrc, load=None):
                i = rr[0]
                rr[0] = (i + 1) % len(ring)
                if load is not None:
                    dma("sp", ring[i], load, [], [b_ring[i]])
                    src, bsrc = ring[i], b_ring[i]
                ss, bss = new_stat()
                act(junk, src, AF.Square, [bsrc], [b_junk, bss], accum=ss)
                rstd_of(ss, D, [bss])
                ts("dve", ring[i], src, ss, None, ALU.mult, None, [bsrc, bss], [b_ring[i]])
                return i

            def back(i, avec, shvec, dst_fn, bdst):
                for hf in range(2):
                    if fixed_banks is None:
                        ps, bps = nextbank()
                    else:
                        ps, bps = banks[fixed_banks[hf]], bank_b[fixed_banks[hf]]
                    for j in range(4):
                        k = hf * 4 + j
                        tp(ps[:, j * 128:(j + 1) * 128], ring[i][:, k * 128:(k + 1) * 128], [b_ring[i]], [bps],
                           inc=(j == 3))
                    for j in range(4):
                        k = hf * 4 + j
                        bd = bdst[k] if isinstance(bdst, list) else bdst
                        if hf == 0:
                            act(dst_fn(k), ps[:, j * 128:(j + 1) * 128], AF.Identity, [bps, b_mod], [bd],
                                bias=shvec[:, k:k + 1], scale=avec[:, k:k + 1])
                        else:
                            ts("dve", dst_fn(k), ps[:, j * 128:(j + 1) * 128], avec[:, k:k + 1], shvec[:, k:k + 1],
                               ALU.mult, ALU.add, [bps, b_mod], [bd])

            return front, back

        p1_front, p1_back = make_nt([xt[0], xt[1], xn[0], xn[1]], [b_xt[0], b_xt[1], b_xn[0], b_xn[1]], fixed_banks=(0, 1))
        p1_slot = {}
        p1_ps = {}

        def p1_A(t):
            p1_slot[t] = p1_front(None, None, load=x_d[t * 128:(t + 1) * 128, :])

        def p1_B(t):
            tl = t % 4
            p1_back(p1_slot.pop(t), a1, sh1, lambda k, tl=tl: hT[:, k, tl * 128:(tl + 1) * 128], b_hT[tl])
            ia = 2 + 2 * (t % 2)
            psA, bA, psB, bB = banks[ia], bank_b[ia], banks[ia + 1], bank_b[ia + 1]
            for k in range(8):
                mm(psA[:, 0:384], hT[:, k, tl * 128:(tl + 1) * 128], w704[:, k, 0:384], k == 0, k == 7,
                   [b_hT[tl], b_w704], [bA], inc=(k == 7))
            for k in range(8):
                mm(psB[:, 0:320], hT[:, k, tl * 128:(tl + 1) * 128], w704[:, k, 384:704], k == 0, k == 7,
                   [b_hT[tl], b_w704], [bB], inc=(k == 7))
            p1_ps[t] = (psA, bA, psB, bB)

        def p1_C(t):
            sl = t % 2
            psA, bA, psB, bB = p1_ps.pop(t)
            ssq, bq = new_stat()
            act(junk[:, 0:384], psA[:, 0:384], AF.Square, [bA], [b_junk, bq], accum=ssq)
            rstd_of(ssq, 384, [bq])
            sskv, bkv = new_stat()
            act(junk[:, 0:256], psB[:, 0:256], AF.Square, [bB], [b_junk, bkv], accum=sskv)
            rstd_of(sskv, 256, [bkv])
            ts("dve", cqkv[sl][:, 0:384], psA[:, 0:384], ssq, None, ALU.mult, None, [bA, bq], [b_cqkv[sl]])
            ts("dve", cqkv[sl][:, 384:640], psB[:, 0:256], sskv, None, ALU.mult, None, [bB, bkv], [b_cqkv[sl]])
            x1_ = psB[:, 256:288]; x2_ = psB[:, 288:320]
            ct = cos_t[:, t, :]; sn = sin_t[:, t, :]
            tt("dve", rtmp[0], x1_, ct, ALU.mult, [bB, b_trig], [b_rtmp])
            tt("dve", rtmp[1], x2_, sn, ALU.mult, [bB, b_trig], [b_rtmp])
            tt("dve", rtmp[2], x2_, ct, ALU.mult, [bB, b_trig], [b_rtmp])
            tt("dve", rtmp[3], x1_, sn, ALU.mult, [bB, b_trig], [b_rtmp])
            tt("dve", krr[sl][:, 0:32], rtmp[0], rtmp[1], ALU.subtract, [b_rtmp], [b_krr[sl]])
            tt("dve", krr[sl][:, 32:64], rtmp[2], rtmp[3], ALU.add, [b_rtmp], [b_krr[sl]])
            cp("dve", krr[sl][:, 64:128], krr[sl][:, 0:64], [b_krr[sl]], [b_krr[sl]])
            psT, bT = banks[6], bank_b[6]
            for j in range(3):
                tp(psT[:, j * 128:(j + 1) * 128], cqkv[sl][:, j * 128:(j + 1) * 128], [b_cqkv[sl]], [bT], inc=(j == 2))
            psU, bU = banks[7], bank_b[7]
            for j in range(2):
                tp(psU[:, j * 128:(j + 1) * 128], cqkv[sl][:, 384 + j * 128:384 + (j + 1) * 128], [b_cqkv[sl]], [bU],
                   inc=False)
            tp(psU[:, 256:384], krr[sl], [b_krr[sl]], [bU], inc=True)
            tok = slice(t * 128, (t + 1) * 128)
            for j in range(3):
                ts("dve", cqnT[:, j, tok], psT[:, j * 128:(j + 1) * 128], qn[:, j:j + 1], None, ALU.mult, None,
                   [bT, b_small], [b_cqn[t]])
            for j in range(2):
                act(ckvnT[:, j, tok], psU[:, j * 128:(j + 1) * 128], AF.Identity, [bU, b_small], [b_ckvn[t]],
                    scale=kvn[:, j:j + 1])
            cp("act", KrT[:, tok], psU[:, 256:384], [bU], [b_kr[t]])

        p1_A(0)
        p1_A(1)
        for n in range(1, NT + 2):
            if 0 <= n - 1 < NT:
                p1_B(n - 1)
            if 0 <= n - 2 < NT:
                p1_C(n - 2)
            if n + 1 < NT:
                p1_A(n + 1)
        dump("cqnT", cqnT[:, :, 0:512], b_cqn[0:4])
        dump("ckvnT", ckvnT[:, :, 0:512], b_ckvn[0:4])
        dump("KrT", KrT[:, 0:512], b_kr[0:4])
        S.barrier()

        KT = [V(TR + 8192 * i, BF16, [SEQ]) for i in range(2)]
        Vt = [V(TR + 16384 + 8192 * i, BF16, [NT, 128]) for i in range(2)]
        QTn = [V(TR + 32768 + 8192 * i, BF16, [SEQ]) for i in range(2)]
        QTr = V(TR + 49152, BF16, [SEQ])
        ropeT = [V(TR + 57344 + 2048 * i, F32, [4, 2, 32]) for i in range(4)]
        b_KT = [[Buf() for _ in range(NB)] for _ in range(2)]
        b_V = [[Buf() for _ in range(NB)] for _ in range(2)]
        b_QTn = [[Buf() for _ in range(NB)] for _ in range(2)]
        b_QTr = [Buf() for _ in range(NB)]
        b_ropeT = Buf("ropeT")
        wqn = [V(TR + 65536 + 1792 * i, BF16, [3, 128]) for i in range(2)]
        wk = [V(TR + 65536 + 1792 * i + 768, BF16, [2, 128]) for i in range(2)]
        wv = [V(TR + 65536 + 1792 * i + 1280, BF16, [2, 128]) for i in range(2)]
        b_wh = [Buf(), Buf()]
        wqr = V(TR + 69120, BF16, [3, 2, 64]); b_wqr = Buf("wqr")
        qrot = V(TR + 69120 + 768, F32, [4, 2, 2, 32])
        PT = [V(TR + 70656 + 1024 * i, BF16, [512]) for i in range(4)] + [V(TR + 57344 + 7168, BF16, [512])]
        b_PT = [Buf() for _ in range(5)]
        recs = [V(TR + 74752, F32, [512])] * 2
        b_recs = [Buf("rec")] * 2
        sc_rr = [0]
        ropeT = [V(TR + 57344 + 1024 * i, F32, [4, 2, 32]) for i in range(3)]
        qrot = V(TR + 57344 + 3072, F32, [4, 2, 2, 32])
        b_qrot = Buf("qrot")
        QTrz = [V(TR + 57344 + 5120 + 1024 * i, BF16, [512]) for i in range(2)]
        b_QTrz = [Buf(), Buf()]
        pt_rr = [0]
        ev_rr = [0]
        dacc = [V(EXTRA + 2048 * i, F32, [512]) for i in range(2)]; b_dacc = [Buf(), Buf()]
        dacc_rr = [0]

        def evac(out, in_, R, W):
            ev_rr[0] += 1
            cp("dve", out, in_, R, W)

        for hp in range(4):
            for e in range(2):
                h = 2 * hp + e
                dma("pool", wqr[:, :, e, :],
                    wuq_d[:, h * 192 + 128:h * 192 + 192].rearrange("(k p) n -> p k n", p=128), [], [b_wqr])
            for g in range(NB):
                ps, bps = nextbank()
                for tl in range(4):
                    t = 4 * g + tl
                    for kc in range(3):
                        mm(ps[:, tl * 128:(tl + 1) * 128], cqnT[:, kc, t * 128:(t + 1) * 128],
                           wqr[:, kc, :, :], kc == 0, kc == 2, [b_cqn[t], b_wqr], [bps], inc=(kc == 2 and tl == 3))
                psv = ps.rearrange("p (t h f i) -> p t h f i", t=4, h=2, f=2)
                cb = cos_t[:, 4 * g:4 * g + 4, :].unsqueeze(2).broadcast_to([128, 4, 2, 32])
                sb_ = sin_t[:, 4 * g:4 * g + 4, :].unsqueeze(2).broadcast_to([128, 4, 2, 32])
                x1 = psv[:, :, :, 0, :]; x2 = psv[:, :, :, 1, :]
                tt("dve", ropeT[0], x1, cb, ALU.mult, [bps, b_trig], [b_ropeT])
                tt("dve", ropeT[1], x2, sb_, ALU.mult, [bps, b_trig], [b_ropeT])
                tt("dve", qrot[:, :, :, 0, :], ropeT[0], ropeT[1], ALU.subtract, [b_ropeT], [b_qrot])
                tt("dve", ropeT[0], x2, cb, ALU.mult, [bps, b_trig], [b_ropeT])
                tt("dve", ropeT[1], x1, sb_, ALU.mult, [bps, b_trig], [b_ropeT])
                tt("dve", qrot[:, :, :, 1, :], ropeT[0], ropeT[1], ALU.add, [b_ropeT], [b_qrot])
                ps2, bps2 = nextbank()
                qflat = qrot.rearrange("p t h f i -> p (t h f i)")
                for tl in range(4):
                    tp(ps2[:, tl * 128:(tl + 1) * 128], qflat[:, tl * 128:(tl + 1) * 128], [b_qrot], [bps2], inc=(tl == 3))
                evac(QTr[:, g * 512:(g + 1) * 512], ps2, [bps2], [b_QTr[g]])
            for e in range(2):
                h = 2 * hp + e
                sl = e
                dma("pool", wqn[sl], wuq_d[:, h * 192:h * 192 + 128].rearrange("(k p) n -> p k n", p=128), [], [b_wh[sl]])
                dma("pool", wk[sl], wukv_d[:, h * 256:h * 256 + 128].rearrange("(k p) n -> p k n", p=128), [], [b_wh[sl]])
                dma("pool", wv[sl], wukv_d[:, h * 256 + 128:h * 256 + 256].rearrange("(k p) n -> p k n", p=128), [],
                    [b_wh[sl]])
                for g in range(NB):
                    cols = slice(g * 512, (g + 1) * 512)
                    ps, bps = nextbank()
                    for kc in range(3):
                        mm(ps, wqn[sl][:, kc, :], cqnT[:, kc, cols], kc == 0, kc == 2,
                           [b_wh[sl]] + b_cqn[4 * g:4 * g + 4], [bps], inc=(kc == 2))
                    evac(QTn[sl][:, cols], ps, [bps], [b_QTn[sl][g]])
                    ps, bps = nextbank()
                    for kc in range(2):
                        mm(ps, wk[sl][:, kc, :], ckvnT[:, kc, cols], kc == 0, kc == 1,
                           [b_wh[sl]] + b_ckvn[4 * g:4 * g + 4], [bps], inc=(kc == 1))
                    evac(KT[sl][:, cols], ps, [bps], [b_KT[sl][g]])
                    ps, bps = nextbank()
                    for tl in range(4):
                        t = 4 * g + tl
                        for kc in range(2):
                            mm(ps[:, tl * 128:(tl + 1) * 128], ckvnT[:, kc, t * 128:(t + 1) * 128], wv[sl][:, kc, :],
                               kc == 0, kc == 1, [b_wh[sl], b_ckvn[t]], [bps], inc=(kc == 1 and tl == 3))
                    evac(Vt[sl][:, 4 * g:4 * g + 4, :].rearrange("p t d -> p (t d)"), ps, [bps], [b_V[sl][g]])
                for zi in range(2):
                    memset("pool", QTrz[zi][(1 - e) * 64:(2 - e) * 64, :], 0.0, [b_QTrz[zi]])
                LA = 3
                items = [(qb, kc) for qb in range(NB) for kc in range(NT)]
                accs = {}
                pend = {}
                dst = {}
                deferred = []

                def acc_of(qb):
                    if qb not in accs:
                        a = (qb % 2) * 2
                        accs[qb] = (banks[a], bank_b[a], banks[a + 1], bank_b[a + 1])
                    return accs[qb]

                def scores(qb, kc, sl=sl, e=e):
                    qc = slice(qb * 512, (qb + 1) * 512)
                    zi = qb % 2
                    if kc == 0:
                        cp("pool", QTrz[zi][e * 64:(e + 1) * 64, :], QTr[e * 64:(e + 1) * 64, qc], [b_QTr[qb]],
                           [b_QTrz[zi]])
                    si = 4 + sc_rr[0]
                    sc_rr[0] = (sc_rr[0] + 1) % 4
                    ps, bps = banks[si], bank_b[si]
                    kcs = slice(kc * 128, (kc + 1) * 128)
                    mm(ps, KT[sl][:, kcs], QTn[sl][:, qc], True, False,
                       [b_KT[sl][kc // 4], b_QTn[sl][qb]], [bps], inc=False)
                    mm(ps, KrT[:, kcs], QTrz[zi], False, True, [b_kr[kc], b_QTrz[zi]], [bps], inc=True)
                    i = pt_rr[0]
                    pt_rr[0] = (i + 1) % len(PT)
                    act(PT[i], ps, AF.Exp, [bps], [b_PT[i]], scale=MLA_SCALE)
                    pend[(qb, kc)] = i

                def pv(qb, kc, sl=sl):
                    accO, bO, accD, bD = acc_of(qb)
                    i = pend.pop((qb, kc))
                    on_pe = False
                    mm(accO, Vt[sl][:, kc, :], PT[i], kc == 0, kc == NT - 1, [b_V[sl][kc // 4], b_PT[i]], [bO],
                       inc=not on_pe)
                    if on_pe:
                        mm(accD, ones_bf, PT[i], kc == 7, False, [b_const, b_PT[i]], [bD], inc=True)
                    else:
                        d = dst.setdefault(qb, {"n": 0, "used": [False, False]})
                        j = d["n"] % 2
                        d["n"] += 1
                        if not d["used"][j]:
                            d["used"][j] = True
                            cp("dve", dacc[j], PT[i], [b_PT[i]], [b_dacc[j]])
                        else:
                            tt("dve", dacc[j], dacc[j], PT[i], ALU.add, [b_dacc[j], b_PT[i]], [b_dacc[j]])

                def epi_pe(qb):
                    accO, bO, accD, bD = acc_of(qb)
                    ri = qb % 2
                    mm(accD, ones_f, recs[ri], True, True, [b_const, b_recs[ri]], [bD], inc=True)

                def epi_dve(qb, h=h):
                    accO, bO, accD, bD = acc_of(qb)
                    ri = qb % 2
                    qc = slice(qb * 512, (qb + 1) * 512)
                    act(recs[ri], accD, AF.Ln, [bD], [b_recs[ri]])
                    act(recs[ri], recs[ri], AF.Exp, [b_recs[ri]], [b_recs[ri]], scale=-1.0)
                    tt("dve", OT[:, h, qc], accO, recs[ri], ALU.mult, [bO, b_recs[ri]], [b_OT[h][qb]])

                for n in range(min(LA, len(items))):
                    scores(*items[n])
                for n, (qb, kc) in enumerate(items):
                    pv(qb, kc)
                    if n + LA < len(items):
                        scores(*items[n + LA])
                    if kc == NT - 1:
                        ri = qb % 2
                        tt("dve", recs[ri], dacc[0], dacc[1], ALU.add, [b_dacc[0], b_dacc[1]], [b_recs[ri]])
                        deferred.append(qb)
                    if kc == 3 and deferred:
                        epi_pe(deferred[0])
                    if kc == 5 and deferred:
                        epi_dve(deferred.pop(0))
                for qb in deferred:
                    epi_pe(qb)
                    epi_dve(qb)
        dump("OT", OT[:, :, 0:512].bitcast(BF16) if False else OT[:, :, 0:512], [b_OT[h][0] for h in range(8)])
        S.barrier()

        P3 = P1O
        G1b = V(P3, F32, [1024]); G2b = V(P3 + 4096, F32, [1024]); gFb = V(P3 + 8192, F32, [1024])
        Bm = V(P3 + 12288, F32, [3, 8, 128])
        b_G = Buf("G"); b_Bm = Buf("Bm")
        hTe = V(P3 + 24576, BF16, [8, 768]); b_hTe = [[Buf() for _ in range(8)] for _ in range(6)]
        U = P3 + 36864
        ksT = V(U, BF16, [2, 768]); b_ks = Buf("ks")
        vs = V(U + 3072, BF16, [6, 256]); b_vs = [Buf() for _ in range(6)]
        qsT = V(U + 6144, BF16, [8, 512]); b_qs = [Buf() for _ in range(8)]
        swaOT = V(U + 14336, BF16, [8, 512]); b_swaO = [[Buf() for _ in range(4)] for _ in range(2)]
        mergedT = V(U + 22528, BF16, [8, 512]); b_mg = [Buf() for _ in range(8)]
        uT = V(U, BF16, [32, 512]); b_uT = [Buf() for _ in range(32)]
        u_old = [b_ks] + b_vs + b_qs + b_swaO[0] + b_swaO[1] + b_mg
        XB = U + 32768
        xb = [V(XB + 4096 * i, F32, [1024]) for i in range(2)]; b_xb = [Buf(), Buf()]
        x1 = V(XB + 8192, F32, [4, 1024]); b_x1 = [Buf() for _ in range(4)]
        PT3 = [V(XB + 24576 + 1024 * i, BF16, [4, 128]) for i in range(3)]; b_PT3 = [Buf() for _ in range(3)]
        WR = XB + 27648
        b_wrh = [Buf() for _ in range(8)]
        b_wr = [[b_wrh[0], b_wrh[1]]]
        xn3 = V(WR + 16384, F32, [1024]); b_xn3 = Buf("xn3")
        sfp = [V(WR + 20480 + 2048 * i, F32, [512]) for i in range(3)]; b_sfp = [Buf() for _ in range(3)]
        junk3 = V(WR + 26624, BF16, [1024])
        dtmp = xn3[:, 0:512]; b_dtmp = b_xn3
        for i_ in range(4):
            PT3.append(V(EXTRA + 1024 * i_, BF16, [4, 128])); b_PT3.append(Buf())
        swb_rr = [0]; sfp_rr = [0]; pt3_rr = [0]
        assert WR + 26624 + 2048 <= ARENA_BYTES
        junk = junk3
        wr_rr = [0]

        def wpiece(*key):
            idx, kch, ncols = pspec[key]
            i = wr_rr[0]
            nh = 1 if kch * ncols <= 1024 else 2
            if nh == 2 and i % 2 == 1:
                i += 1
            i %= 8
            wr_rr[0] = (i + nh) % 8
            bw = b_wrh[i:i + nh]
            v = V(WR + 2048 * i, BF16, [kch, ncols])
            dma("pool", v, wsc[idx][:, 0:kch * ncols].rearrange("p (k n) -> p k n", k=kch), [b_wsc[idx]], bw)
            return v, bw

        dg = [V(WR + 20480 + 2048 * i, F32, [128]) for i in range(2)]
        for gi, (gvec, Gb, bsrc) in enumerate(((g1v, G1b, b_mod), (g2v, G2b, b_mod), (nfin, gFb, b_small))):
            for hf in range(2):
                ps, bps = nextbank()
                for j in range(4):
                    c = hf * 4 + j
                    d = dg[c % 2]
                    bd = b_sfp[c % 2]
                    ts("dve", d, ident_f, gvec[:, c:c + 1], None, ALU.mult, None, [b_const, bsrc], [bd])
                    mm(ps[:, j * 128:(j + 1) * 128], ones_f, d, True, True, [b_const, bd], [bps], inc=True)
                cp("dve", Gb[:, hf * 512:(hf + 1) * 512], ps, [bps], [b_G])
        ps, bps = nextbank()
        mm(ps[0:8, :], rb_aug, oht, True, True, [b_small], [bps], inc=True)
        b_tbl = Buf("tbl")
        cp("dve", tbl_sb, ps[0:8, :], [bps], [b_tbl])
        b_tbld = Buf("tbld")
        dma("sp", tbl_t.ap(), tbl_sb, [b_tbl], [b_tbld])
        hank = V(WR, F32, [8, 128])
        for dl in range(3):
            src = bass.AP(tensor=tbl_t, offset=dl * 128, ap=[[1, 128], [512, 8], [1, 128]])
            dma("sp", hank, src, [b_tbld], [b_wr[0]])
            for hh in range(2):
                ps, bps = nextbank()
                for j in range(4):
                    h = hh * 4 + j
                    mm(ps[:, j * 128:(j + 1) * 128], hank[:, h, :], jrev_f, True, True, [b_wr[0], b_const], [bps],
                       inc=(j == 3))
                cp("dve", Bm[:, dl, hh * 4:(hh + 1) * 4, :].rearrange("p h q -> p (h q)"), ps, [bps], [b_Bm])
        dump("Bm", Bm.rearrange("p a h q -> p (a h q)"), [b_Bm])
        dump("G1b", G1b, [b_G])

        nt_front, nt_back = make_nt([xb[0], xb[1], xn3], [b_xb[0], b_xb[1], b_xn3])

        def hext_steps(b):
            valid = [j for j in range(6) if 0 <= 4 * b - 1 + j < NT]
            slot = {}
            steps = []

            def mk_front(j):
                def f():
                    te = 4 * b - 1 + j
                    slot[j] = nt_front(None, None, load=x_d[te * 128:(te + 1) * 128, :])
                return f

            def mk_back(j):
                def f():
                    nt_back(slot[j], a1, sh1, lambda k, j=j: hTe[:, k, j * 128:(j + 1) * 128], b_hTe[j])
                return f

            for n, j in enumerate(valid):
                steps.append(mk_front(j))
                if n >= 1:
                    steps.append(mk_back(valid[n - 1]))
            steps.append(mk_back(valid[-1]))
            return steps

        def act_recip(buf, src, R, W):
            act(buf, src, AF.Ln, R, W)
            act(buf, buf, AF.Exp, W, W, scale=-1.0)

        for st_ in hext_steps(0):
            st_()
        for b in range(NB):
            S.alias(u_old, b_uT)
            valid = [j for j in range(6) if 0 <= 4 * b - 1 + j < NT]
            own = slice(128, 640)
            b_own = b_hTe[1:5]
            for tl in range(4):
                te = 4 * b + tl
                dma("sp", x1[:, tl, :], x_d[te * 128:(te + 1) * 128, :], [], [b_x1[tl]])
            wp, bwp = wpiece("ks")
            for kv in range(2):
                for (j0, j1) in ((0, 4), (4, 6)):
                    js = [j for j in valid if j0 <= j < j1]
                    if not js:
                        continue
                    cs = slice(js[0] * 128, (js[-1] + 1) * 128)
                    n = (js[-1] + 1 - js[0]) * 128
                    ps, bps = nextbank()
                    for k in range(8):
                        mm(ps[:, 0:n], wp[:, k, kv * 128:(kv + 1) * 128], hTe[:, k, cs], k == 0, k == 7,
                           [bwp] + [b_hTe[j] for j in js], [bps], inc=(k == 7))
                    cp("act", ksT[:, kv, cs], ps[:, 0:n], [bps], [b_ks])
            wp, bwp = wpiece("vs")
            for j in valid:
                ps, bps = nextbank()
                for k in range(8):
                    mm(ps[:, 0:256], hTe[:, k, j * 128:(j + 1) * 128], wp[:, k, :], k == 0, k == 7,
                       [bwp, b_hTe[j]], [bps], inc=(k == 7))
                cp("act", vs[:, j, :], ps[:, 0:256], [bps], [b_vs[j]])
            for pc in range(4):
                wp, bwp = wpiece("qs", pc)
                for hh in range(2):
                    h = pc * 2 + hh
                    ps, bps = nextbank()
                    for k in range(8):
                        mm(ps, wp[:, k, hh * 128:(hh + 1) * 128], hTe[:, k, own], k == 0, k == 7,
                           [bwp] + b_own, [bps], inc=(k == 7))
                    cp("act", qsT[:, h, :], ps, [bps], [b_qs[h]])
            units = [(qt, kv) for qt in range(4) for kv in range(2)]
            sw = {}

            def swa_scores(u):
                qt, kv = units[u]
                j = qt + 1
                hs = slice(kv * 4, (kv + 1) * 4)
                dls = [dl for dl in (-1, 0, 1) if (j + dl) in valid]
                pts = []
                for dl in dls:
                    jk = j + dl
                    bi = 4 + swb_rr[0]
                    swb_rr[0] = (swb_rr[0] + 1) % 4
                    ps, bps = banks[bi], bank_b[bi]
                    psv = ps.rearrange("p (h q) -> p h q", h=4)
                    mm(psv, ksT[:, kv, jk * 128:(jk + 1) * 128], qsT[:, hs, qt * 128:(qt + 1) * 128], True, True,
                       [b_ks] + b_qs[kv * 4:(kv + 1) * 4], [bps], inc=True)
                    si = sfp_rr[0]
                    sfp_rr[0] = (si + 1) % 3
                    pi = pt3_rr[0]
                    pt3_rr[0] = (pi + 1) % len(PT3)
                    sv_ = sfp[si].rearrange("p (h q) -> p h q", h=4)
                    stt("dve", sv_, psv, SWA_SCALE, Bm[:, dl + 1, hs, :], ALU.mult, ALU.add, [bps, b_Bm],
                        [b_sfp[si]])
                    act(PT3[pi], sv_, AF.Exp, [b_sfp[si]], [b_PT3[pi]])
                    pts.append((jk, pi))
                sw[u] = pts

            def swa_pv(u):
                qt, kv = units[u]
                hs = slice(kv * 4, (kv + 1) * 4)
                a = (u % 2) * 2
                accO, bO, accD, bD = banks[a], bank_b[a], banks[a + 1], bank_b[a + 1]
                pts = sw.pop(u)
                for n_i, (jk, pi) in enumerate(pts):
                    first = (n_i == 0)
                    last = (n_i == len(pts) - 1)
                    ptf = PT3[pi].rearrange("p h q -> p (h q)")
                    mm(accO, vs[:, jk, kv * 128:(kv + 1) * 128], ptf, first, last, [b_vs[jk], b_PT3[pi]], [bO],
                       inc=False)
                    mm(accD, ones_bf, ptf, first, last, [b_const, b_PT3[pi]], [bD], inc=True)
                dv = dtmp.rearrange("p (h q) -> p h q", h=4)
                tt("dve", dv, accD.rearrange("p (h q) -> p h q", h=4),
                   sinkexp[:, hs].unsqueeze(2).broadcast_to([128, 4, 128]), ALU.add, [bD, b_small], [b_dtmp])
                act_recip(dtmp, dtmp, [b_dtmp], [b_dtmp])
                tt("dve", swaOT[:, hs, qt * 128:(qt + 1) * 128], accO.rearrange("p (h q) -> p h q", h=4), dv,
                   ALU.mult, [bO, b_dtmp], [b_swaO[kv][qt]])

            reserved.update(range(4))
            swa_scores(0)
            for u in range(len(units)):
                if u + 1 < len(units):
                    swa_scores(u + 1)
                swa_pv(u)
            reserved.clear()
            if b == 0:
                dump("swaOT", swaOT, b_swaO[0] + b_swaO[1])
            for c in range(8):
                if True:
                    pg0, bpg0 = wpiece("g0", c)
                    pg1, bpg1 = wpiece("g1", c)
                    pa, bpa = wpiece("oa", c)
                    pb, bpb = wpiece("ob", c)
                    ccs = slice(0, 128)
                    ps_g0, bg0 = nextbank()
                    for k in range(8):
                        mm(ps_g0, pg0[:, k, ccs], hTe[:, k, own], k == 0, k == 7, [bpg0] + b_own, [bg0], inc=(k == 7))
                    ps_g1, bg1 = nextbank()
                    for k in range(8):
                        mm(ps_g1, pg1[:, k, ccs], hTe[:, k, own], k == 0, k == 7, [bpg1] + b_own, [bg1], inc=(k == 7))
                    ps_a, ba = nextbank()
                    for h in range(8):
                        mm(ps_a, pa[:, h, ccs], OT[:, h, b * 512:(b + 1) * 512], h == 0, h == 7, [bpa, b_OT[h][b]],
                           [ba], inc=(h == 7))
                    ps_b, bb = nextbank()
                    for h in range(8):
                        mm(ps_b, pb[:, h, ccs], swaOT[:, h, :], h == 0, h == 7,
                           [bpb] + [b_swaO[h // 4][q] for q in range(4)], [bb], inc=(h == 7))
                    act(sfp[0], ps_g0, AF.Sigmoid, [bg0], [b_sfp[0]])
                    act(sfp[1], ps_g1, AF.Sigmoid, [bg1], [b_sfp[1]])
                    tt("dve", sfp[0], sfp[0], ps_a, ALU.mult, [b_sfp[0], ba], [b_sfp[0]])
                    tt("dve", sfp[1], sfp[1], ps_b, ALU.mult, [b_sfp[1], bb], [b_sfp[1]])
                    tt("dve", mergedT[:, c, :], sfp[0], sfp[1], ALU.add, [b_sfp[0], b_sfp[1]], [b_mg[c]])
            if b == 0:
                dump("mergedT", mergedT, b_mg)
            slot2 = {}
            for tp2 in range(2):
                for cq in range(4):
                    po, bpo = wpiece("wo", cq)
                    cs = slice(cq * 256, (cq + 1) * 256)
                    ps, bps = nextbank()
                    for tl2 in range(2):
                        tl = tp2 * 2 + tl2
                        for k in range(8):
                            mm(ps[:, tl2 * 256:(tl2 + 1) * 256], mergedT[:, k, tl * 128:(tl + 1) * 128], po[:, k, :],
                               k == 0, k == 7, [bpo, b_mg[k]], [bps], inc=(k == 7 and tl2 == 1))
                    for tl2 in range(2):
                        tl = tp2 * 2 + tl2
                        si = tl2
                        tt("dve", sfp[si][:, 0:256], ps[:, tl2 * 256:(tl2 + 1) * 256], G1b[:, cs], ALU.mult,
                           [bps, b_G], [b_sfp[si]])
                        tt("dve", x1[:, tl, cs], sfp[si][:, 0:256], x1[:, tl, cs], ALU.add, [b_sfp[si], b_x1[tl]],
                           [b_x1[tl]])
                if tp2 == 1:
                    for tl in (0, 1):
                        nt_back(slot2[tl], a2, sh2, lambda k, tl=tl: hTe[:, k, tl * 128:(tl + 1) * 128], b_hTe[tl])
                for tl in (tp2 * 2, tp2 * 2 + 1):
                    slot2[tl] = nt_front(x1[:, tl, :], b_x1[tl])
            for tl in (2, 3):
                nt_back(slot2[tl], a2, sh2, lambda k, tl=tl: hTe[:, k, tl * 128:(tl + 1) * 128], b_hTe[tl])
            if b == 0:
                dump("x1", x1.rearrange("p t d -> p (t d)"), b_x1)
            S.alias(b_uT, u_old)
            h2 = slice(0, 512)
            b_h2 = b_hTe[0:4]
            for pc in range(16):
                p1, bp1 = wpiece("f1", pc)
                for cc in range(2):
                    ch = pc * 2 + cc
                    ps, bps = nextbank()
                    for k in range(8):
                        mm(ps, p1[:, k, cc * 128:(cc + 1) * 128], hTe[:, k, h2], k == 0, k == 7, [bp1] + b_h2, [bps],
                           inc=(k == 7))
                    si = ch % 3
                    act(sfp[si], ps, AF.Relu, [bps], [b_sfp[si]])
                    tt("dve", uT[:, ch, :], sfp[si], sfp[si], ALU.mult, [b_sfp[si]], [b_uT[ch]])
            nxt = hext_steps(b + 1) if b + 1 < NB else []
            for hf in range(2):
                cs = slice(hf * 512, (hf + 1) * 512)
                accs = [(banks[i], bank_b[i]) for i in range(4)]
                reserved.update(range(4))
                for g in range(8):
                    p2, bp2 = wpiece("f2", hf, g)
                    for kk in range(4):
                        kc = g * 4 + kk
                        for tl in range(4):
                            mm(accs[tl][0], uT[:, kc, tl * 128:(tl + 1) * 128], p2[:, kk, :], kc == 0, kc == 31,
                               [bp2, b_uT[kc]], [accs[tl][1]], inc=(kc == 31 or (kk == 3 and tl == 3)))
                    if nxt and not (hf == 1 and g == 7):
                        nxt.pop(0)()
                reserved.clear()
                for tl in range(4):
                    si = tl % 3
                    tt("dve", sfp[si], accs[tl][0], G2b[:, cs], ALU.mult, [accs[tl][1], b_G], [b_sfp[si]])
                    tt("dve", x1[:, tl, cs], sfp[si], x1[:, tl, cs], ALU.add, [b_sfp[si], b_x1[tl]], [b_x1[tl]])
            for tl in range(4):
                te = 4 * b + tl
                ss, bss = new_stat()
                act(junk, x1[:, tl, :], AF.Square, [b_x1[tl]], [b_junk, bss], accum=ss)
                rstd_of(ss, D, [bss])
                stt("dve", x1[:, tl, :], x1[:, tl, :], ss, gFb, ALU.mult, ALU.mult, [b_x1[tl], bss, b_G], [b_x1[tl]])
                dma("sp", out_d[te * 128:(te + 1) * 128, :], x1[:, tl, :], [b_x1[tl]], [])
            while nxt:
                nxt.pop(0)()
        S.barrier()
        S.emit()
    return nc


_PROGRAM = None


def _lay(v, k):
    return np.ascontiguousarray(np.asarray(v, dtype=np.float32).reshape(k, 128).T)


def kernel(x, c, positions, w_ada, b_ada, norm_mix, w_in, q_norm, w_uq, kv_norm, w_ukv,
           rel_bias, sink, w_o_mla, w_o_swa, w_out, norm_mlp, w_ff1, w_ff2, norm_final):
    global _PROGRAM
    if _PROGRAM is None:
        _PROGRAM = build_program()
    nc = _PROGRAM
    f = lambda a: np.ascontiguousarray(np.asarray(a, dtype=np.float32))
    consts = _host_consts()
    shared = {
        "w_ada": f(w_ada[0]), "b_ada_l": _lay(b_ada[0], 48), "norm_mix_l": _lay(norm_mix[0], 8),
        "w_in": f(w_in[0]), "q_norm_l": _lay(q_norm[0], 3), "w_uq": f(w_uq[0]),
        "kv_norm_l": _lay(kv_norm[0], 2), "w_ukv": f(w_ukv[0]), "rel_bias": f(rel_bias),
        "sink": f(sink[0]), "w_o_mla": f(w_o_mla[0]), "w_o_swa": f(w_o_swa[0]), "w_out": f(w_out[0]),
        "norm_mlp_l": _lay(norm_mlp[0], 8), "w_ff1": f(w_ff1[0]), "w_ff2": f(w_ff2[0]),
        "norm_final_l": _lay(norm_final, 8),
    }
    shared.update(consts)
    x = np.asarray(x, dtype=np.float32)
    c = np.asarray(c, dtype=np.float32)
    positions = np.asarray(positions, dtype=np.int32)
    in_maps = []
    for b in range(N_CORES):
        m = dict(shared)
        m["x"] = np.ascontiguousarray(x[b])
        m["c_l"] = _lay(c[b], 8)
        m["pos_l"] = np.ascontiguousarray(positions[b].reshape(NT, 128).T)
        in_maps.append(m)
    res = run_bass_kernel_spmd(nc, in_maps, core_ids=list(range(N_CORES)))
    kernel.last_results = res
    return np.stack([np.asarray(r["out"], dtype=np.float32) for r in res.results], axis=0)
```

```python
import math
import contextlib
import numpy as np
import concourse.bass as bass
import concourse.mybir as mybir
from concourse.bass_utils import run_bass_kernel_spmd

F32 = mybir.dt.float32
BF16 = mybir.dt.bfloat16
I32 = mybir.dt.int32
AF = mybir.ActivationFunctionType
ALU = mybir.AluOpType

D = 1024
SEQ = 4096
NT = SEQ // 128
NB = SEQ // 512
D_IN = 4288
EPS = 1e-6
MLA_SCALE = 192 ** -0.5
SWA_SCALE = 128 ** -0.5
NEG = -30000.0
TWO_PI = 2.0 * math.pi
C1 = 6.28125
C2 = TWO_PI - C1
PI_SAFE = 3.1415925
N_CORES = 8

DEBUG = {}


class Buf:
    __slots__ = ("name", "w", "r", "psum")

    def __init__(self, name="", psum=False):
        self.name = name
        self.w = None
        self.r = {}
        self.psum = psum


class Sched:
    ENGS = ("pe", "act", "dve", "pool", "sp")

    def __init__(self, nc, n_dma_sems=48):
        self.nc = nc
        self.ops = {e: [] for e in self.ENGS}
        self.cnt = {e: 0 for e in self.ENGS}
        self.known = {e: {} for e in self.ENGS}
        self.n_dma = n_dma_sems
        self.dma_cnt = [0] * n_dma_sems
        half = n_dma_sems // 2
        self.dma_pools = {"sp": list(range(0, half)), "pool": list(range(half, n_dma_sems))}
        self.dma_rr = {"sp": 0, "pool": 0}
        self.sems = {}

    def _need(self, X, waits, key, val):
        if self.known[X].get(key, 0) >= val:
            return
        if waits.get(key, 0) < val:
            waits[key] = val

    @staticmethod
    def _flat(bs):
        out = []
        for b in bs:
            if isinstance(b, (list, tuple)):
                out.extend(Sched._flat(b))
            else:
                out.append(b)
        return out

    def op(self, eng, fn, reads=(), writes=(), inc=True, dma=False):
        X = eng
        reads = self._flat(reads)
        writes = self._flat(writes)
        waits = {}
        for b in reads:
            if b.psum:
                for k, (v, e) in b.r.items():
                    if e != X:
                        self._need(X, waits, k, v)
            if b.w is not None:
                k, v, e = b.w
                if e == X and k == X and X == "pe":
                    continue
                self._need(X, waits, k, v)
        for b in writes:
            if b.w is not None:
                k, v, e = b.w
                if not (e == X and k == X):
                    self._need(X, waits, k, v)
            for k, (v, e) in b.r.items():
                if e == X and k == X:
                    continue
                self._need(X, waits, k, v)
        if dma:
            pl = "pool" if X == "pool" else "sp"
            lst = self.dma_pools[pl]
            i = lst[self.dma_rr[pl]]
            self.dma_rr[pl] = (self.dma_rr[pl] + 1) % len(lst)
            key = "d%d" % i
            if self.dma_cnt[i] > 0:
                self._need(X, waits, key, 16 * self.dma_cnt[i])
            self.dma_cnt[i] += 1
            tok = (key, 16 * self.dma_cnt[i], X)
            incspec = (key, 16)
        else:
            if inc:
                self.cnt[X] += 1
                tok = (X, self.cnt[X], X)
                incspec = (X, 1)
            else:
                tok = (X, self.cnt[X] + 1, X)
                incspec = None
        for k, v in waits.items():
            self.known[X][k] = v
        self.ops[X].append((tuple(waits.items()), fn, incspec))
        k, v, e = tok
        for b in reads:
            old = b.r.get(k)
            if old is None or old[0] < v:
                b.r[k] = (v, e)
        for b in writes:
            b.w = tok
            b.r = {}
        return tok

    def alias(self, news, olds):
        acc = {}
        for o in olds:
            for k, (v, e) in o.r.items():
                if k not in acc or acc[k][0] < v:
                    acc[k] = (v, "?")
            if o.w is not None:
                k, v, e = o.w
                if k not in acc or acc[k][0] < v:
                    acc[k] = (v, "?")
        for n in news:
            for k, (v, e) in acc.items():
                old = n.r.get(k)
                if old is None or old[0] < v:
                    n.r[k] = (v, e)

    def barrier(self):
        for X in self.ENGS:
            waits = {}
            for i in range(self.n_dma):
                if self.dma_cnt[i] > 0:
                    self._need(X, waits, "d%d" % i, 16 * self.dma_cnt[i])
            for e in self.ENGS:
                if e != X and self.cnt[e] > 0:
                    self._need(X, waits, e, self.cnt[e])
            for k, v in waits.items():
                self.known[X][k] = v
            self.ops[X].append((tuple(waits.items()), None, None))

    def emit(self):
        nc = self.nc
        with contextlib.ExitStack() as st:
            keys = list(self.ENGS) + ["d%d" % i for i in range(self.n_dma)]
            for k in keys:
                self.sems[k] = st.enter_context(nc.semaphore("s_" + k))
            block = st.enter_context(nc.Block())
            sems = self.sems

            def run(e, engobj):
                for waits, fn, incspec in self.ops[e]:
                    for k, v in waits:
                        engobj.wait_ge(sems[k], v)
                    if fn is None:
                        continue
                    ins = fn(engobj)
                    if incspec is not None:
                        ins.then_inc(sems[incspec[0]], incspec[1])

            @block.tensor
            def _(t):
                run("pe", t)

            @block.scalar
            def _(s):
                run("act", s)

            @block.vector
            def _(v):
                run("dve", v)

            @block.gpsimd
            def _(g):
                run("pool", g)

            @block.sync
            def _(s):
                run("sp", s)


def _t5_bucket_np(rel):
    rel = np.asarray(rel, dtype=np.int32)
    half, max_exact = 16, 8
    ret = np.where(rel > 0, half, 0)
    n = np.abs(rel)
    nf = np.maximum(n, 1).astype(np.float32)
    large = max_exact + (np.log(nf / np.float32(max_exact)) / np.float32(math.log(128 / max_exact))
                         * np.float32(half - max_exact)).astype(np.int32)
    large = np.minimum(large, half - 1)
    return ret + np.where(n < max_exact, n, large)


_BUCKET_FIX = {16: 10, 32: 12, 64: 14, 128: 15}


def _host_consts():
    ident = np.eye(128, dtype=np.float32)
    J = np.ascontiguousarray(ident[::-1])
    inv = (10000.0 ** (-np.arange(0, 64, 2, dtype=np.float32) / np.float32(64))).astype(np.float32)
    invt = np.ascontiguousarray(np.broadcast_to(inv[None, :], (128, 32))).astype(np.float32)
    oht = np.zeros((33, 512), dtype=np.float32)
    for m in range(512):
        rel = m - 255
        if abs(rel) <= 128:
            n = abs(rel)
            bk = int(_t5_bucket_np(rel))
            if n in _BUCKET_FIX:
                bk = _BUCKET_FIX[n] + (16 if rel > 0 else 0)
            oht[bk, m] = 1.0
        else:
            oht[32, m] = 1.0
    return {"ident": ident, "jrev": J, "invt": invt, "oht": oht}


ARENA_BYTES = 211968
EXTRA = 207872


def build_program():
    nc = bass.Bass("TRN2", target_bir_lowering=False)
    S = Sched(nc, n_dma_sems=48)

    def dram(name, shape, dt, kind="ExternalInput"):
        return nc.dram_tensor(name, shape, dt, kind=kind)

    x_d = dram("x", [SEQ, D], F32).ap()
    c_d = dram("c_l", [128, 8], F32).ap()
    pos_d = dram("pos_l", [128, NT], I32).ap()
    wada_d = dram("w_ada", [D, 6 * D], F32).ap()
    bada_d = dram("b_ada_l", [128, 48], F32).ap()
    nmix_d = dram("norm_mix_l", [128, 8], F32).ap()
    win_d = dram("w_in", [D, D_IN], F32).ap()
    qn_d = dram("q_norm_l", [128, 3], F32).ap()
    wuq_d = dram("w_uq", [384, 1536], F32).ap()
    kvn_d = dram("kv_norm_l", [128, 2], F32).ap()
    wukv_d = dram("w_ukv", [256, 2048], F32).ap()
    rb_d = dram("rel_bias", [32, 8], F32).ap()
    sink_d = dram("sink", [8], F32).ap()
    womla_d = dram("w_o_mla", [D, D], F32).ap()
    woswa_d = dram("w_o_swa", [D, D], F32).ap()
    wout_d = dram("w_out", [D, D], F32).ap()
    nmlp_d = dram("norm_mlp_l", [128, 8], F32).ap()
    wff1_d = dram("w_ff1", [D, 4 * D], F32).ap()
    wff2_d = dram("w_ff2", [4 * D, D], F32).ap()
    nfin_d = dram("norm_final_l", [128, 8], F32).ap()
    ident_d = dram("ident", [128, 128], F32).ap()
    jrev_d = dram("jrev", [128, 128], F32).ap()
    invt_d = dram("invt", [128, 32], F32).ap()
    oht_d = dram("oht", [33, 512], F32).ap()
    tbl_t = dram("tbl_scratch", [8, 512], F32, kind="Internal")
    NPIECE = 74
    wsc = dram("wsc", [NPIECE, 128, 2048], BF16, kind="Internal").ap()
    out_d = dram("out", [SEQ, D], F32, kind="ExternalOutput").ap()
    dbg_d = {}
    for name, shape in DEBUG.items():
        dbg_d[name] = dram("dbg_" + name, list(shape), F32, kind="ExternalOutput").ap()

    st = contextlib.ExitStack()
    with st:
        arena = st.enter_context(nc.sbuf_tensor("arena", [128, ARENA_BYTES // 2], BF16))
        banks = [st.enter_context(nc.psum_tensor("bank%d" % i, [128, 512], F32))[:, :] for i in range(8)]
        bank_b = [Buf("bank%d" % i, psum=True) for i in range(8)]

        def V(off, dt, shape, p0=0, p1=128):
            n = int(np.prod(shape))
            esz = 2 if dt == BF16 else 4
            assert off % 4 == 0 and off + n * esz <= ARENA_BYTES, (off, n, esz)
            v = arena[p0:p1, off // 2:(off + n * esz) // 2]
            if dt != BF16:
                v = v.bitcast(dt)
            if len(shape) == 2:
                v = v.rearrange("p (a b) -> p a b", a=shape[0])
            elif len(shape) == 3:
                v = v.rearrange("p (a b c) -> p a b c", a=shape[0], b=shape[1])
            elif len(shape) == 4:
                v = v.rearrange("p (a b c d) -> p a b c d", a=shape[0], b=shape[1], c=shape[2])
            return v

        def mm(out, lhsT, rhs, start, stop, R, W, inc):
            S.op("pe", lambda e: e.matmul(out, lhsT=lhsT, rhs=rhs, start=start, stop=stop),
                 reads=R, writes=W, inc=inc)

        def tp(out, in_, R, W, inc):
            S.op("pe", lambda e: e.transpose(out=out, in_=in_, identity=ident_f), reads=R + [b_const],
                 writes=W, inc=inc)

        def act(out, in_, func, R, W, bias=None, scale=None, accum=None):
            kw = {}
            if bias is not None:
                kw["bias"] = bias
            if scale is not None:
                kw["scale"] = scale
            if accum is not None:
                kw["accum_out"] = accum
            S.op("act", lambda e: e.activation(out=out, in_=in_, func=func, **kw), reads=R, writes=W)

        def ts(eng, out, in0, s1, s2, op0, op1, R, W):
            if op1 is None:
                S.op(eng, lambda e: e.tensor_scalar(out=out, in0=in0, scalar1=s1, scalar2=None, op0=op0),
                     reads=R, writes=W)
            else:
                S.op(eng, lambda e: e.tensor_scalar(out=out, in0=in0, scalar1=s1, scalar2=s2, op0=op0, op1=op1),
                     reads=R, writes=W)

        def tt(eng, out, in0, in1, op, R, W):
            S.op(eng, lambda e: e.tensor_tensor(out=out, in0=in0, in1=in1, op=op), reads=R, writes=W)

        def stt(eng, out, in0, scalar, in1, op0, op1, R, W):
            S.op(eng, lambda e: e.scalar_tensor_tensor(out=out, in0=in0, scalar=scalar, in1=in1, op0=op0, op1=op1),
                 reads=R, writes=W)

        def cp(eng, out, in_, R, W):
            if eng == "act":
                S.op("act", lambda e: e.copy(out=out, in_=in_), reads=R, writes=W)
            else:
                S.op(eng, lambda e: e.tensor_copy(out=out, in_=in_), reads=R, writes=W)

        def recip(out, in_, R, W):
            S.op("dve", lambda e: e.reciprocal(out=out, in_=in_), reads=R, writes=W)

        def dma(eng, out, in_, R, W):
            S.op(eng, lambda e: e.dma_start(out=out, in_=in_), reads=R, writes=W, dma=True)

        def memset(eng, ap, val, W):
            S.op(eng, lambda e: e.memset(ap, val), writes=W)

        bank_rr = [0]

        reserved = set()

        def nextbank():
            while True:
                i = bank_rr[0]
                bank_rr[0] = (i + 1) % 8
                if i not in reserved:
                    return banks[i], bank_b[i]

        def dump(name, ap, R):
            if name in dbg_d:
                dma("sp", dbg_d[name], ap, R, [])

        ident_f = V(0, F32, [128])
        ones_f = V(512, F32, [128])
        ones_bf = V(1024, BF16, [128])
        jrev_f = V(1280, F32, [128])
        b_const = Buf("const")
        SV = 2048

        def sv(i, n):
            return V(SV + 4 * i, F32, [n])

        cT = sv(0, 8); cexp = sv(8, 8); cact2 = V(SV + 64, F32, [8, 2])
        modT = sv(32, 48); badaT = sv(80, 48)
        nmix = sv(128, 8); nmlp = sv(136, 8); nfin = sv(144, 8)
        a1 = sv(152, 8); a2 = sv(160, 8)
        qn = sv(168, 3); kvn = sv(172, 2)
        sinkexp = sv(176, 8)
        pos_f = sv(184, 32)
        pos_i = V(SV + 4 * 216, I32, [32])
        stat = sv(248, 16)
        invt = sv(264, 32)
        cos_t = V(4096, F32, [32, 32])
        sin_t = V(8192, F32, [32, 32])
        rb_aug = V(3328, F32, [8], 0, 33)
        tbl_sb = V(12288, F32, [512], 0, 8)
        oht = V(14336, F32, [512], 0, 33)
        b_small = Buf("small")
        b_trig = Buf("trig")
        b_mod = Buf("mod")

        OT_OFF = 16384
        OT = V(OT_OFF, BF16, [8, SEQ])
        b_OT = [[Buf("OT%d_%d" % (h, q)) for q in range(NB)] for h in range(8)]
        P1O = 81920
        cqnT = V(P1O, BF16, [3, SEQ])
        ckvnT = V(P1O + 24576, BF16, [2, SEQ])
        KrT = V(P1O + 40960, BF16, [SEQ])
        b_cqn = [Buf("cqn%d" % t) for t in range(NT)]
        b_ckvn = [Buf("ckvn%d" % t) for t in range(NT)]
        b_kr = [Buf("kr%d" % t) for t in range(NT)]
        TR = 131072

        dma("sp", ident_f, ident_d, [], [b_const])
        dma("sp", jrev_f, jrev_d, [], [b_const])
        dma("sp", invt, invt_d, [], [b_small])
        dma("sp", oht, oht_d, [], [b_small])
        dma("sp", cT, c_d, [], [b_small])
        dma("sp", badaT, bada_d, [], [b_small])
        dma("sp", nmix, nmix_d, [], [b_small])
        dma("sp", nmlp, nmlp_d, [], [b_small])
        dma("sp", nfin, nfin_d, [], [b_small])
        dma("sp", qn, qn_d, [], [b_small])
        dma("sp", kvn, kvn_d, [], [b_small])
        dma("sp", pos_i, pos_d, [], [b_small])
        dma("sp", sinkexp, sink_d.partition_broadcast(128), [], [b_small])
        dma("sp", rb_aug[0:32, :], rb_d, [], [b_small])
        memset("dve", rb_aug[32:33, :], NEG, [b_small])
        memset("dve", ones_f, 1.0, [b_const])
        memset("dve", ones_bf, 1.0, [b_const])

        act(cexp, cT, AF.Exp, [b_small], [b_small], scale=-1.0)
        ts("dve", cexp, cexp, 1.0, None, ALU.add, None, [b_small], [b_small])
        recip(cexp, cexp, [b_small], [b_small])
        tt("dve", cact2[:, :, 0], cT, cexp, ALU.mult, [b_small], [b_small])
        tt("dve", cact2[:, :, 1], cT, cexp, ALU.mult, [b_small], [b_small])
        act(sinkexp, sinkexp, AF.Exp, [b_small], [b_small])

        tg = [V(TR + 32768 + 4096 * i, F32, [32, 32]) for i in range(4)]
        tgi = V(TR + 32768 + 4096 * 4, I32, [32, 32])
        b_tg = Buf("tg")
        cp("dve", pos_f, pos_i, [b_small], [b_small])
        ang, nf, rr, mk = tg
        tt("dve", ang, pos_f.unsqueeze(2).broadcast_to([128, 32, 32]),
           invt.unsqueeze(1).broadcast_to([128, 32, 32]), ALU.mult, [b_small], [b_tg])
        ts("dve", tgi, ang, 1.0 / TWO_PI, None, ALU.mult, None, [b_tg], [b_tg])
        cp("dve", nf, tgi, [b_tg], [b_tg])
        stt("dve", rr, nf, -C1, ang, ALU.mult, ALU.add, [b_tg], [b_tg])
        stt("dve", rr, nf, -C2, rr, ALU.mult, ALU.add, [b_tg], [b_tg])

        def wrap(r):
            ts("dve", mk, r, math.pi, TWO_PI, ALU.is_gt, ALU.mult, [b_tg], [b_tg])
            tt("dve", r, r, mk, ALU.subtract, [b_tg], [b_tg])
            ts("dve", mk, r, -math.pi, TWO_PI, ALU.is_lt, ALU.mult, [b_tg], [b_tg])
            tt("dve", r, r, mk, ALU.add, [b_tg], [b_tg])
            ts("dve", r, r, -PI_SAFE, PI_SAFE, ALU.max, ALU.min, [b_tg], [b_tg])

        wrap(rr)
        act(sin_t, rr, AF.Sin, [b_tg], [b_trig])
        ts("dve", rr, rr, math.pi / 2, None, ALU.add, None, [b_tg], [b_tg])
        wrap(rr)
        act(cos_t, rr, AF.Sin, [b_tg], [b_trig])

        stg = [V(TR + 16384 * i, F32, [8, 512]) for i in range(2)]
        b_stg = [Buf("stg0"), Buf("stg1")]
        psM, b_psM = nextbank()
        psMv = psM[:, 0:96].rearrange("p (a b) -> p a b", b=2)
        for pc in range(12):
            sl = pc % 2
            dma("sp", stg[sl], wada_d[:, pc * 512:(pc + 1) * 512].rearrange("(k p) n -> p k n", p=128),
                [], [b_stg[sl]])
            for j in range(4):
                cc = pc * 4 + j
                for k in range(8):
                    mm(psMv[:, cc, :], stg[sl][:, k, j * 128:(j + 1) * 128], cact2[:, k, :], k == 0, k == 7,
                       [b_stg[sl], b_small], [b_psM], inc=(k == 7))
        tt("dve", modT, psMv[:, :, 0], badaT, ALU.add, [b_psM, b_small], [b_mod])
        stt("dve", a1, modT[:, 8:16], 1.0, nmix, ALU.add, ALU.mult, [b_mod, b_small], [b_mod])
        stt("dve", a2, modT[:, 32:40], 1.0, nmlp, ALU.add, ALU.mult, [b_mod, b_small], [b_mod])
        sh1 = modT[:, 0:8]; g1v = modT[:, 16:24]; sh2 = modT[:, 24:32]; g2v = modT[:, 40:48]
        S.barrier()

        stat_rr = [0]

        def rstd_of(ss_ap, n, R):
            act(ss_ap, ss_ap, AF.Ln, R, R, bias=EPS, scale=1.0 / n)
            act(ss_ap, ss_ap, AF.Exp, R, R, scale=-0.5)
            return ss_ap

        stat_bufs = [Buf("stat%d" % i) for i in range(16)]

        def new_stat():
            i = stat_rr[0]
            stat_rr[0] = (i + 1) % 16
            return stat[:, i:i + 1], stat_bufs[i]

        w704 = V(TR, BF16, [8, 704]); b_w704 = Buf("w704")
        xt = [V(TR + 11264 + 4096 * i, F32, [1024]) for i in range(2)]; b_xt = [Buf(), Buf()]
        xn = [V(TR + 19456 + 4096 * i, F32, [1024]) for i in range(2)]; b_xn = [Buf(), Buf()]
        hT = V(TR + 27648, BF16, [8, 512]); b_hT = [[Buf() for _ in range(8)] for _ in range(4)]
        junk = V(TR + 35840, BF16, [1024]); b_junk = Buf("junk")
        cqkv = [V(TR + 37888 + 2560 * i, F32, [640]) for i in range(2)]; b_cqkv = [Buf(), Buf()]
        krr = [V(TR + 43008 + 512 * i, F32, [128]) for i in range(2)]; b_krr = [Buf(), Buf()]
        rtmp = [V(TR + 44032 + 128 * i, F32, [32]) for i in range(4)]; b_rtmp = Buf("rtmp")

        dma("pool", w704, win_d[:, 0:704].rearrange("(k p) n -> p k n", p=128), [], [b_w704])

        pspec = {}
        plist = []

        def addp(key, src2d, kch, ncols):
            pspec[key] = (len(plist), kch, ncols)
            plist.append((key, src2d, kch, ncols))

        addp(("ks",), win_d[:, 1728:1984], 8, 256)
        addp(("vs",), win_d[:, 1984:2240], 8, 256)
        for pc in range(4):
            addp(("qs", pc), win_d[:, 704 + pc * 256:704 + (pc + 1) * 256], 8, 256)
        for c in range(8):
            addp(("g0", c), win_d[:, 2240 + c * 128:2240 + (c + 1) * 128], 8, 128)
            addp(("g1", c), win_d[:, 3264 + c * 128:3264 + (c + 1) * 128], 8, 128)
            addp(("oa", c), womla_d[:, c * 128:(c + 1) * 128], 8, 128)
            addp(("ob", c), woswa_d[:, c * 128:(c + 1) * 128], 8, 128)
        for cq in range(4):
            addp(("wo", cq), wout_d[:, cq * 256:(cq + 1) * 256], 8, 256)
        for pc in range(16):
            addp(("f1", pc), wff1_d[:, pc * 256:(pc + 1) * 256], 8, 256)
        for hf in range(2):
            for g in range(8):
                addp(("f2", hf, g), wff2_d[g * 512:(g + 1) * 512, hf * 512:(hf + 1) * 512], 4, 512)
        assert len(plist) == NPIECE
        b_wsc = [Buf("wsc%d" % i) for i in range(NPIECE)]
        for i, (key, src2d, kch, ncols) in enumerate(plist):
            dma("pool", wsc[i][:, 0:kch * ncols].rearrange("p (k n) -> p k n", k=kch),
                src2d.rearrange("(k p) n -> p k n", p=128), [], [b_wsc[i]])

        def make_nt(ring, b_ring, fixed_banks=None):
            rr = [0]

            def front(src, bsrc, load=None):
                i = rr[0]
                rr[0] = (i + 1) % len(ring)
                if load is not None:
                    dma("sp", ring[i], load, [], [b_ring[i]])
                    src, bsrc = ring[i], b_ring[i]
                ss, bss = new_stat()
                act(junk, src, AF.Square, [bsrc], [b_junk, bss], accum=ss)
                rstd_of(ss, D, [bss])
                ts("dve", ring[i], src, ss, None, ALU.mult, None, [bsrc, bss], [b_ring[i]])
                return i

            def back(i, avec, shvec, dst_fn, bdst):
                for hf in range(2):
                    if fixed_banks is None:
                        ps, bps = nextbank()
                    else:
                        ps, bps = banks[fixed_banks[hf]], bank_b[fixed_banks[hf]]
                    for j in range(4):
                        k = hf * 4 + j
                        tp(ps[:, j * 128:(j + 1) * 128], ring[i][:, k * 128:(k + 1) * 128], [b_ring[i]], [bps],
                           inc=(j == 3))
                    for j in range(4):
                        k = hf * 4 + j
                        bd = bdst[k] if isinstance(bdst, list) else bdst
                        if hf == 0:
                            act(dst_fn(k), ps[:, j * 128:(j + 1) * 128], AF.Identity, [bps, b_mod], [bd],
                                bias=shvec[:, k:k + 1], scale=avec[:, k:k + 1])
                        else:
                            ts("dve", dst_fn(k), ps[:, j * 128:(j + 1) * 128], avec[:, k:k + 1], shvec[:, k:k + 1],
                               ALU.mult, ALU.add, [bps, b_mod], [bd])

            return front, back

        p1_front, p1_back = make_nt([xt[0], xt[1], xn[0], xn[1]], [b_xt[0], b_xt[1], b_xn[0], b_xn[1]], fixed_banks=(0, 1))
        p1_slot = {}
        p1_ps = {}

        def p1_A(t):
            p1_slot[t] = p1_front(None, None, load=x_d[t * 128:(t + 1) * 128, :])

        def p1_B(t):
            tl = t % 4
            p1_back(p1_slot.pop(t), a1, sh1, lambda k, tl=tl: hT[:, k, tl * 128:(tl + 1) * 128], b_hT[tl])
            ia = 2 + 2 * (t % 2)
            psA, bA, psB, bB = banks[ia], bank_b[ia], banks[ia + 1], bank_b[ia + 1]
            for k in range(8):
                mm(psA[:, 0:384], hT[:, k, tl * 128:(tl + 1) * 128], w704[:, k, 0:384], k == 0, k == 7,
                   [b_hT[tl], b_w704], [bA], inc=(k == 7))
            for k in range(8):
                mm(psB[:, 0:320], hT[:, k, tl * 128:(tl + 1) * 128], w704[:, k, 384:704], k == 0, k == 7,
                   [b_hT[tl], b_w704], [bB], inc=(k == 7))
            p1_ps[t] = (psA, bA, psB, bB)

        def p1_C(t):
            sl = t % 2
            psA, bA, psB, bB = p1_ps.pop(t)
            ssq, bq = new_stat()
            act(junk[:, 0:384], psA[:, 0:384], AF.Square, [bA], [b_junk, bq], accum=ssq)
            rstd_of(ssq, 384, [bq])
            sskv, bkv = new_stat()
            act(junk[:, 0:256], psB[:, 0:256], AF.Square, [bB], [b_junk, bkv], accum=sskv)
            rstd_of(sskv, 256, [bkv])
            ts("dve", cqkv[sl][:, 0:384], psA[:, 0:384], ssq, None, ALU.mult, None, [bA, bq], [b_cqkv[sl]])
            ts("dve", cqkv[sl][:, 384:640], psB[:, 0:256], sskv, None, ALU.mult, None, [bB, bkv], [b_cqkv[sl]])
            x1_ = psB[:, 256:288]; x2_ = psB[:, 288:320]
            ct = cos_t[:, t, :]; sn = sin_t[:, t, :]
            tt("dve", rtmp[0], x1_, ct, ALU.mult, [bB, b_trig], [b_rtmp])
            tt("dve", rtmp[1], x2_, sn, ALU.mult, [bB, b_trig], [b_rtmp])
            tt("dve", rtmp[2], x2_, ct, ALU.mult, [bB, b_trig], [b_rtmp])
            tt("dve", rtmp[3], x1_, sn, ALU.mult, [bB, b_trig], [b_rtmp])
            tt("dve", krr[sl][:, 0:32], rtmp[0], rtmp[1], ALU.subtract, [b_rtmp], [b_krr[sl]])
            tt("dve", krr[sl][:, 32:64], rtmp[2], rtmp[3], ALU.add, [b_rtmp], [b_krr[sl]])
            cp("dve", krr[sl][:, 64:128], krr[sl][:, 0:64], [b_krr[sl]], [b_krr[sl]])
            psT, bT = banks[6], bank_b[6]
            for j in range(3):
                tp(psT[:, j * 128:(j + 1) * 128], cqkv[sl][:, j * 128:(j + 1) * 128], [b_cqkv[sl]], [bT], inc=(j == 2))
            psU, bU = banks[7], bank_b[7]
            for j in range(2):
                tp(psU[:, j * 128:(j + 1) * 128], cqkv[sl][:, 384 + j * 128:384 + (j + 1) * 128], [b_cqkv[sl]], [bU],
                   inc=False)
            tp(psU[:, 256:384], krr[sl], [b_krr[sl]], [bU], inc=True)
            tok = slice(t * 128, (t + 1) * 128)
            for j in range(3):
                ts("dve", cqnT[:, j, tok], psT[:, j * 128:(j + 1) * 128], qn[:, j:j + 1], None, ALU.mult, None,
                   [bT, b_small], [b_cqn[t]])
            for j in range(2):
                act(ckvnT[:, j, tok], psU[:, j * 128:(j + 1) * 128], AF.Identity, [bU, b_small], [b_ckvn[t]],
                    scale=kvn[:, j:j + 1])
            cp("act", KrT[:, tok], psU[:, 256:384], [bU], [b_kr[t]])

        p1_A(0)
        p1_A(1)
        for n in range(1, NT + 2):
            if 0 <= n - 1 < NT:
                p1_B(n - 1)
            if 0 <= n - 2 < NT:
                p1_C(n - 2)
            if n + 1 < NT:
                p1_A(n + 1)
        dump("cqnT", cqnT[:, :, 0:512], b_cqn[0:4])
        dump("ckvnT", ckvnT[:, :, 0:512], b_ckvn[0:4])
        dump("KrT", KrT[:, 0:512], b_kr[0:4])
        S.barrier()

        KT = [V(TR + 8192 * i, BF16, [SEQ]) for i in range(2)]
        Vt = [V(TR + 16384 + 8192 * i, BF16, [NT, 128]) for i in range(2)]
        QTn = [V(TR + 32768 + 8192 * i, BF16, [SEQ]) for i in range(2)]
        QTr = V(TR + 49152, BF16, [SEQ])
        ropeT = [V(TR + 57344 + 2048 * i, F32, [4, 2, 32]) for i in range(4)]
        b_KT = [[Buf() for _ in range(NB)] for _ in range(2)]
        b_V = [[Buf() for _ in range(NB)] for _ in range(2)]
        b_QTn = [[Buf() for _ in range(NB)] for _ in range(2)]
        b_QTr = [Buf() for _ in range(NB)]
        b_ropeT = Buf("ropeT")
        wqn = [V(TR + 65536 + 1792 * i, BF16, [3, 128]) for i in range(2)]
        wk = [V(TR + 65536 + 1792 * i + 768, BF16, [2, 128]) for i in range(2)]
        wv = [V(TR + 65536 + 1792 * i + 1280, BF16, [2, 128]) for i in range(2)]
        b_wh = [Buf(), Buf()]
        wqr = V(TR + 69120, BF16, [3, 2, 64]); b_wqr = Buf("wqr")
        qrot = V(TR + 69120 + 768, F32, [4, 2, 2, 32])
        PT = [V(TR + 70656 + 1024 * i, BF16, [512]) for i in range(4)] + [V(TR + 57344 + 7168, BF16, [512])]
        b_PT = [Buf() for _ in range(5)]
        recs = [V(TR + 74752, F32, [512])] * 2
        b_recs = [Buf("rec")] * 2
        sc_rr = [0]
        ropeT = [V(TR + 57344 + 1024 * i, F32, [4, 2, 32]) for i in range(3)]
        qrot = V(TR + 57344 + 3072, F32, [4, 2, 2, 32])
        b_qrot = Buf("qrot")
        QTrz = [V(TR + 57344 + 5120 + 1024 * i, BF16, [512]) for i in range(2)]
        b_QTrz = [Buf(), Buf()]
        pt_rr = [0]
        ev_rr = [0]
        dacc = [V(EXTRA + 2048 * i, F32, [512]) for i in range(2)]; b_dacc = [Buf(), Buf()]
        dacc_rr = [0]

        def evac(out, in_, R, W):
            ev_rr[0] += 1
            cp("dve", out, in_, R, W)

        def load_head_weights(h):
            sl = h % 2
            dma("pool", wqn[sl], wuq_d[:, h * 192:h * 192 + 128].rearrange("(k p) n -> p k n", p=128), [], [b_wh[sl]])
            dma("pool", wk[sl], wukv_d[:, h * 256:h * 256 + 128].rearrange("(k p) n -> p k n", p=128), [], [b_wh[sl]])
            dma("pool", wv[sl], wukv_d[:, h * 256 + 128:h * 256 + 256].rearrange("(k p) n -> p k n", p=128), [],
                [b_wh[sl]])

        def prod_steps(h, bank):
            sl = h % 2
            steps = []

            def getbank():
                if bank is None:
                    return nextbank()
                return banks[bank], bank_b[bank]

            def mk_q(g):
                def f():
                    cols = slice(g * 512, (g + 1) * 512)
                    ps, bps = getbank()
                    for kc in range(3):
                        mm(ps, wqn[sl][:, kc, :], cqnT[:, kc, cols], kc == 0, kc == 2,
                           [b_wh[sl]] + b_cqn[4 * g:4 * g + 4], [bps], inc=(kc == 2))
                    evac(QTn[sl][:, cols], ps, [bps], [b_QTn[sl][g]])
                return f

            def mk_k(g):
                def f():
                    cols = slice(g * 512, (g + 1) * 512)
                    ps, bps = getbank()
                    for kc in range(2):
                        mm(ps, wk[sl][:, kc, :], ckvnT[:, kc, cols], kc == 0, kc == 1,
                           [b_wh[sl]] + b_ckvn[4 * g:4 * g + 4], [bps], inc=(kc == 1))
                    evac(KT[sl][:, cols], ps, [bps], [b_KT[sl][g]])
                return f

            def mk_v(g):
                def f():
                    ps, bps = getbank()
                    for tl in range(4):
                        t = 4 * g + tl
                        for kc in range(2):
                            mm(ps[:, tl * 128:(tl + 1) * 128], ckvnT[:, kc, t * 128:(t + 1) * 128], wv[sl][:, kc, :],
                               kc == 0, kc == 1, [b_wh[sl], b_ckvn[t]], [bps], inc=(kc == 1 and tl == 3))
                    evac(Vt[sl][:, 4 * g:4 * g + 4, :].rearrange("p t d -> p (t d)"), ps, [bps], [b_V[sl][g]])
                return f

            for g in range(NB):
                steps += [mk_q(g), mk_k(g), mk_v(g)]
            return steps

        for hp in range(4):
            for e in range(2):
                h = 2 * hp + e
                dma("pool", wqr[:, :, e, :],
                    wuq_d[:, h * 192 + 128:h * 192 + 192].rearrange("(k p) n -> p k n", p=128), [], [b_wqr])
            for g in range(NB):
                ps, bps = nextbank()
                for tl in range(4):
                    t = 4 * g + tl
                    for kc in range(3):
                        mm(ps[:, tl * 128:(tl + 1) * 128], cqnT[:, kc, t * 128:(t + 1) * 128],
                           wqr[:, kc, :, :], kc == 0, kc == 2, [b_cqn[t], b_wqr], [bps], inc=(kc == 2 and tl == 3))
                psv = ps.rearrange("p (t h f i) -> p t h f i", t=4, h=2, f=2)
                cb = cos_t[:, 4 * g:4 * g + 4, :].unsqueeze(2).broadcast_to([128, 4, 2, 32])
                sb_ = sin_t[:, 4 * g:4 * g + 4, :].unsqueeze(2).broadcast_to([128, 4, 2, 32])
                x1 = psv[:, :, :, 0, :]; x2 = psv[:, :, :, 1, :]
                tt("dve", ropeT[0], x1, cb, ALU.mult, [bps, b_trig], [b_ropeT])
                tt("dve", ropeT[1], x2, sb_, ALU.mult, [bps, b_trig], [b_ropeT])
                tt("dve", qrot[:, :, :, 0, :], ropeT[0], ropeT[1], ALU.subtract, [b_ropeT], [b_qrot])
                tt("dve", ropeT[0], x2, cb, ALU.mult, [bps, b_trig], [b_ropeT])
                tt("dve", ropeT[1], x1, sb_, ALU.mult, [bps, b_trig], [b_ropeT])
                tt("dve", qrot[:, :, :, 1, :], ropeT[0], ropeT[1], ALU.add, [b_ropeT], [b_qrot])
                ps2, bps2 = nextbank()
                qflat = qrot.rearrange("p t h f i -> p (t h f i)")
                for tl in range(4):
                    tp(ps2[:, tl * 128:(tl + 1) * 128], qflat[:, tl * 128:(tl + 1) * 128], [b_qrot], [bps2], inc=(tl == 3))
                evac(QTr[:, g * 512:(g + 1) * 512], ps2, [bps2], [b_QTr[g]])
            for e in range(2):
                h = 2 * hp + e
                sl = e
                if h == 0:
                    load_head_weights(0)
                    for st_ in prod_steps(0, None):
                        st_()
                nxt_prod = []
                if h + 1 < 8:
                    load_head_weights(h + 1)
                    nxt_prod = prod_steps(h + 1, 3)
                for zi in range(2):
                    memset("pool", QTrz[zi][(1 - e) * 64:(2 - e) * 64, :], 0.0, [b_QTrz[zi]])
                LA = 3
                items = [(qb, kc) for qb in range(NB) for kc in range(NT)]
                accs = {}
                pend = {}
                dst = {}
                deferred = []

                def acc_of(qb):
                    if qb not in accs:
                        a = (qb % 2) * 2
                        accs[qb] = (banks[a], bank_b[a], banks[1], bank_b[1])
                    return accs[qb]

                def scores(qb, kc, sl=sl, e=e):
                    qc = slice(qb * 512, (qb + 1) * 512)
                    zi = qb % 2
                    if kc == 0:
                        cp("pool", QTrz[zi][e * 64:(e + 1) * 64, :], QTr[e * 64:(e + 1) * 64, qc], [b_QTr[qb]],
                           [b_QTrz[zi]])
                    si = 4 + sc_rr[0]
                    sc_rr[0] = (sc_rr[0] + 1) % 4
                    ps, bps = banks[si], bank_b[si]
                    kcs = slice(kc * 128, (kc + 1) * 128)
                    mm(ps, KT[sl][:, kcs], QTn[sl][:, qc], True, False,
                       [b_KT[sl][kc // 4], b_QTn[sl][qb]], [bps], inc=False)
                    mm(ps, KrT[:, kcs], QTrz[zi], False, True, [b_kr[kc], b_QTrz[zi]], [bps], inc=True)
                    i = pt_rr[0]
                    pt_rr[0] = (i + 1) % len(PT)
                    act(PT[i], ps, AF.Exp, [bps], [b_PT[i]], scale=MLA_SCALE)
                    pend[(qb, kc)] = i

                def pv(qb, kc, sl=sl):
                    accO, bO, accD, bD = acc_of(qb)
                    i = pend.pop((qb, kc))
                    on_pe = False
                    mm(accO, Vt[sl][:, kc, :], PT[i], kc == 0, kc == NT - 1, [b_V[sl][kc // 4], b_PT[i]], [bO],
                       inc=not on_pe)
                    if on_pe:
                        mm(accD, ones_bf, PT[i], kc == 7, False, [b_const, b_PT[i]], [bD], inc=True)
                    else:
                        d = dst.setdefault(qb, {"n": 0, "used": [False, False]})
                        j = d["n"] % 2
                        d["n"] += 1
                        if not d["used"][j]:
                            d["used"][j] = True
                            cp("dve", dacc[j], PT[i], [b_PT[i]], [b_dacc[j]])
                        else:
                            tt("dve", dacc[j], dacc[j], PT[i], ALU.add, [b_dacc[j], b_PT[i]], [b_dacc[j]])

                def epi_pe(qb):
                    accO, bO, accD, bD = acc_of(qb)
                    ri = qb % 2
                    mm(accD, ones_f, recs[ri], True, True, [b_const, b_recs[ri]], [bD], inc=True)

                def epi_dve(qb, h=h):
                    accO, bO, accD, bD = acc_of(qb)
                    ri = qb % 2
                    qc = slice(qb * 512, (qb + 1) * 512)
                    act(recs[ri], accD, AF.Ln, [bD], [b_recs[ri]])
                    act(recs[ri], recs[ri], AF.Exp, [b_recs[ri]], [b_recs[ri]], scale=-1.0)
                    tt("dve", OT[:, h, qc], accO, recs[ri], ALU.mult, [bO, b_recs[ri]], [b_OT[h][qb]])

                for n in range(min(LA, len(items))):
                    scores(*items[n])
                for n, (qb, kc) in enumerate(items):
                    pv(qb, kc)
                    if n + LA < len(items):
                        scores(*items[n + LA])
                    if kc == NT - 1:
                        ri = qb % 2
                        tt("dve", recs[ri], dacc[0], dacc[1], ALU.add, [b_dacc[0], b_dacc[1]], [b_recs[ri]])
                        deferred.append(qb)
                    if kc == 3 and deferred:
                        epi_pe(deferred[0])
                    if kc == 5 and deferred:
                        epi_dve(deferred.pop(0))
                    if n % 10 == 8 and nxt_prod:
                        nxt_prod.pop(0)()
                for qb in deferred:
                    epi_pe(qb)
                    epi_dve(qb)
                while nxt_prod:
                    nxt_prod.pop(0)()
        dump("OT", OT[:, :, 0:512].bitcast(BF16) if False else OT[:, :, 0:512], [b_OT[h][0] for h in range(8)])
        S.barrier()

        P3 = P1O
        G1b = V(P3, F32, [1024]); G2b = V(P3 + 4096, F32, [1024]); gFb = V(P3 + 8192, F32, [1024])
        Bm = V(P3 + 12288, F32, [3, 8, 128])
        b_G = Buf("G"); b_Bm = Buf("Bm")
        hTe = V(P3 + 24576, BF16, [8, 768]); b_hTe = [[Buf() for _ in range(8)] for _ in range(6)]
        U = P3 + 36864
        ksT = V(U, BF16, [2, 768]); b_ks = Buf("ks")
        vs = V(U + 3072, BF16, [6, 256]); b_vs = [Buf() for _ in range(6)]
        qsT = V(U + 6144, BF16, [8, 512]); b_qs = [Buf() for _ in range(8)]
        swaOT = V(U + 14336, BF16, [8, 512]); b_swaO = [[Buf() for _ in range(4)] for _ in range(2)]
        mergedT = V(U + 22528, BF16, [8, 512]); b_mg = [Buf() for _ in range(8)]
        uT = V(U, BF16, [32, 512]); b_uT = [Buf() for _ in range(32)]
        u_old = [b_ks] + b_vs + b_qs + b_swaO[0] + b_swaO[1] + b_mg
        XB = U + 32768
        xb = [V(XB + 4096 * i, F32, [1024]) for i in range(2)]; b_xb = [Buf(), Buf()]
        x1 = V(XB + 8192, F32, [4, 1024]); b_x1 = [Buf() for _ in range(4)]
        PT3 = [V(XB + 24576 + 1024 * i, BF16, [4, 128]) for i in range(3)]; b_PT3 = [Buf() for _ in range(3)]
        WR = XB + 27648
        b_wrh = [Buf() for _ in range(8)]
        b_wr = [[b_wrh[0], b_wrh[1]]]
        xn3 = V(WR + 16384, F32, [1024]); b_xn3 = Buf("xn3")
        sfp = [V(WR + 20480 + 2048 * i, F32, [512]) for i in range(3)]; b_sfp = [Buf() for _ in range(3)]
        junk3 = V(WR + 26624, BF16, [1024])
        dtmp = xn3[:, 0:512]; b_dtmp = b_xn3
        for i_ in range(4):
            PT3.append(V(EXTRA + 1024 * i_, BF16, [4, 128])); b_PT3.append(Buf())
        swb_rr = [0]; sfp_rr = [0]; pt3_rr = [0]
        assert WR + 26624 + 2048 <= ARENA_BYTES
        junk = junk3
        wr_rr = [0]

        def wpiece(*key):
            idx, kch, ncols = pspec[key]
            i = wr_rr[0]
            nh = 1 if kch * ncols <= 1024 else 2
            if nh == 2 and i % 2 == 1:
                i += 1
            i %= 8
            wr_rr[0] = (i + nh) % 8
            bw = b_wrh[i:i + nh]
            v = V(WR + 2048 * i, BF16, [kch, ncols])
            dma("pool", v, wsc[idx][:, 0:kch * ncols].rearrange("p (k n) -> p k n", k=kch), [b_wsc[idx]], bw)
            return v, bw

        dg = [V(WR + 20480 + 2048 * i, F32, [128]) for i in range(2)]
        for gi, (gvec, Gb, bsrc) in enumerate(((g1v, G1b, b_mod), (g2v, G2b, b_mod), (nfin, gFb, b_small))):
            for hf in range(2):
                ps, bps = nextbank()
                for j in range(4):
                    c = hf * 4 + j
                    d = dg[c % 2]
                    bd = b_sfp[c % 2]
                    ts("dve", d, ident_f, gvec[:, c:c + 1], None, ALU.mult, None, [b_const, bsrc], [bd])
                    mm(ps[:, j * 128:(j + 1) * 128], ones_f, d, True, True, [b_const, bd], [bps], inc=True)
                cp("dve", Gb[:, hf * 512:(hf + 1) * 512], ps, [bps], [b_G])
        ps, bps = nextbank()
        mm(ps[0:8, :], rb_aug, oht, True, True, [b_small], [bps], inc=True)
        b_tbl = Buf("tbl")
        cp("dve", tbl_sb, ps[0:8, :], [bps], [b_tbl])
        b_tbld = Buf("tbld")
        dma("sp", tbl_t.ap(), tbl_sb, [b_tbl], [b_tbld])
        hank = V(WR, F32, [8, 128])
        for dl in range(3):
            src = bass.AP(tensor=tbl_t, offset=dl * 128, ap=[[1, 128], [512, 8], [1, 128]])
            dma("sp", hank, src, [b_tbld], [b_wr[0]])
            for hh in range(2):
                ps, bps = nextbank()
                for j in range(4):
                    h = hh * 4 + j
                    mm(ps[:, j * 128:(j + 1) * 128], hank[:, h, :], jrev_f, True, True, [b_wr[0], b_const], [bps],
                       inc=(j == 3))
                cp("dve", Bm[:, dl, hh * 4:(hh + 1) * 4, :].rearrange("p h q -> p (h q)"), ps, [bps], [b_Bm])
        dump("Bm", Bm.rearrange("p a h q -> p (a h q)"), [b_Bm])
        dump("G1b", G1b, [b_G])

        nt_front, nt_back = make_nt([xb[0], xb[1], xn3], [b_xb[0], b_xb[1], b_xn3])

        def hext_steps(b):
            valid = [j for j in range(6) if 0 <= 4 * b - 1 + j < NT]
            slot = {}
            steps = []

            def mk_front(j):
                def f():
                    te = 4 * b - 1 + j
                    slot[j] = nt_front(None, None, load=x_d[te * 128:(te + 1) * 128, :])
                return f

            def mk_back(j):
                def f():
                    nt_back(slot[j], a1, sh1, lambda k, j=j: hTe[:, k, j * 128:(j + 1) * 128], b_hTe[j])
                return f

            for n, j in enumerate(valid):
                steps.append(mk_front(j))
                if n >= 1:
                    steps.append(mk_back(valid[n - 1]))
            steps.append(mk_back(valid[-1]))
            return steps

        def act_recip(buf, src, R, W):
            act(buf, src, AF.Ln, R, W)
            act(buf, buf, AF.Exp, W, W, scale=-1.0)

        for st_ in hext_steps(0):
            st_()
        for b in range(NB):
            S.alias(u_old, b_uT)
            valid = [j for j in range(6) if 0 <= 4 * b - 1 + j < NT]
            own = slice(128, 640)
            b_own = b_hTe[1:5]
            for tl in range(4):
                te = 4 * b + tl
                dma("sp", x1[:, tl, :], x_d[te * 128:(te + 1) * 128, :], [], [b_x1[tl]])
            wp, bwp = wpiece("ks")
            for kv in range(2):
                for (j0, j1) in ((0, 4), (4, 6)):
                    js = [j for j in valid if j0 <= j < j1]
                    if not js:
                        continue
                    cs = slice(js[0] * 128, (js[-1] + 1) * 128)
                    n = (js[-1] + 1 - js[0]) * 128
                    ps, bps = nextbank()
                    for k in range(8):
                        mm(ps[:, 0:n], wp[:, k, kv * 128:(kv + 1) * 128], hTe[:, k, cs], k == 0, k == 7,
                           [bwp] + [b_hTe[j] for j in js], [bps], inc=(k == 7))
                    cp("act", ksT[:, kv, cs], ps[:, 0:n], [bps], [b_ks])
            wp, bwp = wpiece("vs")
            for j in valid:
                ps, bps = nextbank()
                for k in range(8):
                    mm(ps[:, 0:256], hTe[:, k, j * 128:(j + 1) * 128], wp[:, k, :], k == 0, k == 7,
                       [bwp, b_hTe[j]], [bps], inc=(k == 7))
                cp("act", vs[:, j, :], ps[:, 0:256], [bps], [b_vs[j]])
            for pc in range(4):
                wp, bwp = wpiece("qs", pc)
                for hh in range(2):
                    h = pc * 2 + hh
                    ps, bps = nextbank()
                    for k in range(8):
                        mm(ps, wp[:, k, hh * 128:(hh + 1) * 128], hTe[:, k, own], k == 0, k == 7,
                           [bwp] + b_own, [bps], inc=(k == 7))
                    cp("act", qsT[:, h, :], ps, [bps], [b_qs[h]])
            units = [(qt, kv) for qt in range(4) for kv in range(2)]
            sw = {}

            def swa_scores(u):
                qt, kv = units[u]
                j = qt + 1
                hs = slice(kv * 4, (kv + 1) * 4)
                dls = [dl for dl in (-1, 0, 1) if (j + dl) in valid]
                pts = []
                for dl in dls:
                    jk = j + dl
                    bi = 4 + swb_rr[0]
                    swb_rr[0] = (swb_rr[0] + 1) % 4
                    ps, bps = banks[bi], bank_b[bi]
                    psv = ps.rearrange("p (h q) -> p h q", h=4)
                    mm(psv, ksT[:, kv, jk * 128:(jk + 1) * 128], qsT[:, hs, qt * 128:(qt + 1) * 128], True, True,
                       [b_ks] + b_qs[kv * 4:(kv + 1) * 4], [bps], inc=True)
                    si = sfp_rr[0]
                    sfp_rr[0] = (si + 1) % 3
                    pi = pt3_rr[0]
                    pt3_rr[0] = (pi + 1) % len(PT3)
                    sv_ = sfp[si].rearrange("p (h q) -> p h q", h=4)
                    stt("dve", sv_, psv, SWA_SCALE, Bm[:, dl + 1, hs, :], ALU.mult, ALU.add, [bps, b_Bm],
                        [b_sfp[si]])
                    act(PT3[pi], sv_, AF.Exp, [b_sfp[si]], [b_PT3[pi]])
                    pts.append((jk, pi))
                sw[u] = pts

            def swa_pv(u):
                qt, kv = units[u]
                hs = slice(kv * 4, (kv + 1) * 4)
                a = (u % 2) * 2
                accO, bO, accD, bD = banks[a], bank_b[a], banks[a + 1], bank_b[a + 1]
                pts = sw.pop(u)
                for n_i, (jk, pi) in enumerate(pts):
                    first = (n_i == 0)
                    last = (n_i == len(pts) - 1)
                    ptf = PT3[pi].rearrange("p h q -> p (h q)")
                    mm(accO, vs[:, jk, kv * 128:(kv + 1) * 128], ptf, first, last, [b_vs[jk], b_PT3[pi]], [bO],
                       inc=False)
                    mm(accD, ones_bf, ptf, first, last, [b_const, b_PT3[pi]], [bD], inc=True)
                dv = dtmp.rearrange("p (h q) -> p h q", h=4)
                tt("dve", dv, accD.rearrange("p (h q) -> p h q", h=4),
                   sinkexp[:, hs].unsqueeze(2).broadcast_to([128, 4, 128]), ALU.add, [bD, b_small], [b_dtmp])
                act_recip(dtmp, dtmp, [b_dtmp], [b_dtmp])
                tt("dve", swaOT[:, hs, qt * 128:(qt + 1) * 128], accO.rearrange("p (h q) -> p h q", h=4), dv,
                   ALU.mult, [bO, b_dtmp], [b_swaO[kv][qt]])

            reserved.update(range(4))
            swa_scores(0)
            for u in range(len(units)):
                if u + 1 < len(units):
                    swa_scores(u + 1)
                swa_pv(u)
            reserved.clear()
            if b == 0:
                dump("swaOT", swaOT, b_swaO[0] + b_swaO[1])
            for c in range(8):
                if True:
                    pg0, bpg0 = wpiece("g0", c)
                    pg1, bpg1 = wpiece("g1", c)
                    pa, bpa = wpiece("oa", c)
                    pb, bpb = wpiece("ob", c)
                    ccs = slice(0, 128)
                    ps_g0, bg0 = nextbank()
                    for k in range(8):
                        mm(ps_g0, pg0[:, k, ccs], hTe[:, k, own], k == 0, k == 7, [bpg0] + b_own, [bg0], inc=(k == 7))
                    ps_g1, bg1 = nextbank()
                    for k in range(8):
                        mm(ps_g1, pg1[:, k, ccs], hTe[:, k, own], k == 0, k == 7, [bpg1] + b_own, [bg1], inc=(k == 7))
                    ps_a, ba = nextbank()
                    for h in range(8):
                        mm(ps_a, pa[:, h, ccs], OT[:, h, b * 512:(b + 1) * 512], h == 0, h == 7, [bpa, b_OT[h][b]],
                           [ba], inc=(h == 7))
                    ps_b, bb = nextbank()
                    for h in range(8):
                        mm(ps_b, pb[:, h, ccs], swaOT[:, h, :], h == 0, h == 7,
                           [bpb] + [b_swaO[h // 4][q] for q in range(4)], [bb], inc=(h == 7))
                    act(sfp[0], ps_g0, AF.Sigmoid, [bg0], [b_sfp[0]])
                    act(sfp[1], ps_g1, AF.Sigmoid, [bg1], [b_sfp[1]])
                    tt("dve", sfp[0], sfp[0], ps_a, ALU.mult, [b_sfp[0], ba], [b_sfp[0]])
                    tt("dve", sfp[1], sfp[1], ps_b, ALU.mult, [b_sfp[1], bb], [b_sfp[1]])
                    tt("dve", mergedT[:, c, :], sfp[0], sfp[1], ALU.add, [b_sfp[0], b_sfp[1]], [b_mg[c]])
            if b == 0:
                dump("mergedT", mergedT, b_mg)
            slot2 = {}
            for tp2 in range(2):
                for cq in range(4):
                    po, bpo = wpiece("wo", cq)
                    cs = slice(cq * 256, (cq + 1) * 256)
                    ps, bps = nextbank()
                    for tl2 in range(2):
                        tl = tp2 * 2 + tl2
                        for k in range(8):
                            mm(ps[:, tl2 * 256:(tl2 + 1) * 256], mergedT[:, k, tl * 128:(tl + 1) * 128], po[:, k, :],
                               k == 0, k == 7, [bpo, b_mg[k]], [bps], inc=(k == 7 and tl2 == 1))
                    for tl2 in range(2):
                        tl = tp2 * 2 + tl2
                        si = tl2
                        tt("dve", sfp[si][:, 0:256], ps[:, tl2 * 256:(tl2 + 1) * 256], G1b[:, cs], ALU.mult,
                           [bps, b_G], [b_sfp[si]])
                        tt("dve", x1[:, tl, cs], sfp[si][:, 0:256], x1[:, tl, cs], ALU.add, [b_sfp[si], b_x1[tl]],
                           [b_x1[tl]])
                if tp2 == 1:
                    for tl in (0, 1):
                        nt_back(slot2[tl], a2, sh2, lambda k, tl=tl: hTe[:, k, tl * 128:(tl + 1) * 128], b_hTe[tl])
                for tl in (tp2 * 2, tp2 * 2 + 1):
                    slot2[tl] = nt_front(x1[:, tl, :], b_x1[tl])
            for tl in (2, 3):
                nt_back(slot2[tl], a2, sh2, lambda k, tl=tl: hTe[:, k, tl * 128:(tl + 1) * 128], b_hTe[tl])
            if b == 0:
                dump("x1", x1.rearrange("p t d -> p (t d)"), b_x1)
            S.alias(b_uT, u_old)
            h2 = slice(0, 512)
            b_h2 = b_hTe[0:4]
            for pc in range(16):
                p1, bp1 = wpiece("f1", pc)
                for cc in range(2):
                    ch = pc * 2 + cc
                    ps, bps = nextbank()
                    for k in range(8):
                        mm(ps, p1[:, k, cc * 128:(cc + 1) * 128], hTe[:, k, h2], k == 0, k == 7, [bp1] + b_h2, [bps],
                           inc=(k == 7))
                    si = ch % 3
                    act(sfp[si], ps, AF.Relu, [bps], [b_sfp[si]])
                    tt("dve", uT[:, ch, :], sfp[si], sfp[si], ALU.mult, [b_sfp[si]], [b_uT[ch]])
            nxt = hext_steps(b + 1) if b + 1 < NB else []
            for hf in range(2):
                cs = slice(hf * 512, (hf + 1) * 512)
                accs = [(banks[i], bank_b[i]) for i in range(4)]
                reserved.update(range(4))
                for g in range(8):
                    p2, bp2 = wpiece("f2", hf, g)
                    for kk in range(4):
                        kc = g * 4 + kk
                        for tl in range(4):
                            mm(accs[tl][0], uT[:, kc, tl * 128:(tl + 1) * 128], p2[:, kk, :], kc == 0, kc == 31,
                               [bp2, b_uT[kc]], [accs[tl][1]], inc=(kc == 31 or (kk == 3 and tl == 3)))
                    if nxt and not (hf == 1 and g == 7):
                        nxt.pop(0)()
                reserved.clear()
                for tl in range(4):
                    si = tl % 3
                    tt("dve", sfp[si], accs[tl][0], G2b[:, cs], ALU.mult, [accs[tl][1], b_G], [b_sfp[si]])
                    tt("dve", x1[:, tl, cs], sfp[si], x1[:, tl, cs], ALU.add, [b_sfp[si], b_x1[tl]], [b_x1[tl]])
            for tl in range(4):
                te = 4 * b + tl
                ss, bss = new_stat()
                act(junk, x1[:, tl, :], AF.Square, [b_x1[tl]], [b_junk, bss], accum=ss)
                rstd_of(ss, D, [bss])
                stt("dve", x1[:, tl, :], x1[:, tl, :], ss, gFb, ALU.mult, ALU.mult, [b_x1[tl], bss, b_G], [b_x1[tl]])
                dma("sp", out_d[te * 128:(te + 1) * 128, :], x1[:, tl, :], [b_x1[tl]], [])
            while nxt:
                nxt.pop(0)()
        S.barrier()
        S.emit()
    return nc


_PROGRAM = None


def _lay(v, k):
    return np.ascontiguousarray(np.asarray(v, dtype=np.float32).reshape(k, 128).T)


def kernel(x, c, positions, w_ada, b_ada, norm_mix, w_in, q_norm, w_uq, kv_norm, w_ukv,
           rel_bias, sink, w_o_mla, w_o_swa, w_out, norm_mlp, w_ff1, w_ff2, norm_final):
    global _PROGRAM
    if _PROGRAM is None:
        _PROGRAM = build_program()
    nc = _PROGRAM
    f = lambda a: np.ascontiguousarray(np.asarray(a, dtype=np.float32))
    consts = _host_consts()
    shared = {
        "w_ada": f(w_ada[0]), "b_ada_l": _lay(b_ada[0], 48), "norm_mix_l": _lay(norm_mix[0], 8),
        "w_in": f(w_in[0]), "q_norm_l": _lay(q_norm[0], 3), "w_uq": f(w_uq[0]),
        "kv_norm_l": _lay(kv_norm[0], 2), "w_ukv": f(w_ukv[0]), "rel_bias": f(rel_bias),
        "sink": f(sink[0]), "w_o_mla": f(w_o_mla[0]), "w_o_swa": f(w_o_swa[0]), "w_out": f(w_out[0]),
        "norm_mlp_l": _lay(norm_mlp[0], 8), "w_ff1": f(w_ff1[0]), "w_ff2": f(w_ff2[0]),
        "norm_final_l": _lay(norm_final, 8),
    }
    shared.update(consts)
    x = np.asarray(x, dtype=np.float32)
    c = np.asarray(c, dtype=np.float32)
    positions = np.asarray(positions, dtype=np.int32)
    in_maps = []
    for b in range(N_CORES):
        m = dict(shared)
        m["x"] = np.ascontiguousarray(x[b])
        m["c_l"] = _lay(c[b], 8)
        m["pos_l"] = np.ascontiguousarray(positions[b].reshape(NT, 128).T)
        in_maps.append(m)
    res = run_bass_kernel_spmd(nc, in_maps, core_ids=list(range(N_CORES)))
    kernel.last_results = res
    return np.stack([np.asarray(r["out"], dtype=np.float32) for r in res.results], axis=0)
```

```python
import math
import contextlib
import numpy as np
import concourse.bass as bass
import concourse.mybir as mybir
from concourse.bass_utils import run_bass_kernel_spmd

F32 = mybir.dt.float32
BF16 = mybir.dt.bfloat16
I32 = mybir.dt.int32
AF = mybir.ActivationFunctionType
ALU = mybir.AluOpType

D = 1024
SEQ = 4096
NT = SEQ // 128
NB = SEQ // 512
D_IN = 4288
EPS = 1e-6
MLA_SCALE = 192 ** -0.5
SWA_SCALE = 128 ** -0.5
NEG = -30000.0
TWO_PI = 2.0 * math.pi
C1 = 6.28125
C2 = TWO_PI - C1
PI_SAFE = 3.1415925
N_CORES = 8

DEBUG = {}


class Buf:
    __slots__ = ("name", "w", "r", "psum")

    def __init__(self, name="", psum=False):
        self.name = name
        self.w = None
        self.r = {}
        self.psum = psum


class Sched:
    ENGS = ("pe", "act", "dve", "pool", "sp")

    def __init__(self, nc, n_dma_sems=48):
        self.nc = nc
        self.ops = {e: [] for e in self.ENGS}
        self.cnt = {e: 0 for e in self.ENGS}
        self.known = {e: {} for e in self.ENGS}
        self.n_dma = n_dma_sems
        self.dma_cnt = [0] * n_dma_sems
        half = n_dma_sems // 2
        self.dma_pools = {"sp": list(range(0, half)), "pool": list(range(half, n_dma_sems))}
        self.dma_rr = {"sp": 0, "pool": 0}
        self.sems = {}

    def _need(self, X, waits, key, val):
        if self.known[X].get(key, 0) >= val:
            return
        if waits.get(key, 0) < val:
            waits[key] = val

    @staticmethod
    def _flat(bs):
        out = []
        for b in bs:
            if isinstance(b, (list, tuple)):
                out.extend(Sched._flat(b))
            else:
                out.append(b)
        return out

    def op(self, eng, fn, reads=(), writes=(), inc=True, dma=False):
        X = eng
        reads = self._flat(reads)
        writes = self._flat(writes)
        waits = {}
        for b in reads:
            if b.psum:
                for k, (v, e) in b.r.items():
                    if e != X:
                        self._need(X, waits, k, v)
            if b.w is not None:
                k, v, e = b.w
                if e == X and k == X and X == "pe":
                    continue
                self._need(X, waits, k, v)
        for b in writes:
            if b.w is not None:
                k, v, e = b.w
                if not (e == X and k == X):
                    self._need(X, waits, k, v)
            for k, (v, e) in b.r.items():
                if e == X and k == X:
                    continue
                self._need(X, waits, k, v)
        if dma:
            pl = "pool" if X == "pool" else "sp"
            lst = self.dma_pools[pl]
            i = lst[self.dma_rr[pl]]
            self.dma_rr[pl] = (self.dma_rr[pl] + 1) % len(lst)
            key = "d%d" % i
            if self.dma_cnt[i] > 0:
                self._need(X, waits, key, 16 * self.dma_cnt[i])
            self.dma_cnt[i] += 1
            tok = (key, 16 * self.dma_cnt[i], X)
            incspec = (key, 16)
        else:
            if inc:
                self.cnt[X] += 1
                tok = (X, self.cnt[X], X)
                incspec = (X, 1)
            else:
                tok = (X, self.cnt[X] + 1, X)
                incspec = None
        for k, v in waits.items():
            self.known[X][k] = v
        self.ops[X].append((tuple(waits.items()), fn, incspec))
        k, v, e = tok
        for b in reads:
            old = b.r.get(k)
            if old is None or old[0] < v:
                b.r[k] = (v, e)
        for b in writes:
            b.w = tok
            b.r = {}
        return tok

    def alias(self, news, olds):
        acc = {}
        for o in olds:
            for k, (v, e) in o.r.items():
                if k not in acc or acc[k][0] < v:
                    acc[k] = (v, "?")
            if o.w is not None:
                k, v, e = o.w
                if k not in acc or acc[k][0] < v:
                    acc[k] = (v, "?")
        for n in news:
            for k, (v, e) in acc.items():
                old = n.r.get(k)
                if old is None or old[0] < v:
                    n.r[k] = (v, e)

    def barrier(self):
        for X in self.ENGS:
            waits = {}
            for i in range(self.n_dma):
                if self.dma_cnt[i] > 0:
                    self._need(X, waits, "d%d" % i, 16 * self.dma_cnt[i])
            for e in self.ENGS:
                if e != X and self.cnt[e] > 0:
                    self._need(X, waits, e, self.cnt[e])
            for k, v in waits.items():
                self.known[X][k] = v
            self.ops[X].append((tuple(waits.items()), None, None))

    def emit(self):
        nc = self.nc
        with contextlib.ExitStack() as st:
            keys = list(self.ENGS) + ["d%d" % i for i in range(self.n_dma)]
            for k in keys:
                self.sems[k] = st.enter_context(nc.semaphore("s_" + k))
            block = st.enter_context(nc.Block())
            sems = self.sems

            def run(e, engobj):
                for waits, fn, incspec in self.ops[e]:
                    for k, v in waits:
                        engobj.wait_ge(sems[k], v)
                    if fn is None:
                        continue
                    ins = fn(engobj)
                    if incspec is not None:
                        ins.then_inc(sems[incspec[0]], incspec[1])

            @block.tensor
            def _(t):
                run("pe", t)

            @block.scalar
            def _(s):
                run("act", s)

            @block.vector
            def _(v):
                run("dve", v)

            @block.gpsimd
            def _(g):
                run("pool", g)

            @block.sync
            def _(s):
                run("sp", s)


def _t5_bucket_np(rel):
    rel = np.asarray(rel, dtype=np.int32)
    half, max_exact = 16, 8
    ret = np.where(rel > 0, half, 0)
    n = np.abs(rel)
    nf = np.maximum(n, 1).astype(np.float32)
    large = max_exact + (np.log(nf / np.float32(max_exact)) / np.float32(math.log(128 / max_exact))
                         * np.float32(half - max_exact)).astype(np.int32)
    large = np.minimum(large, half - 1)
    return ret + np.where(n < max_exact, n, large)


_BUCKET_FIX = {16: 10, 32: 12, 64: 14, 128: 15}


def _host_consts():
    ident = np.eye(128, dtype=np.float32)
    J = np.ascontiguousarray(ident[::-1])
    inv = (10000.0 ** (-np.arange(0, 64, 2, dtype=np.float32) / np.float32(64))).astype(np.float32)
    invt = np.ascontiguousarray(np.broadcast_to(inv[None, :], (128, 32))).astype(np.float32)
    oht = np.zeros((33, 512), dtype=np.float32)
    for m in range(512):
        rel = m - 255
        if abs(rel) <= 128:
            n = abs(rel)
            bk = int(_t5_bucket_np(rel))
            if n in _BUCKET_FIX:
                bk = _BUCKET_FIX[n] + (16 if rel > 0 else 0)
            oht[bk, m] = 1.0
        else:
            oht[32, m] = 1.0
    return {"ident": ident, "jrev": J, "invt": invt, "oht": oht}


ARENA_BYTES = 211968
EXTRA = 207872


def build_program():
    nc = bass.Bass("TRN2", target_bir_lowering=False)
    S = Sched(nc, n_dma_sems=48)

    def dram(name, shape, dt, kind="ExternalInput"):
        return nc.dram_tensor(name, shape, dt, kind=kind)

    x_d = dram("x", [SEQ, D], F32).ap()
    c_d = dram("c_l", [128, 8], F32).ap()
    pos_d = dram("pos_l", [128, NT], I32).ap()
    wada_d = dram("w_ada", [D, 6 * D], F32).ap()
    bada_d = dram("b_ada_l", [128, 48], F32).ap()
    nmix_d = dram("norm_mix_l", [128, 8], F32).ap()
    win_d = dram("w_in", [D, D_IN], F32).ap()
    qn_d = dram("q_norm_l", [128, 3], F32).ap()
    wuq_d = dram("w_uq", [384, 1536], F32).ap()
    kvn_d = dram("kv_norm_l", [128, 2], F32).ap()
    wukv_d = dram("w_ukv", [256, 2048], F32).ap()
    rb_d = dram("rel_bias", [32, 8], F32).ap()
    sink_d = dram("sink", [8], F32).ap()
    womla_d = dram("w_o_mla", [D, D], F32).ap()
    woswa_d = dram("w_o_swa", [D, D], F32).ap()
    wout_d = dram("w_out", [D, D], F32).ap()
    nmlp_d = dram("norm_mlp_l", [128, 8], F32).ap()
    wff1_d = dram("w_ff1", [D, 4 * D], F32).ap()
    wff2_d = dram("w_ff2", [4 * D, D], F32).ap()
    nfin_d = dram("norm_final_l", [128, 8], F32).ap()
    ident_d = dram("ident", [128, 128], F32).ap()
    jrev_d = dram("jrev", [128, 128], F32).ap()
    invt_d = dram("invt", [128, 32], F32).ap()
    oht_d = dram("oht", [33, 512], F32).ap()
    tbl_t = dram("tbl_scratch", [8, 512], F32, kind="Internal")
    NPIECE = 74
    wsc = dram("wsc", [NPIECE, 128, 2048], BF16, kind="Internal").ap()
    out_d = dram("out", [SEQ, D], F32, kind="ExternalOutput").ap()
    dbg_d = {}
    for name, shape in DEBUG.items():
        dbg_d[name] = dram("dbg_" + name, list(shape), F32, kind="ExternalOutput").ap()

    st = contextlib.ExitStack()
    with st:
        arena = st.enter_context(nc.sbuf_tensor("arena", [128, ARENA_BYTES // 2], BF16))
        banks = [st.enter_context(nc.psum_tensor("bank%d" % i, [128, 512], F32))[:, :] for i in range(8)]
        bank_b = [Buf("bank%d" % i, psum=True) for i in range(8)]

        def V(off, dt, shape, p0=0, p1=128):
            n = int(np.prod(shape))
            esz = 2 if dt == BF16 else 4
            assert off % 4 == 0 and off + n * esz <= ARENA_BYTES, (off, n, esz)
            v = arena[p0:p1, off // 2:(off + n * esz) // 2]
            if dt != BF16:
                v = v.bitcast(dt)
            if len(shape) == 2:
                v = v.rearrange("p (a b) -> p a b", a=shape[0])
            elif len(shape) == 3:
                v = v.rearrange("p (a b c) -> p a b c", a=shape[0], b=shape[1])
            elif len(shape) == 4:
                v = v.rearrange("p (a b c d) -> p a b c d", a=shape[0], b=shape[1], c=shape[2])
            return v

        def mm(out, lhsT, rhs, start, stop, R, W, inc):
            S.op("pe", lambda e: e.matmul(out, lhsT=lhsT, rhs=rhs, start=start, stop=stop),
                 reads=R, writes=W, inc=inc)

        def tp(out, in_, R, W, inc):
            S.op("pe", lambda e: e.transpose(out=out, in_=in_, identity=ident_f), reads=R + [b_const],
                 writes=W, inc=inc)

        def act(out, in_, func, R, W, bias=None, scale=None, accum=None):
            kw = {}
            if bias is not None:
                kw["bias"] = bias
            if scale is not None:
                kw["scale"] = scale
            if accum is not None:
                kw["accum_out"] = accum
            S.op("act", lambda e: e.activation(out=out, in_=in_, func=func, **kw), reads=R, writes=W)

        def ts(eng, out, in0, s1, s2, op0, op1, R, W):
            if op1 is None:
                S.op(eng, lambda e: e.tensor_scalar(out=out, in0=in0, scalar1=s1, scalar2=None, op0=op0),
                     reads=R, writes=W)
            else:
                S.op(eng, lambda e: e.tensor_scalar(out=out, in0=in0, scalar1=s1, scalar2=s2, op0=op0, op1=op1),
                     reads=R, writes=W)

        def tt(eng, out, in0, in1, op, R, W):
            S.op(eng, lambda e: e.tensor_tensor(out=out, in0=in0, in1=in1, op=op), reads=R, writes=W)

        def stt(eng, out, in0, scalar, in1, op0, op1, R, W):
            S.op(eng, lambda e: e.scalar_tensor_tensor(out=out, in0=in0, scalar=scalar, in1=in1, op0=op0, op1=op1),
                 reads=R, writes=W)

        def cp(eng, out, in_, R, W):
            if eng == "act":
                S.op("act", lambda e: e.copy(out=out, in_=in_), reads=R, writes=W)
            else:
                S.op(eng, lambda e: e.tensor_copy(out=out, in_=in_), reads=R, writes=W)

        def recip(out, in_, R, W):
            S.op("dve", lambda e: e.reciprocal(out=out, in_=in_), reads=R, writes=W)

        def dma(eng, out, in_, R, W):
            S.op(eng, lambda e: e.dma_start(out=out, in_=in_), reads=R, writes=W, dma=True)

        def memset(eng, ap, val, W):
            S.op(eng, lambda e: e.memset(ap, val), writes=W)

        bank_rr = [0]

        reserved = set()

        def nextbank():
            while True:
                i = bank_rr[0]
                bank_rr[0] = (i + 1) % 8
                if i not in reserved:
                    return banks[i], bank_b[i]

        def dump(name, ap, R):
            if name in dbg_d:
                dma("sp", dbg_d[name], ap, R, [])

        ident_f = V(0, F32, [128])
        ones_f = V(512, F32, [128])
        ones_bf = V(1024, BF16, [128])
        jrev_f = V(1280, F32, [128])
        b_const = Buf("const")
        SV = 2048

        def sv(i, n):
            return V(SV + 4 * i, F32, [n])

        cT = sv(0, 8); cexp = sv(8, 8); cact2 = V(SV + 64, F32, [8, 2])
        modT = sv(32, 48); badaT = sv(80, 48)
        nmix = sv(128, 8); nmlp = sv(136, 8); nfin = sv(144, 8)
        a1 = sv(152, 8); a2 = sv(160, 8)
        qn = sv(168, 3); kvn = sv(172, 2)
        sinkexp = sv(176, 8)
        pos_f = sv(184, 32)
        pos_i = V(SV + 4 * 216, I32, [32])
        stat = sv(248, 16)
        invt = sv(264, 32)
        cos_t = V(4096, F32, [32, 32])
        sin_t = V(8192, F32, [32, 32])
        rb_aug = V(3328, F32, [8], 0, 33)
        tbl_sb = V(12288, F32, [512], 0, 8)
        oht = V(14336, F32, [512], 0, 33)
        b_small = Buf("small")
        b_trig = Buf("trig")
        b_mod = Buf("mod")

        OT_OFF = 16384
        OT = V(OT_OFF, BF16, [8, SEQ])
        b_OT = [[Buf("OT%d_%d" % (h, q)) for q in range(NB)] for h in range(8)]
        P1O = 81920
        cqnT = V(P1O, BF16, [3, SEQ])
        ckvnT = V(P1O + 24576, BF16, [2, SEQ])
        KrT = V(P1O + 40960, BF16, [SEQ])
        b_cqn = [Buf("cqn%d" % t) for t in range(NT)]
        b_ckvn = [Buf("ckvn%d" % t) for t in range(NT)]
        b_kr = [Buf("kr%d" % t) for t in range(NT)]
        TR = 131072

        dma("sp", ident_f, ident_d, [], [b_const])
        dma("sp", jrev_f, jrev_d, [], [b_const])
        dma("sp", invt, invt_d, [], [b_small])
        dma("sp", oht, oht_d, [], [b_small])
        dma("sp", cT, c_d, [], [b_small])
        dma("sp", badaT, bada_d, [], [b_small])
        dma("sp", nmix, nmix_d, [], [b_small])
        dma("sp", nmlp, nmlp_d, [], [b_small])
        dma("sp", nfin, nfin_d, [], [b_small])
        dma("sp", qn, qn_d, [], [b_small])
        dma("sp", kvn, kvn_d, [], [b_small])
        dma("sp", pos_i, pos_d, [], [b_small])
        dma("sp", sinkexp, sink_d.partition_broadcast(128), [], [b_small])
        dma("sp", rb_aug[0:32, :], rb_d, [], [b_small])
        memset("dve", rb_aug[32:33, :], NEG, [b_small])
        memset("dve", ones_f, 1.0, [b_const])
        memset("dve", ones_bf, 1.0, [b_const])

        act(cexp, cT, AF.Exp, [b_small], [b_small], scale=-1.0)
        ts("dve", cexp, cexp, 1.0, None, ALU.add, None, [b_small], [b_small])
        recip(cexp, cexp, [b_small], [b_small])
        tt("dve", cact2[:, :, 0], cT, cexp, ALU.mult, [b_small], [b_small])
        tt("dve", cact2[:, :, 1], cT, cexp, ALU.mult, [b_small], [b_small])
        act(sinkexp, sinkexp, AF.Exp, [b_small], [b_small])

        tg = [V(TR + 32768 + 4096 * i, F32, [32, 32]) for i in range(4)]
        tgi = V(TR + 32768 + 4096 * 4, I32, [32, 32])
        b_tg = Buf("tg")
        cp("dve", pos_f, pos_i, [b_small], [b_small])
        ang, nf, rr, mk = tg
        tt("dve", ang, pos_f.unsqueeze(2).broadcast_to([128, 32, 32]),
           invt.unsqueeze(1).broadcast_to([128, 32, 32]), ALU.mult, [b_small], [b_tg])
        ts("dve", tgi, ang, 1.0 / TWO_PI, None, ALU.mult, None, [b_tg], [b_tg])
        cp("dve", nf, tgi, [b_tg], [b_tg])
        stt("dve", rr, nf, -C1, ang, ALU.mult, ALU.add, [b_tg], [b_tg])
        stt("dve", rr, nf, -C2, rr, ALU.mult, ALU.add, [b_tg], [b_tg])

        def wrap(r):
            ts("dve", mk, r, math.pi, TWO_PI, ALU.is_gt, ALU.mult, [b_tg], [b_tg])
            tt("dve", r, r, mk, ALU.subtract, [b_tg], [b_tg])
            ts("dve", mk, r, -math.pi, TWO_PI, ALU.is_lt, ALU.mult, [b_tg], [b_tg])
            tt("dve", r, r, mk, ALU.add, [b_tg], [b_tg])
            ts("dve", r, r, -PI_SAFE, PI_SAFE, ALU.max, ALU.min, [b_tg], [b_tg])

        wrap(rr)
        act(sin_t, rr, AF.Sin, [b_tg], [b_trig])
        ts("dve", rr, rr, math.pi / 2, None, ALU.add, None, [b_tg], [b_tg])
        wrap(rr)
        act(cos_t, rr, AF.Sin, [b_tg], [b_trig])

        stg = [V(TR + 16384 * i, F32, [8, 512]) for i in range(2)]
        b_stg = [Buf("stg0"), Buf("stg1")]
        psM, b_psM = nextbank()
        psMv = psM[:, 0:96].rearrange("p (a b) -> p a b", b=2)
        for pc in range(12):
            sl = pc % 2
            dma("sp", stg[sl], wada_d[:, pc * 512:(pc + 1) * 512].rearrange("(k p) n -> p k n", p=128),
                [], [b_stg[sl]])
            for j in range(4):
                cc = pc * 4 + j
                for k in range(8):
                    mm(psMv[:, cc, :], stg[sl][:, k, j * 128:(j + 1) * 128], cact2[:, k, :], k == 0, k == 7,
                       [b_stg[sl], b_small], [b_psM], inc=(k == 7))
        tt("dve", modT, psMv[:, :, 0], badaT, ALU.add, [b_psM, b_small], [b_mod])
        stt("dve", a1, modT[:, 8:16], 1.0, nmix, ALU.add, ALU.mult, [b_mod, b_small], [b_mod])
        stt("dve", a2, modT[:, 32:40], 1.0, nmlp, ALU.add, ALU.mult, [b_mod, b_small], [b_mod])
        sh1 = modT[:, 0:8]; g1v = modT[:, 16:24]; sh2 = modT[:, 24:32]; g2v = modT[:, 40:48]
        S.barrier()

        stat_rr = [0]

        def rstd_of(ss_ap, n, R):
            act(ss_ap, ss_ap, AF.Ln, R, R, bias=EPS, scale=1.0 / n)
            act(ss_ap, ss_ap, AF.Exp, R, R, scale=-0.5)
            return ss_ap

        stat_bufs = [Buf("stat%d" % i) for i in range(16)]

        def new_stat():
            i = stat_rr[0]
            stat_rr[0] = (i + 1) % 16
            return stat[:, i:i + 1], stat_bufs[i]

        w704 = V(TR, BF16, [8, 704]); b_w704 = Buf("w704")
        xt = [V(TR + 11264 + 4096 * i, F32, [1024]) for i in range(2)]; b_xt = [Buf(), Buf()]
        xn = [V(TR + 19456 + 4096 * i, F32, [1024]) for i in range(2)]; b_xn = [Buf(), Buf()]
        hT = V(TR + 27648, BF16, [8, 512]); b_hT = [[Buf() for _ in range(8)] for _ in range(4)]
        junk = V(TR + 35840, BF16, [1024]); b_junk = Buf("junk")
        cqkv = [V(TR + 37888 + 2560 * i, F32, [640]) for i in range(2)]; b_cqkv = [Buf(), Buf()]
        krr = [V(TR + 43008 + 512 * i, F32, [128]) for i in range(2)]; b_krr = [Buf(), Buf()]
        rtmp = [V(TR + 44032 + 128 * i, F32, [32]) for i in range(4)]; b_rtmp = Buf("rtmp")

        dma("pool", w704, win_d[:, 0:704].rearrange("(k p) n -> p k n", p=128), [], [b_w704])

        pspec = {}
        plist = []

        def addp(key, src2d, kch, ncols):
            pspec[key] = (len(plist), kch, ncols)
            plist.append((key, src2d, kch, ncols))

        addp(("ks",), win_d[:, 1728:1984], 8, 256)
        addp(("vs",), win_d[:, 1984:2240], 8, 256)
        for pc in range(4):
            addp(("qs", pc), win_d[:, 704 + pc * 256:704 + (pc + 1) * 256], 8, 256)
        for c in range(8):
            addp(("g0", c), win_d[:, 2240 + c * 128:2240 + (c + 1) * 128], 8, 128)
            addp(("g1", c), win_d[:, 3264 + c * 128:3264 + (c + 1) * 128], 8, 128)
            addp(("oa", c), womla_d[:, c * 128:(c + 1) * 128], 8, 128)
            addp(("ob", c), woswa_d[:, c * 128:(c + 1) * 128], 8, 128)
        for cq in range(4):
            addp(("wo", cq), wout_d[:, cq * 256:(cq + 1) * 256], 8, 256)
        for pc in range(16):
            addp(("f1", pc), wff1_d[:, pc * 256:(pc + 1) * 256], 8, 256)
        for hf in range(2):
            for g in range(8):
                addp(("f2", hf, g), wff2_d[g * 512:(g + 1) * 512, hf * 512:(hf + 1) * 512], 4, 512)
        assert len(plist) == NPIECE
        b_wsc = [Buf("wsc%d" % i) for i in range(NPIECE)]
        for i, (key, src2d, kch, ncols) in enumerate(plist):
            dma("pool", wsc[i][:, 0:kch * ncols].rearrange("p (k n) -> p k n", k=kch),
                src2d.rearrange("(k p) n -> p k n", p=128), [], [b_wsc[i]])

        def make_nt(ring, b_ring, fixed_banks=None):
            rr = [0]

            def front(src, bsrc, load=None):
                i = rr[0]
                rr[0] = (i + 1) % len(ring)
                if load is not None:
                    dma("sp", ring[i], load, [], [b_ring[i]])
                    src, bsrc = ring[i], b_ring[i]
                ss, bss = new_stat()
                act(junk, src, AF.Square, [bsrc], [b_junk, bss], accum=ss)
                rstd_of(ss, D, [bss])
                ts("dve", ring[i], src, ss, None, ALU.mult, None, [bsrc, bss], [b_ring[i]])
                return i

            def back(i, avec, shvec, dst_fn, bdst):
                for hf in range(2):
                    if fixed_banks is None:
                        ps, bps = nextbank()
                    else:
                        ps, bps = banks[fixed_banks[hf]], bank_b[fixed_banks[hf]]
                    for j in range(4):
                        k = hf * 4 + j
                        tp(ps[:, j * 128:(j + 1) * 128], ring[i][:, k * 128:(k + 1) * 128], [b_ring[i]], [bps],
                           inc=(j == 3))
                    for j in range(4):
                        k = hf * 4 + j
                        bd = bdst[k] if isinstance(bdst, list) else bdst
                        if hf == 0:
                            act(dst_fn(k), ps[:, j * 128:(j + 1) * 128], AF.Identity, [bps, b_mod], [bd],
                                bias=shvec[:, k:k + 1], scale=avec[:, k:k + 1])
                        else:
                            ts("dve", dst_fn(k), ps[:, j * 128:(j + 1) * 128], avec[:, k:k + 1], shvec[:, k:k + 1],
                               ALU.mult, ALU.add, [bps, b_mod], [bd])

            return front, back

        p1_front, p1_back = make_nt([xt[0], xt[1], xn[0], xn[1]], [b_xt[0], b_xt[1], b_xn[0], b_xn[1]], fixed_banks=(0, 1))
        p1_slot = {}
        p1_ps = {}

        def p1_A(t):
            p1_slot[t] = p1_front(None, None, load=x_d[t * 128:(t + 1) * 128, :])

        def p1_B(t):
            tl = t % 4
            p1_back(p1_slot.pop(t), a1, sh1, lambda k, tl=tl: hT[:, k, tl * 128:(tl + 1) * 128], b_hT[tl])
            ia = 2 + 2 * (t % 2)
            psA, bA, psB, bB = banks[ia], bank_b[ia], banks[ia + 1], bank_b[ia + 1]
            for k in range(8):
                mm(psA[:, 0:384], hT[:, k, tl * 128:(tl + 1) * 128], w704[:, k, 0:384], k == 0, k == 7,
                   [b_hT[tl], b_w704], [bA], inc=(k == 7))
            for k in range(8):
                mm(psB[:, 0:320], hT[:, k, tl * 128:(tl + 1) * 128], w704[:, k, 384:704], k == 0, k == 7,
                   [b_hT[tl], b_w704], [bB], inc=(k == 7))
            p1_ps[t] = (psA, bA, psB, bB)

        def p1_C(t):
            sl = t % 2
            psA, bA, psB, bB = p1_ps.pop(t)
            ssq, bq = new_stat()
            act(junk[:, 0:384], psA[:, 0:384], AF.Square, [bA], [b_junk, bq], accum=ssq)
            rstd_of(ssq, 384, [bq])
            sskv, bkv = new_stat()
            act(junk[:, 0:256], psB[:, 0:256], AF.Square, [bB], [b_junk, bkv], accum=sskv)
            rstd_of(sskv, 256, [bkv])
            ts("dve", cqkv[sl][:, 0:384], psA[:, 0:384], ssq, None, ALU.mult, None, [bA, bq], [b_cqkv[sl]])
            ts("dve", cqkv[sl][:, 384:640], psB[:, 0:256], sskv, None, ALU.mult, None, [bB, bkv], [b_cqkv[sl]])
            x1_ = psB[:, 256:288]; x2_ = psB[:, 288:320]
            ct = cos_t[:, t, :]; sn = sin_t[:, t, :]
            tt("dve", rtmp[0], x1_, ct, ALU.mult, [bB, b_trig], [b_rtmp])
            tt("dve", rtmp[1], x2_, sn, ALU.mult, [bB, b_trig], [b_rtmp])
            tt("dve", rtmp[2], x2_, ct, ALU.mult, [bB, b_trig], [b_rtmp])
            tt("dve", rtmp[3], x1_, sn, ALU.mult, [bB, b_trig], [b_rtmp])
            tt("dve", krr[sl][:, 0:32], rtmp[0], rtmp[1], ALU.subtract, [b_rtmp], [b_krr[sl]])
            tt("dve", krr[sl][:, 32:64], rtmp[2], rtmp[3], ALU.add, [b_rtmp], [b_krr[sl]])
            cp("dve", krr[sl][:, 64:128], krr[sl][:, 0:64], [b_krr[sl]], [b_krr[sl]])
            psT, bT = banks[6], bank_b[6]
            for j in range(3):
                tp(psT[:, j * 128:(j + 1) * 128], cqkv[sl][:, j * 128:(j + 1) * 128], [b_cqkv[sl]], [bT], inc=(j == 2))
            psU, bU = banks[7], bank_b[7]
            for j in range(2):
                tp(psU[:, j * 128:(j + 1) * 128], cqkv[sl][:, 384 + j * 128:384 + (j + 1) * 128], [b_cqkv[sl]], [bU],
                   inc=False)
            tp(psU[:, 256:384], krr[sl], [b_krr[sl]], [bU], inc=True)
            tok = slice(t * 128, (t + 1) * 128)
            for j in range(3):
                ts("dve", cqnT[:, j, tok], psT[:, j * 128:(j + 1) * 128], qn[:, j:j + 1], None, ALU.mult, None,
                   [bT, b_small], [b_cqn[t]])
            for j in range(2):
                act(ckvnT[:, j, tok], psU[:, j * 128:(j + 1) * 128], AF.Identity, [bU, b_small], [b_ckvn[t]],
                    scale=kvn[:, j:j + 1])
            cp("act", KrT[:, tok], psU[:, 256:384], [bU], [b_kr[t]])

        p1_A(0)
        p1_A(1)
        for n in range(1, NT + 2):
            if 0 <= n - 1 < NT:
                p1_B(n - 1)
            if 0 <= n - 2 < NT:
                p1_C(n - 2)
            if n + 1 < NT:
                p1_A(n + 1)
        dump("cqnT", cqnT[:, :, 0:512], b_cqn[0:4])
        dump("ckvnT", ckvnT[:, :, 0:512], b_ckvn[0:4])
        dump("KrT", KrT[:, 0:512], b_kr[0:4])
        S.barrier()

        KT = [V(TR + 8192 * i, BF16, [SEQ]) for i in range(2)]
        Vt = [V(TR + 16384 + 8192 * i, BF16, [NT, 128]) for i in range(2)]
        QTn = [V(TR + 32768 + 8192 * i, BF16, [SEQ]) for i in range(2)]
        QTr = V(TR + 49152, BF16, [SEQ])
        ropeT = [V(TR + 57344 + 2048 * i, F32, [4, 2, 32]) for i in range(4)]
        b_KT = [[Buf() for _ in range(NB)] for _ in range(2)]
        b_V = [[Buf() for _ in range(NB)] for _ in range(2)]
        b_QTn = [[Buf() for _ in range(NB)] for _ in range(2)]
        b_QTr = [[Buf() for _ in range(NB)] for _ in range(2)]
        b_ropeT = Buf("ropeT")
        wqn = [V(TR + 65536 + 1792 * i, BF16, [3, 128]) for i in range(2)]
        wk = [V(TR + 65536 + 1792 * i + 768, BF16, [2, 128]) for i in range(2)]
        wv = [V(TR + 65536 + 1792 * i + 1280, BF16, [2, 128]) for i in range(2)]
        b_wh = [Buf(), Buf()]
        wqr = V(TR + 69120, BF16, [3, 2, 64]); b_wqr = Buf("wqr")
        qrot = V(TR + 69120 + 768, F32, [4, 2, 2, 32])
        PT = [V(TR + 70656 + 1024 * i, BF16, [512]) for i in range(4)] + [V(TR + 57344 + 7168, BF16, [512])]
        b_PT = [Buf() for _ in range(5)]
        recs = [V(TR + 74752, F32, [512])] * 2
        b_recs = [Buf("rec")] * 2
        sc_rr = [0]
        ropeT = [V(TR + 57344 + 1024 * i, F32, [4, 2, 32]) for i in range(3)]
        qrot = V(TR + 57344 + 3072, F32, [4, 2, 2, 32])
        b_qrot = Buf("qrot")
        QTrz = [V(TR + 57344 + 5120 + 1024 * i, BF16, [512]) for i in range(2)]
        b_QTrz = [Buf(), Buf()]
        pt_rr = [0]
        ev_rr = [0]
        dacc = [V(EXTRA + 2048 * i, F32, [512]) for i in range(2)]; b_dacc = [Buf(), Buf()]
        dacc_rr = [0]

        def evac(out, in_, R, W):
            ev_rr[0] += 1
            cp("dve", out, in_, R, W)

        def load_head_weights(h):
            sl = h % 2
            dma("pool", wqn[sl], wuq_d[:, h * 192:h * 192 + 128].rearrange("(k p) n -> p k n", p=128), [], [b_wh[sl]])
            dma("pool", wk[sl], wukv_d[:, h * 256:h * 256 + 128].rearrange("(k p) n -> p k n", p=128), [], [b_wh[sl]])
            dma("pool", wv[sl], wukv_d[:, h * 256 + 128:h * 256 + 256].rearrange("(k p) n -> p k n", p=128), [],
                [b_wh[sl]])
            dma("pool", wqr[:, :, sl, :],
                wuq_d[:, h * 192 + 128:h * 192 + 192].rearrange("(k p) n -> p k n", p=128), [], [b_wh[sl]])

        def prod_steps(h, bank):
            sl = h % 2
            steps = []

            def getbank():
                if bank is None:
                    return nextbank()
                return banks[bank], bank_b[bank]

            def mk_q(g):
                def f():
                    cols = slice(g * 512, (g + 1) * 512)
                    ps, bps = getbank()
                    for kc in range(3):
                        mm(ps, wqn[sl][:, kc, :], cqnT[:, kc, cols], kc == 0, kc == 2,
                           [b_wh[sl]] + b_cqn[4 * g:4 * g + 4], [bps], inc=(kc == 2))
                    evac(QTn[sl][:, cols], ps, [bps], [b_QTn[sl][g]])
                return f

            def mk_k(g):
                def f():
                    cols = slice(g * 512, (g + 1) * 512)
                    ps, bps = getbank()
                    for kc in range(2):
                        mm(ps, wk[sl][:, kc, :], ckvnT[:, kc, cols], kc == 0, kc == 1,
                           [b_wh[sl]] + b_ckvn[4 * g:4 * g + 4], [bps], inc=(kc == 1))
                    evac(KT[sl][:, cols], ps, [bps], [b_KT[sl][g]])
                return f

            def mk_v(g):
                def f():
                    ps, bps = getbank()
                    for tl in range(4):
                        t = 4 * g + tl
                        for kc in range(2):
                            mm(ps[:, tl * 128:(tl + 1) * 128], ckvnT[:, kc, t * 128:(t + 1) * 128], wv[sl][:, kc, :],
                               kc == 0, kc == 1, [b_wh[sl], b_ckvn[t]], [bps], inc=(kc == 1 and tl == 3))
                    evac(Vt[sl][:, 4 * g:4 * g + 4, :].rearrange("p t d -> p (t d)"), ps, [bps], [b_V[sl][g]])
                return f

            e_ = h % 2

            def mk_ra(g):
                def f():
                    ps, bps = getbank()
                    for tl in range(4):
                        t = 4 * g + tl
                        for kc in range(3):
                            mm(ps[:, tl * 64:(tl + 1) * 64], cqnT[:, kc, t * 128:(t + 1) * 128], wqr[:, kc, e_, :],
                               kc == 0, kc == 2, [b_cqn[t], b_wh[sl]], [bps], inc=(kc == 2 and tl == 3))
                    psv = ps[:, 0:256].rearrange("p (t f i) -> p t f i", t=4, f=2)
                    cb = cos_t[:, 4 * g:4 * g + 4, :]
                    sb_ = sin_t[:, 4 * g:4 * g + 4, :]
                    x1_ = psv[:, :, 0, :]; x2_ = psv[:, :, 1, :]
                    r0 = ropeT[0][:, :, 0, :]; r1 = ropeT[1][:, :, 0, :]
                    tt("dve", r0, x1_, cb, ALU.mult, [bps, b_trig], [b_ropeT])
                    tt("dve", r1, x2_, sb_, ALU.mult, [bps, b_trig], [b_ropeT])
                    tt("dve", qrot[:, :, e_, 0, :], r0, r1, ALU.subtract, [b_ropeT], [b_qrot])
                    tt("dve", r0, x2_, cb, ALU.mult, [bps, b_trig], [b_ropeT])
                    tt("dve", r1, x1_, sb_, ALU.mult, [bps, b_trig], [b_ropeT])
                    tt("dve", qrot[:, :, e_, 1, :], r0, r1, ALU.add, [b_ropeT], [b_qrot])
                return f

            def mk_rb(g):
                def f():
                    ps2, bps2 = getbank()
                    qflat = qrot.rearrange("p t h f i -> p (t h f i)")
                    for tl in range(4):
                        tp(ps2[:, tl * 128:(tl + 1) * 128], qflat[:, tl * 128:(tl + 1) * 128], [b_qrot], [bps2],
                           inc=(tl == 3))
                    evac(QTr[e_ * 64:(e_ + 1) * 64, g * 512:(g + 1) * 512], ps2[e_ * 64:(e_ + 1) * 64, :], [bps2],
                         [b_QTr[e_][g]])
                return f

            for g in range(NB):
                steps += [mk_ra(g), mk_q(g), mk_rb(g), mk_k(g), mk_v(g)]
            return steps

        for hp in range(4):
            for e in range(2):
                h = 2 * hp + e
                sl = e
                if h == 0:
                    load_head_weights(0)
                    for st_ in prod_steps(0, None):
                        st_()
                nxt_prod = []
                if h + 1 < 8:
                    load_head_weights(h + 1)
                    nxt_prod = prod_steps(h + 1, 3)
                for zi in range(2):
                    memset("pool", QTrz[zi][(1 - e) * 64:(2 - e) * 64, :], 0.0, [b_QTrz[zi]])
                LA = 3
                items = [(qb, kc) for qb in range(NB) for kc in range(NT)]
                accs = {}
                pend = {}
                dst = {}
                deferred = []

                def acc_of(qb):
                    if qb not in accs:
                        a = (qb % 2) * 2
                        accs[qb] = (banks[a], bank_b[a], banks[1], bank_b[1])
                    return accs[qb]

                def scores(qb, kc, sl=sl, e=e):
                    qc = slice(qb * 512, (qb + 1) * 512)
                    zi = qb % 2
                    if kc == 0:
                        cp("pool", QTrz[zi][e * 64:(e + 1) * 64, :], QTr[e * 64:(e + 1) * 64, qc], [b_QTr[e][qb]],
                           [b_QTrz[zi]])
                    si = 4 + sc_rr[0]
                    sc_rr[0] = (sc_rr[0] + 1) % 4
                    ps, bps = banks[si], bank_b[si]
                    kcs = slice(kc * 128, (kc + 1) * 128)
                    mm(ps, KT[sl][:, kcs], QTn[sl][:, qc], True, False,
                       [b_KT[sl][kc // 4], b_QTn[sl][qb]], [bps], inc=False)
                    mm(ps, KrT[:, kcs], QTrz[zi], False, True, [b_kr[kc], b_QTrz[zi]], [bps], inc=True)
                    i = pt_rr[0]
                    pt_rr[0] = (i + 1) % len(PT)
                    act(PT[i], ps, AF.Exp, [bps], [b_PT[i]], scale=MLA_SCALE)
                    pend[(qb, kc)] = i

                def pv(qb, kc, sl=sl):
                    accO, bO, accD, bD = acc_of(qb)
                    i = pend.pop((qb, kc))
                    on_pe = False
                    mm(accO, Vt[sl][:, kc, :], PT[i], kc == 0, kc == NT - 1, [b_V[sl][kc // 4], b_PT[i]], [bO],
                       inc=not on_pe)
                    if on_pe:
                        mm(accD, ones_bf, PT[i], kc == 7, False, [b_const, b_PT[i]], [bD], inc=True)
                    else:
                        d = dst.setdefault(qb, {"n": 0, "used": [False, False]})
                        j = d["n"] % 2
                        d["n"] += 1
                        if not d["used"][j]:
                            d["used"][j] = True
                            cp("dve", dacc[j], PT[i], [b_PT[i]], [b_dacc[j]])
                        else:
                            tt("dve", dacc[j], dacc[j], PT[i], ALU.add, [b_dacc[j], b_PT[i]], [b_dacc[j]])

                def epi_pe(qb):
                    accO, bO, accD, bD = acc_of(qb)
                    ri = qb % 2
                    mm(accD, ones_f, recs[ri], True, True, [b_const, b_recs[ri]], [bD], inc=True)

                def epi_dve(qb, h=h):
                    accO, bO, accD, bD = acc_of(qb)
                    ri = qb % 2
                    qc = slice(qb * 512, (qb + 1) * 512)
                    act(recs[ri], accD, AF.Ln, [bD], [b_recs[ri]])
                    act(recs[ri], recs[ri], AF.Exp, [b_recs[ri]], [b_recs[ri]], scale=-1.0)
                    tt("dve", OT[:, h, qc], accO, recs[ri], ALU.mult, [bO, b_recs[ri]], [b_OT[h][qb]])

                for n in range(min(LA, len(items))):
                    scores(*items[n])
                for n, (qb, kc) in enumerate(items):
                    pv(qb, kc)
                    if n + LA < len(items):
                        scores(*items[n + LA])
                    if kc == NT - 1:
                        ri = qb % 2
                        tt("dve", recs[ri], dacc[0], dacc[1], ALU.add, [b_dacc[0], b_dacc[1]], [b_recs[ri]])
                        deferred.append(qb)
                    if kc == 3 and deferred:
                        epi_pe(deferred[0])
                    if kc == 5 and deferred:
                        epi_dve(deferred.pop(0))
                    if n % 6 == 4 and nxt_prod:
                        nxt_prod.pop(0)()
                for qb in deferred:
                    epi_pe(qb)
                    epi_dve(qb)
                while nxt_prod:
                    nxt_prod.pop(0)()
        dump("OT", OT[:, :, 0:512].bitcast(BF16) if False else OT[:, :, 0:512], [b_OT[h][0] for h in range(8)])
        S.barrier()

        P3 = P1O
        G1b = V(P3, F32, [1024]); G2b = V(P3 + 4096, F32, [1024]); gFb = V(P3 + 8192, F32, [1024])
        Bm = V(P3 + 12288, F32, [3, 8, 128])
        b_G = Buf("G"); b_Bm = Buf("Bm")
        hTe = V(P3 + 24576, BF16, [8, 768]); b_hTe = [[Buf() for _ in range(8)] for _ in range(6)]
        U = P3 + 36864
        ksT = V(U, BF16, [2, 768]); b_ks = Buf("ks")
        vs = V(U + 3072, BF16, [6, 256]); b_vs = [Buf() for _ in range(6)]
        qsT = V(U + 6144, BF16, [8, 512]); b_qs = [Buf() for _ in range(8)]
        swaOT = V(U + 14336, BF16, [8, 512]); b_swaO = [[Buf() for _ in range(4)] for _ in range(2)]
        mergedT = V(U + 22528, BF16, [8, 512]); b_mg = [Buf() for _ in range(8)]
        uT = V(U, BF16, [32, 512]); b_uT = [Buf() for _ in range(32)]
        u_old = [b_ks] + b_vs + b_qs + b_swaO[0] + b_swaO[1] + b_mg
        XB = U + 32768
        xb = [V(XB + 4096 * i, F32, [1024]) for i in range(2)]; b_xb = [Buf(), Buf()]
        x1 = V(XB + 8192, F32, [4, 1024]); b_x1 = [Buf() for _ in range(4)]
        PT3 = [V(XB + 24576 + 1024 * i, BF16, [4, 128]) for i in range(3)]; b_PT3 = [Buf() for _ in range(3)]
        WR = XB + 27648
        b_wrh = [Buf() for _ in range(8)]
        b_wr = [[b_wrh[0], b_wrh[1]]]
        xn3 = V(WR + 16384, F32, [1024]); b_xn3 = Buf("xn3")
        sfp = [V(WR + 20480 + 2048 * i, F32, [512]) for i in range(3)]; b_sfp = [Buf() for _ in range(3)]
        junk3 = V(WR + 26624, BF16, [1024])
        dtmp = xn3[:, 0:512]; b_dtmp = b_xn3
        for i_ in range(4):
            PT3.append(V(EXTRA + 1024 * i_, BF16, [4, 128])); b_PT3.append(Buf())
        swb_rr = [0]; sfp_rr = [0]; pt3_rr = [0]
        assert WR + 26624 + 2048 <= ARENA_BYTES
        junk = junk3
        wr_rr = [0]

        def wpiece(*key):
            idx, kch, ncols = pspec[key]
            i = wr_rr[0]
            nh = 1 if kch * ncols <= 1024 else 2
            if nh == 2 and i % 2 == 1:
                i += 1
            i %= 8
            wr_rr[0] = (i + nh) % 8
            bw = b_wrh[i:i + nh]
            v = V(WR + 2048 * i, BF16, [kch, ncols])
            dma("pool", v, wsc[idx][:, 0:kch * ncols].rearrange("p (k n) -> p k n", k=kch), [b_wsc[idx]], bw)
            return v, bw

        dg = [V(WR + 20480 + 2048 * i, F32, [128]) for i in range(2)]
        for gi, (gvec, Gb, bsrc) in enumerate(((g1v, G1b, b_mod), (g2v, G2b, b_mod), (nfin, gFb, b_small))):
            for hf in range(2):
                ps, bps = nextbank()
                for j in range(4):
                    c = hf * 4 + j
                    d = dg[c % 2]
                    bd = b_sfp[c % 2]
                    ts("dve", d, ident_f, gvec[:, c:c + 1], None, ALU.mult, None, [b_const, bsrc], [bd])
                    mm(ps[:, j * 128:(j + 1) * 128], ones_f, d, True, True, [b_const, bd], [bps], inc=True)
                cp("dve", Gb[:, hf * 512:(hf + 1) * 512], ps, [bps], [b_G])
        ps, bps = nextbank()
        mm(ps[0:8, :], rb_aug, oht, True, True, [b_small], [bps], inc=True)
        b_tbl = Buf("tbl")
        cp("dve", tbl_sb, ps[0:8, :], [bps], [b_tbl])
        b_tbld = Buf("tbld")
        dma("sp", tbl_t.ap(), tbl_sb, [b_tbl], [b_tbld])
        hank = V(WR, F32, [8, 128])
        for dl in range(3):
            src = bass.AP(tensor=tbl_t, offset=dl * 128, ap=[[1, 128], [512, 8], [1, 128]])
            dma("sp", hank, src, [b_tbld], [b_wr[0]])
            for hh in range(2):
                ps, bps = nextbank()
                for j in range(4):
                    h = hh * 4 + j
                    mm(ps[:, j * 128:(j + 1) * 128], hank[:, h, :], jrev_f, True, True, [b_wr[0], b_const], [bps],
                       inc=(j == 3))
                cp("dve", Bm[:, dl, hh * 4:(hh + 1) * 4, :].rearrange("p h q -> p (h q)"), ps, [bps], [b_Bm])
        dump("Bm", Bm.rearrange("p a h q -> p (a h q)"), [b_Bm])
        dump("G1b", G1b, [b_G])

        nt_front, nt_back = make_nt([xb[0], xb[1], xn3], [b_xb[0], b_xb[1], b_xn3])

        def hext_steps(b):
            valid = [j for j in range(6) if 0 <= 4 * b - 1 + j < NT]
            slot = {}
            steps = []

            def mk_front(j):
                def f():
                    te = 4 * b - 1 + j
                    slot[j] = nt_front(None, None, load=x_d[te * 128:(te + 1) * 128, :])
                return f

            def mk_back(j):
                def f():
                    nt_back(slot[j], a1, sh1, lambda k, j=j: hTe[:, k, j * 128:(j + 1) * 128], b_hTe[j])
                return f

            for n, j in enumerate(valid):
                steps.append(mk_front(j))
                if n >= 1:
                    steps.append(mk_back(valid[n - 1]))
            steps.append(mk_back(valid[-1]))
            return steps

        def act_recip(buf, src, R, W):
            act(buf, src, AF.Ln, R, W)
            act(buf, buf, AF.Exp, W, W, scale=-1.0)

        for st_ in hext_steps(0):
            st_()
        for b in range(NB):
            S.alias(u_old, b_uT)
            valid = [j for j in range(6) if 0 <= 4 * b - 1 + j < NT]
            own = slice(128, 640)
            b_own = b_hTe[1:5]
            for tl in range(4):
                te = 4 * b + tl
                dma("sp", x1[:, tl, :], x_d[te * 128:(te + 1) * 128, :], [], [b_x1[tl]])
            wp, bwp = wpiece("ks")
            for kv in range(2):
                for (j0, j1) in ((0, 4), (4, 6)):
                    js = [j for j in valid if j0 <= j < j1]
                    if not js:
                        continue
                    cs = slice(js[0] * 128, (js[-1] + 1) * 128)
                    n = (js[-1] + 1 - js[0]) * 128
                    ps, bps = nextbank()
                    for k in range(8):
                        mm(ps[:, 0:n], wp[:, k, kv * 128:(kv + 1) * 128], hTe[:, k, cs], k == 0, k == 7,
                           [bwp] + [b_hTe[j] for j in js], [bps], inc=(k == 7))
                    cp("act", ksT[:, kv, cs], ps[:, 0:n], [bps], [b_ks])
            wp, bwp = wpiece("vs")
            for j in valid:
                ps, bps = nextbank()
                for k in range(8):
                    mm(ps[:, 0:256], hTe[:, k, j * 128:(j + 1) * 128], wp[:, k, :], k == 0, k == 7,
                       [bwp, b_hTe[j]], [bps], inc=(k == 7))
                cp("act", vs[:, j, :], ps[:, 0:256], [bps], [b_vs[j]])
            for pc in range(4):
                wp, bwp = wpiece("qs", pc)
                for hh in range(2):
                    h = pc * 2 + hh
                    ps, bps = nextbank()
                    for k in range(8):
                        mm(ps, wp[:, k, hh * 128:(hh + 1) * 128], hTe[:, k, own], k == 0, k == 7,
                           [bwp] + b_own, [bps], inc=(k == 7))
                    cp("act", qsT[:, h, :], ps, [bps], [b_qs[h]])
            units = [(qt, kv) for qt in range(4) for kv in range(2)]
            sw = {}

            def swa_scores(u):
                qt, kv = units[u]
                j = qt + 1
                hs = slice(kv * 4, (kv + 1) * 4)
                dls = [dl for dl in (-1, 0, 1) if (j + dl) in valid]
                pts = []
                for dl in dls:
                    jk = j + dl
                    bi = 4 + swb_rr[0]
                    swb_rr[0] = (swb_rr[0] + 1) % 4
                    ps, bps = banks[bi], bank_b[bi]
                    psv = ps.rearrange("p (h q) -> p h q", h=4)
                    mm(psv, ksT[:, kv, jk * 128:(jk + 1) * 128], qsT[:, hs, qt * 128:(qt + 1) * 128], True, True,
                       [b_ks] + b_qs[kv * 4:(kv + 1) * 4], [bps], inc=True)
                    si = sfp_rr[0]
                    sfp_rr[0] = (si + 1) % 3
                    pi = pt3_rr[0]
                    pt3_rr[0] = (pi + 1) % len(PT3)
                    sv_ = sfp[si].rearrange("p (h q) -> p h q", h=4)
                    stt("dve", sv_, psv, SWA_SCALE, Bm[:, dl + 1, hs, :], ALU.mult, ALU.add, [bps, b_Bm],
                        [b_sfp[si]])
                    act(PT3[pi], sv_, AF.Exp, [b_sfp[si]], [b_PT3[pi]])
                    pts.append((jk, pi))
                sw[u] = pts

            def swa_pv(u):
                qt, kv = units[u]
                hs = slice(kv * 4, (kv + 1) * 4)
                a = (u % 2) * 2
                accO, bO, accD, bD = banks[a], bank_b[a], banks[a + 1], bank_b[a + 1]
                pts = sw.pop(u)
                for n_i, (jk, pi) in enumerate(pts):
                    first = (n_i == 0)
                    last = (n_i == len(pts) - 1)
                    ptf = PT3[pi].rearrange("p h q -> p (h q)")
                    mm(accO, vs[:, jk, kv * 128:(kv + 1) * 128], ptf, first, last, [b_vs[jk], b_PT3[pi]], [bO],
                       inc=False)
                    mm(accD, ones_bf, ptf, first, last, [b_const, b_PT3[pi]], [bD], inc=True)
                dv = dtmp.rearrange("p (h q) -> p h q", h=4)
                tt("dve", dv, accD.rearrange("p (h q) -> p h q", h=4),
                   sinkexp[:, hs].unsqueeze(2).broadcast_to([128, 4, 128]), ALU.add, [bD, b_small], [b_dtmp])
                act_recip(dtmp, dtmp, [b_dtmp], [b_dtmp])
                tt("dve", swaOT[:, hs, qt * 128:(qt + 1) * 128], accO.rearrange("p (h q) -> p h q", h=4), dv,
                   ALU.mult, [bO, b_dtmp], [b_swaO[kv][qt]])

            reserved.update(range(4))
            swa_scores(0)
            for u in range(len(units)):
                if u + 1 < len(units):
                    swa_scores(u + 1)
                swa_pv(u)
            reserved.clear()
            if b == 0:
                dump("swaOT", swaOT, b_swaO[0] + b_swaO[1])
            for c in range(8):
                if True:
                    pg0, bpg0 = wpiece("g0", c)
                    pg1, bpg1 = wpiece("g1", c)
                    pa, bpa = wpiece("oa", c)
                    pb, bpb = wpiece("ob", c)
                    ccs = slice(0, 128)
                    ps_g0, bg0 = nextbank()
                    for k in range(8):
                        mm(ps_g0, pg0[:, k, ccs], hTe[:, k, own], k == 0, k == 7, [bpg0] + b_own, [bg0], inc=(k == 7))
                    ps_g1, bg1 = nextbank()
                    for k in range(8):
                        mm(ps_g1, pg1[:, k, ccs], hTe[:, k, own], k == 0, k == 7, [bpg1] + b_own, [bg1], inc=(k == 7))
                    ps_a, ba = nextbank()
                    for h in range(8):
                        mm(ps_a, pa[:, h, ccs], OT[:, h, b * 512:(b + 1) * 512], h == 0, h == 7, [bpa, b_OT[h][b]],
                           [ba], inc=(h == 7))
                    ps_b, bb = nextbank()
                    for h in range(8):
                        mm(ps_b, pb[:, h, ccs], swaOT[:, h, :], h == 0, h == 7,
                           [bpb] + [b_swaO[h // 4][q] for q in range(4)], [bb], inc=(h == 7))
                    act(sfp[0], ps_g0, AF.Sigmoid, [bg0], [b_sfp[0]])
                    act(sfp[1], ps_g1, AF.Sigmoid, [bg1], [b_sfp[1]])
                    tt("dve", sfp[0], sfp[0], ps_a, ALU.mult, [b_sfp[0], ba], [b_sfp[0]])
                    tt("dve", sfp[1], sfp[1], ps_b, ALU.mult, [b_sfp[1], bb], [b_sfp[1]])
                    tt("dve", mergedT[:, c, :], sfp[0], sfp[1], ALU.add, [b_sfp[0], b_sfp[1]], [b_mg[c]])
            if b == 0:
                dump("mergedT", mergedT, b_mg)
            slot2 = {}
            for tp2 in range(2):
                for cq in range(4):
                    po, bpo = wpiece("wo", cq)
                    cs = slice(cq * 256, (cq + 1) * 256)
                    ps, bps = nextbank()
                    for tl2 in range(2):
                        tl = tp2 * 2 + tl2
                        for k in range(8):
                            mm(ps[:, tl2 * 256:(tl2 + 1) * 256], mergedT[:, k, tl * 128:(tl + 1) * 128], po[:, k, :],
                               k == 0, k == 7, [bpo, b_mg[k]], [bps], inc=(k == 7 and tl2 == 1))
                    for tl2 in range(2):
                        tl = tp2 * 2 + tl2
                        si = tl2
                        tt("dve", sfp[si][:, 0:256], ps[:, tl2 * 256:(tl2 + 1) * 256], G1b[:, cs], ALU.mult,
                           [bps, b_G], [b_sfp[si]])
                        tt("dve", x1[:, tl, cs], sfp[si][:, 0:256], x1[:, tl, cs], ALU.add, [b_sfp[si], b_x1[tl]],
                           [b_x1[tl]])
                if tp2 == 1:
                    for tl in (0, 1):
                        nt_back(slot2[tl], a2, sh2, lambda k, tl=tl: hTe[:, k, tl * 128:(tl + 1) * 128], b_hTe[tl])
                for tl in (tp2 * 2, tp2 * 2 + 1):
                    slot2[tl] = nt_front(x1[:, tl, :], b_x1[tl])
            for tl in (2, 3):
                nt_back(slot2[tl], a2, sh2, lambda k, tl=tl: hTe[:, k, tl * 128:(tl + 1) * 128], b_hTe[tl])
            if b == 0:
                dump("x1", x1.rearrange("p t d -> p (t d)"), b_x1)
            S.alias(b_uT, u_old)
            h2 = slice(0, 512)
            b_h2 = b_hTe[0:4]
            for pc in range(16):
                p1, bp1 = wpiece("f1", pc)
                for cc in range(2):
                    ch = pc * 2 + cc
                    ps, bps = nextbank()
                    for k in range(8):
                        mm(ps, p1[:, k, cc * 128:(cc + 1) * 128], hTe[:, k, h2], k == 0, k == 7, [bp1] + b_h2, [bps],
                           inc=(k == 7))
                    si = ch % 3
                    act(sfp[si], ps, AF.Relu, [bps], [b_sfp[si]])
                    tt("dve", uT[:, ch, :], sfp[si], sfp[si], ALU.mult, [b_sfp[si]], [b_uT[ch]])
            nxt = hext_steps(b + 1) if b + 1 < NB else []
            for hf in range(2):
                cs = slice(hf * 512, (hf + 1) * 512)
                accs = [(banks[i], bank_b[i]) for i in range(4)]
                reserved.update(range(4))
                for g in range(8):
                    p2, bp2 = wpiece("f2", hf, g)
                    for kk in range(4):
                        kc = g * 4 + kk
                        for tl in range(4):
                            mm(accs[tl][0], uT[:, kc, tl * 128:(tl + 1) * 128], p2[:, kk, :], kc == 0, kc == 31,
                               [bp2, b_uT[kc]], [accs[tl][1]], inc=(kc == 31 or (kk == 3 and tl == 3)))
                    if nxt and not (hf == 1 and g == 7):
                        nxt.pop(0)()
                reserved.clear()
                for tl in range(4):
                    si = tl % 3
                    tt("dve", sfp[si], accs[tl][0], G2b[:, cs], ALU.mult, [accs[tl][1], b_G], [b_sfp[si]])
                    tt("dve", x1[:, tl, cs], sfp[si], x1[:, tl, cs], ALU.add, [b_sfp[si], b_x1[tl]], [b_x1[tl]])
            for tl in range(4):
                te = 4 * b + tl
                ss, bss = new_stat()
                act(junk, x1[:, tl, :], AF.Square, [b_x1[tl]], [b_junk, bss], accum=ss)
                rstd_of(ss, D, [bss])
                stt("dve", x1[:, tl, :], x1[:, tl, :], ss, gFb, ALU.mult, ALU.mult, [b_x1[tl], bss, b_G], [b_x1[tl]])
                dma("sp", out_d[te * 128:(te + 1) * 128, :], x1[:, tl, :], [b_x1[tl]], [])
            while nxt:
                nxt.pop(0)()
        S.barrier()
        S.emit()
    return nc


_PROGRAM = None


def _lay(v, k):
    return np.ascontiguousarray(np.asarray(v, dtype=np.float32).reshape(k, 128).T)


def kernel(x, c, positions, w_ada, b_ada, norm_mix, w_in, q_norm, w_uq, kv_norm, w_ukv,
           rel_bias, sink, w_o_mla, w_o_swa, w_out, norm_mlp, w_ff1, w_ff2, norm_final):
    global _PROGRAM
    if _PROGRAM is None:
        _PROGRAM = build_program()
    nc = _PROGRAM
    f = lambda a: np.ascontiguousarray(np.asarray(a, dtype=np.float32))
    consts = _host_consts()
    shared = {
        "w_ada": f(w_ada[0]), "b_ada_l": _lay(b_ada[0], 48), "norm_mix_l": _lay(norm_mix[0], 8),
        "w_in": f(w_in[0]), "q_norm_l": _lay(q_norm[0], 3), "w_uq": f(w_uq[0]),
        "kv_norm_l": _lay(kv_norm[0], 2), "w_ukv": f(w_ukv[0]), "rel_bias": f(rel_bias),
        "sink": f(sink[0]), "w_o_mla": f(w_o_mla[0]), "w_o_swa": f(w_o_swa[0]), "w_out": f(w_out[0]),
        "norm_mlp_l": _lay(norm_mlp[0], 8), "w_ff1": f(w_ff1[0]), "w_ff2": f(w_ff2[0]),
        "norm_final_l": _lay(norm_final, 8),
    }
    shared.update(consts)
    x = np.asarray(x, dtype=np.float32)
    c = np.asarray(c, dtype=np.float32)
    positions = np.asarray(positions, dtype=np.int32)
    in_maps = []
    for b in range(N_CORES):
        m = dict(shared)
        m["x"] = np.ascontiguousarray(x[b])
        m["c_l"] = _lay(c[b], 8)
        m["pos_l"] = np.ascontiguousarray(positions[b].reshape(NT, 128).T)
        in_maps.append(m)
    res = run_bass_kernel_spmd(nc, in_maps, core_ids=list(range(N_CORES)))
    kernel.last_results = res
    return np.stack([np.asarray(r["out"], dtype=np.float32) for r in res.results], axis=0)
```

```python
import math
import contextlib
import numpy as np
import concourse.bass as bass
import concourse.mybir as mybir
from concourse.bass_utils import run_bass_kernel_spmd

F32 = mybir.dt.float32
BF16 = mybir.dt.bfloat16
I32 = mybir.dt.int32
AF = mybir.ActivationFunctionType
ALU = mybir.AluOpType

D = 1024
SEQ = 4096
NT = SEQ // 128
NB = SEQ // 512
D_IN = 4288
EPS = 1e-6
MLA_SCALE = 192 ** -0.5
SWA_SCALE = 128 ** -0.5
NEG = -30000.0
TWO_PI = 2.0 * math.pi
C1 = 6.28125
C2 = TWO_PI - C1
PI_SAFE = 3.1415925
N_CORES = 8

DEBUG = {}


class Buf:
    __slots__ = ("name", "w", "r", "psum")

    def __init__(self, name="", psum=False):
        self.name = name
        self.w = None
        self.r = {}
        self.psum = psum


class Sched:
    ENGS = ("pe", "act", "dve", "pool", "sp")

    def __init__(self, nc, n_dma_sems=48):
        self.nc = nc
        self.ops = {e: [] for e in self.ENGS}
        self.cnt = {e: 0 for e in self.ENGS}
        self.known = {e: {} for e in self.ENGS}
        self.n_dma = n_dma_sems
        self.dma_cnt = [0] * n_dma_sems
        half = n_dma_sems // 2
        self.dma_pools = {"sp": list(range(0, half)), "pool": list(range(half, n_dma_sems))}
        self.dma_rr = {"sp": 0, "pool": 0}
        self.sems = {}

    def _need(self, X, waits, key, val):
        if self.known[X].get(key, 0) >= val:
            return
        if waits.get(key, 0) < val:
            waits[key] = val

    @staticmethod
    def _flat(bs):
        out = []
        for b in bs:
            if isinstance(b, (list, tuple)):
                out.extend(Sched._flat(b))
            else:
                out.append(b)
        return out

    def op(self, eng, fn, reads=(), writes=(), inc=True, dma=False):
        X = eng
        reads = self._flat(reads)
        writes = self._flat(writes)
        waits = {}
        for b in reads:
            if b.psum:
                for k, (v, e) in b.r.items():
                    if e != X:
                        self._need(X, waits, k, v)
            if b.w is not None:
                k, v, e = b.w
                if e == X and k == X and X == "pe":
                    continue
                self._need(X, waits, k, v)
        for b in writes:
            if b.w is not None:
                k, v, e = b.w
                if not (e == X and k == X):
                    self._need(X, waits, k, v)
            for k, (v, e) in b.r.items():
                if e == X and k == X:
                    continue
                self._need(X, waits, k, v)
        if dma:
            pl = "pool" if X == "pool" else "sp"
            lst = self.dma_pools[pl]
            i = lst[self.dma_rr[pl]]
            self.dma_rr[pl] = (self.dma_rr[pl] + 1) % len(lst)
            key = "d%d" % i
            if self.dma_cnt[i] > 0:
                self._need(X, waits, key, 16 * self.dma_cnt[i])
            self.dma_cnt[i] += 1
            tok = (key, 16 * self.dma_cnt[i], X)
            incspec = (key, 16)
        else:
            if inc:
                self.cnt[X] += 1
                tok = (X, self.cnt[X], X)
                incspec = (X, 1)
            else:
                tok = (X, self.cnt[X] + 1, X)
                incspec = None
        for k, v in waits.items():
            self.known[X][k] = v
        self.ops[X].append((tuple(waits.items()), fn, incspec))
        k, v, e = tok
        for b in reads:
            old = b.r.get(k)
            if old is None or old[0] < v:
                b.r[k] = (v, e)
        for b in writes:
            b.w = tok
            b.r = {}
        return tok

    def alias(self, news, olds):
        acc = {}
        for o in olds:
            for k, (v, e) in o.r.items():
                if k not in acc or acc[k][0] < v:
                    acc[k] = (v, "?")
            if o.w is not None:
                k, v, e = o.w
                if k not in acc or acc[k][0] < v:
                    acc[k] = (v, "?")
        for n in news:
            for k, (v, e) in acc.items():
                old = n.r.get(k)
                if old is None or old[0] < v:
                    n.r[k] = (v, e)

    def barrier(self):
        for X in self.ENGS:
            waits = {}
            for i in range(self.n_dma):
                if self.dma_cnt[i] > 0:
                    self._need(X, waits, "d%d" % i, 16 * self.dma_cnt[i])
            for e in self.ENGS:
                if e != X and self.cnt[e] > 0:
                    self._need(X, waits, e, self.cnt[e])
            for k, v in waits.items():
                self.known[X][k] = v
            self.ops[X].append((tuple(waits.items()), None, None))

    def emit(self):
        nc = self.nc
        with contextlib.ExitStack() as st:
            keys = list(self.ENGS) + ["d%d" % i for i in range(self.n_dma)]
            for k in keys:
                self.sems[k] = st.enter_context(nc.semaphore("s_" + k))
            block = st.enter_context(nc.Block())
            sems = self.sems

            def run(e, engobj):
                for waits, fn, incspec in self.ops[e]:
                    for k, v in waits:
                        engobj.wait_ge(sems[k], v)
                    if fn is None:
                        continue
                    ins = fn(engobj)
                    if incspec is not None:
                        ins.then_inc(sems[incspec[0]], incspec[1])

            @block.tensor
            def _(t):
                run("pe", t)

            @block.scalar
            def _(s):
                run("act", s)

            @block.vector
            def _(v):
                run("dve", v)

            @block.gpsimd
            def _(g):
                run("pool", g)

            @block.sync
            def _(s):
                run("sp", s)


def _t5_bucket_np(rel):
    rel = np.asarray(rel, dtype=np.int32)
    half, max_exact = 16, 8
    ret = np.where(rel > 0, half, 0)
    n = np.abs(rel)
    nf = np.maximum(n, 1).astype(np.float32)
    large = max_exact + (np.log(nf / np.float32(max_exact)) / np.float32(math.log(128 / max_exact))
                         * np.float32(half - max_exact)).astype(np.int32)
    large = np.minimum(large, half - 1)
    return ret + np.where(n < max_exact, n, large)


_BUCKET_FIX = {16: 10, 32: 12, 64: 14, 128: 15}


def _host_consts():
    ident = np.eye(128, dtype=np.float32)
    J = np.ascontiguousarray(ident[::-1])
    inv = (10000.0 ** (-np.arange(0, 64, 2, dtype=np.float32) / np.float32(64))).astype(np.float32)
    invt = np.ascontiguousarray(np.broadcast_to(inv[None, :], (128, 32))).astype(np.float32)
    oht = np.zeros((33, 512), dtype=np.float32)
    for m in range(512):
        rel = m - 255
        if abs(rel) <= 128:
            n = abs(rel)
            bk = int(_t5_bucket_np(rel))
            if n in _BUCKET_FIX:
                bk = _BUCKET_FIX[n] + (16 if rel > 0 else 0)
            oht[bk, m] = 1.0
        else:
            oht[32, m] = 1.0
    return {"ident": ident, "jrev": J, "invt": invt, "oht": oht}


ARENA_BYTES = 211968
EXTRA = 207872


def build_program():
    nc = bass.Bass("TRN2", target_bir_lowering=False)
    S = Sched(nc, n_dma_sems=48)

    def dram(name, shape, dt, kind="ExternalInput"):
        return nc.dram_tensor(name, shape, dt, kind=kind)

    x_d = dram("x", [SEQ, D], F32).ap()
    c_d = dram("c_l", [128, 8], F32).ap()
    pos_d = dram("pos_l", [128, NT], I32).ap()
    wada_d = dram("w_ada", [D, 6 * D], F32).ap()
    bada_d = dram("b_ada_l", [128, 48], F32).ap()
    nmix_d = dram("norm_mix_l", [128, 8], F32).ap()
    win_d = dram("w_in", [D, D_IN], F32).ap()
    qn_d = dram("q_norm_l", [128, 3], F32).ap()
    wuq_d = dram("w_uq", [384, 1536], F32).ap()
    kvn_d = dram("kv_norm_l", [128, 2], F32).ap()
    wukv_d = dram("w_ukv", [256, 2048], F32).ap()
    rb_d = dram("rel_bias", [32, 8], F32).ap()
    sink_d = dram("sink", [8], F32).ap()
    womla_d = dram("w_o_mla", [D, D], F32).ap()
    woswa_d = dram("w_o_swa", [D, D], F32).ap()
    wout_d = dram("w_out", [D, D], F32).ap()
    nmlp_d = dram("norm_mlp_l", [128, 8], F32).ap()
    wff1_d = dram("w_ff1", [D, 4 * D], F32).ap()
    wff2_d = dram("w_ff2", [4 * D, D], F32).ap()
    nfin_d = dram("norm_final_l", [128, 8], F32).ap()
    ident_d = dram("ident", [128, 128], F32).ap()
    jrev_d = dram("jrev", [128, 128], F32).ap()
    invt_d = dram("invt", [128, 32], F32).ap()
    oht_d = dram("oht", [33, 512], F32).ap()
    tbl_t = dram("tbl_scratch", [8, 512], F32, kind="Internal")
    NPIECE = 74
    wsc = dram("wsc", [NPIECE, 128, 2048], BF16, kind="Internal").ap()
    out_d = dram("out", [SEQ, D], F32, kind="ExternalOutput").ap()
    dbg_d = {}
    for name, shape in DEBUG.items():
        dbg_d[name] = dram("dbg_" + name, list(shape), F32, kind="ExternalOutput").ap()

    st = contextlib.ExitStack()
    with st:
        arena = st.enter_context(nc.sbuf_tensor("arena", [128, ARENA_BYTES // 2], BF16))
        banks = [st.enter_context(nc.psum_tensor("bank%d" % i, [128, 512], F32))[:, :] for i in range(8)]
        bank_b = [Buf("bank%d" % i, psum=True) for i in range(8)]

        def V(off, dt, shape, p0=0, p1=128):
            n = int(np.prod(shape))
            esz = 2 if dt == BF16 else 4
            assert off % 4 == 0 and off + n * esz <= ARENA_BYTES, (off, n, esz)
            v = arena[p0:p1, off // 2:(off + n * esz) // 2]
            if dt != BF16:
                v = v.bitcast(dt)
            if len(shape) == 2:
                v = v.rearrange("p (a b) -> p a b", a=shape[0])
            elif len(shape) == 3:
                v = v.rearrange("p (a b c) -> p a b c", a=shape[0], b=shape[1])
            elif len(shape) == 4:
                v = v.rearrange("p (a b c d) -> p a b c d", a=shape[0], b=shape[1], c=shape[2])
            return v

        def mm(out, lhsT, rhs, start, stop, R, W, inc):
            S.op("pe", lambda e: e.matmul(out, lhsT=lhsT, rhs=rhs, start=start, stop=stop),
                 reads=R, writes=W, inc=inc)

        def tp(out, in_, R, W, inc):
            S.op("pe", lambda e: e.transpose(out=out, in_=in_, identity=ident_f), reads=R + [b_const],
                 writes=W, inc=inc)

        def act(out, in_, func, R, W, bias=None, scale=None, accum=None):
            kw = {}
            if bias is not None:
                kw["bias"] = bias
            if scale is not None:
                kw["scale"] = scale
            if accum is not None:
                kw["accum_out"] = accum
            S.op("act", lambda e: e.activation(out=out, in_=in_, func=func, **kw), reads=R, writes=W)

        def ts(eng, out, in0, s1, s2, op0, op1, R, W):
            if op1 is None:
                S.op(eng, lambda e: e.tensor_scalar(out=out, in0=in0, scalar1=s1, scalar2=None, op0=op0),
                     reads=R, writes=W)
            else:
                S.op(eng, lambda e: e.tensor_scalar(out=out, in0=in0, scalar1=s1, scalar2=s2, op0=op0, op1=op1),
                     reads=R, writes=W)

        def tt(eng, out, in0, in1, op, R, W):
            S.op(eng, lambda e: e.tensor_tensor(out=out, in0=in0, in1=in1, op=op), reads=R, writes=W)

        def stt(eng, out, in0, scalar, in1, op0, op1, R, W):
            S.op(eng, lambda e: e.scalar_tensor_tensor(out=out, in0=in0, scalar=scalar, in1=in1, op0=op0, op1=op1),
                 reads=R, writes=W)

        def cp(eng, out, in_, R, W):
            if eng == "act":
                S.op("act", lambda e: e.copy(out=out, in_=in_), reads=R, writes=W)
            else:
                S.op(eng, lambda e: e.tensor_copy(out=out, in_=in_), reads=R, writes=W)

        def recip(out, in_, R, W):
            S.op("dve", lambda e: e.reciprocal(out=out, in_=in_), reads=R, writes=W)

        def dma(eng, out, in_, R, W):
            S.op(eng, lambda e: e.dma_start(out=out, in_=in_), reads=R, writes=W, dma=True)

        def memset(eng, ap, val, W):
            S.op(eng, lambda e: e.memset(ap, val), writes=W)

        bank_rr = [0]

        reserved = set()

        def nextbank():
            while True:
                i = bank_rr[0]
                bank_rr[0] = (i + 1) % 8
                if i not in reserved:
                    return banks[i], bank_b[i]

        def dump(name, ap, R):
            if name in dbg_d:
                dma("sp", dbg_d[name], ap, R, [])

        ident_f = V(0, F32, [128])
        ones_f = V(512, F32, [128])
        ones_bf = V(1024, BF16, [128])
        jrev_f = V(1280, F32, [128])
        b_const = Buf("const")
        SV = 2048

        def sv(i, n):
            return V(SV + 4 * i, F32, [n])

        cT = sv(0, 8); cexp = sv(8, 8); cact2 = V(SV + 64, F32, [8, 2])
        modT = sv(32, 48); badaT = sv(80, 48)
        nmix = sv(128, 8); nmlp = sv(136, 8); nfin = sv(144, 8)
        a1 = sv(152, 8); a2 = sv(160, 8)
        qn = sv(168, 3); kvn = sv(172, 2)
        sinkexp = sv(176, 8)
        pos_f = sv(184, 32)
        pos_i = V(SV + 4 * 216, I32, [32])
        stat = sv(248, 16)
        invt = sv(264, 32)
        cos_t = V(4096, F32, [32, 32])
        sin_t = V(8192, F32, [32, 32])
        rb_aug = V(3328, F32, [8], 0, 33)
        tbl_sb = V(12288, F32, [512], 0, 8)
        oht = V(14336, F32, [512], 0, 33)
        b_small = Buf("small")
        b_trig = Buf("trig")
        b_mod = Buf("mod")

        OT_OFF = 16384
        OT = V(OT_OFF, BF16, [8, SEQ])
        b_OT = [[Buf("OT%d_%d" % (h, q)) for q in range(NB)] for h in range(8)]
        P1O = 81920
        cqnT = V(P1O, BF16, [3, SEQ])
        ckvnT = V(P1O + 24576, BF16, [2, SEQ])
        KrT = V(P1O + 40960, BF16, [SEQ])
        b_cqn = [Buf("cqn%d" % t) for t in range(NT)]
        b_ckvn = [Buf("ckvn%d" % t) for t in range(NT)]
        b_kr = [Buf("kr%d" % t) for t in range(NT)]
        TR = 131072

        dma("sp", ident_f, ident_d, [], [b_const])
        dma("sp", jrev_f, jrev_d, [], [b_const])
        dma("sp", invt, invt_d, [], [b_small])
        dma("sp", oht, oht_d, [], [b_small])
        dma("sp", cT, c_d, [], [b_small])
        dma("sp", badaT, bada_d, [], [b_small])
        dma("sp", nmix, nmix_d, [], [b_small])
        dma("sp", nmlp, nmlp_d, [], [b_small])
        dma("sp", nfin, nfin_d, [], [b_small])
        dma("sp", qn, qn_d, [], [b_small])
        dma("sp", kvn, kvn_d, [], [b_small])
        dma("sp", pos_i, pos_d, [], [b_small])
        dma("sp", sinkexp, sink_d.partition_broadcast(128), [], [b_small])
        dma("sp", rb_aug[0:32, :], rb_d, [], [b_small])
        memset("dve", rb_aug[32:33, :], NEG, [b_small])
        memset("dve", ones_f, 1.0, [b_const])
        memset("dve", ones_bf, 1.0, [b_const])

        act(cexp, cT, AF.Exp, [b_small], [b_small], scale=-1.0)
        ts("dve", cexp, cexp, 1.0, None, ALU.add, None, [b_small], [b_small])
        recip(cexp, cexp, [b_small], [b_small])
        tt("dve", cact2[:, :, 0], cT, cexp, ALU.mult, [b_small], [b_small])
        tt("dve", cact2[:, :, 1], cT, cexp, ALU.mult, [b_small], [b_small])
        act(sinkexp, sinkexp, AF.Exp, [b_small], [b_small])

        tg = [V(TR + 32768 + 4096 * i, F32, [32, 32]) for i in range(4)]
        tgi = V(TR + 32768 + 4096 * 4, I32, [32, 32])
        b_tg = Buf("tg")
        cp("dve", pos_f, pos_i, [b_small], [b_small])
        ang, nf, rr, mk = tg
        tt("dve", ang, pos_f.unsqueeze(2).broadcast_to([128, 32, 32]),
           invt.unsqueeze(1).broadcast_to([128, 32, 32]), ALU.mult, [b_small], [b_tg])
        ts("dve", tgi, ang, 1.0 / TWO_PI, None, ALU.mult, None, [b_tg], [b_tg])
        cp("dve", nf, tgi, [b_tg], [b_tg])
        stt("dve", rr, nf, -C1, ang, ALU.mult, ALU.add, [b_tg], [b_tg])
        stt("dve", rr, nf, -C2, rr, ALU.mult, ALU.add, [b_tg], [b_tg])

        def wrap(r):
            ts("dve", mk, r, math.pi, TWO_PI, ALU.is_gt, ALU.mult, [b_tg], [b_tg])
            tt("dve", r, r, mk, ALU.subtract, [b_tg], [b_tg])
            ts("dve", mk, r, -math.pi, TWO_PI, ALU.is_lt, ALU.mult, [b_tg], [b_tg])
            tt("dve", r, r, mk, ALU.add, [b_tg], [b_tg])
            ts("dve", r, r, -PI_SAFE, PI_SAFE, ALU.max, ALU.min, [b_tg], [b_tg])

        wrap(rr)
        act(sin_t, rr, AF.Sin, [b_tg], [b_trig])
        ts("dve", rr, rr, math.pi / 2, None, ALU.add, None, [b_tg], [b_tg])
        wrap(rr)
        act(cos_t, rr, AF.Sin, [b_tg], [b_trig])

        stg = [V(TR + 16384 * i, F32, [8, 512]) for i in range(2)]
        b_stg = [Buf("stg0"), Buf("stg1")]
        psM, b_psM = nextbank()
        psMv = psM[:, 0:96].rearrange("p (a b) -> p a b", b=2)
        for pc in range(12):
            sl = pc % 2
            dma("sp", stg[sl], wada_d[:, pc * 512:(pc + 1) * 512].rearrange("(k p) n -> p k n", p=128),
                [], [b_stg[sl]])
            for j in range(4):
                cc = pc * 4 + j
                for k in range(8):
                    mm(psMv[:, cc, :], stg[sl][:, k, j * 128:(j + 1) * 128], cact2[:, k, :], k == 0, k == 7,
                       [b_stg[sl], b_small], [b_psM], inc=(k == 7))
        tt("dve", modT, psMv[:, :, 0], badaT, ALU.add, [b_psM, b_small], [b_mod])
        stt("dve", a1, modT[:, 8:16], 1.0, nmix, ALU.add, ALU.mult, [b_mod, b_small], [b_mod])
        stt("dve", a2, modT[:, 32:40], 1.0, nmlp, ALU.add, ALU.mult, [b_mod, b_small], [b_mod])
        sh1 = modT[:, 0:8]; g1v = modT[:, 16:24]; sh2 = modT[:, 24:32]; g2v = modT[:, 40:48]
        S.barrier()

        stat_rr = [0]

        def rstd_of(ss_ap, n, R):
            act(ss_ap, ss_ap, AF.Ln, R, R, bias=EPS, scale=1.0 / n)
            act(ss_ap, ss_ap, AF.Exp, R, R, scale=-0.5)
            return ss_ap

        stat_bufs = [Buf("stat%d" % i) for i in range(16)]

        def new_stat():
            i = stat_rr[0]
            stat_rr[0] = (i + 1) % 16
            return stat[:, i:i + 1], stat_bufs[i]

        w704 = V(TR, BF16, [8, 704]); b_w704 = Buf("w704")
        xt = [V(TR + 11264 + 4096 * i, F32, [1024]) for i in range(2)]; b_xt = [Buf(), Buf()]
        xn = [V(TR + 19456 + 4096 * i, F32, [1024]) for i in range(2)]; b_xn = [Buf(), Buf()]
        hT = V(TR + 27648, BF16, [8, 512]); b_hT = [[Buf() for _ in range(8)] for _ in range(4)]
        junk = V(TR + 35840, BF16, [1024]); b_junk = Buf("junk")
        cqkv = [V(TR + 37888 + 2560 * i, F32, [640]) for i in range(2)]; b_cqkv = [Buf(), Buf()]
        krr = [V(TR + 43008 + 512 * i, F32, [128]) for i in range(2)]; b_krr = [Buf(), Buf()]
        rtmp = [V(TR + 44032 + 128 * i, F32, [32]) for i in range(4)]; b_rtmp = Buf("rtmp")

        dma("pool", w704, win_d[:, 0:704].rearrange("(k p) n -> p k n", p=128), [], [b_w704])

        pspec = {}
        plist = []

        def addp(key, src2d, kch, ncols):
            pspec[key] = (len(plist), kch, ncols)
            plist.append((key, src2d, kch, ncols))

        addp(("ks",), win_d[:, 1728:1984], 8, 256)
        addp(("vs",), win_d[:, 1984:2240], 8, 256)
        for pc in range(4):
            addp(("qs", pc), win_d[:, 704 + pc * 256:704 + (pc + 1) * 256], 8, 256)
        for c in range(8):
            addp(("g0", c), win_d[:, 2240 + c * 128:2240 + (c + 1) * 128], 8, 128)
            addp(("g1", c), win_d[:, 3264 + c * 128:3264 + (c + 1) * 128], 8, 128)
            addp(("oa", c), womla_d[:, c * 128:(c + 1) * 128], 8, 128)
            addp(("ob", c), woswa_d[:, c * 128:(c + 1) * 128], 8, 128)
        for cq in range(4):
            addp(("wo", cq), wout_d[:, cq * 256:(cq + 1) * 256], 8, 256)
        for pc in range(16):
            addp(("f1", pc), wff1_d[:, pc * 256:(pc + 1) * 256], 8, 256)
        for hf in range(2):
            for g in range(8):
                addp(("f2", hf, g), wff2_d[g * 512:(g + 1) * 512, hf * 512:(hf + 1) * 512], 4, 512)
        assert len(plist) == NPIECE
        b_wsc = [Buf("wsc%d" % i) for i in range(NPIECE)]
        pre_rr = [0]

        def precast_some(n):
            for _ in range(n):
                i = pre_rr[0]
                if i >= NPIECE:
                    return
                pre_rr[0] = i + 1
                key, src2d, kch, ncols = plist[i]
                dma("pool", wsc[i][:, 0:kch * ncols].rearrange("p (k n) -> p k n", k=kch),
                    src2d.rearrange("(k p) n -> p k n", p=128), [], [b_wsc[i]])

        def make_nt(ring, b_ring, fixed_banks=None):
            rr = [0]

            def front(src, bsrc, load=None):
                i = rr[0]
                rr[0] = (i + 1) % len(ring)
                if load is not None:
                    dma("sp", ring[i], load, [], [b_ring[i]])
                    src, bsrc = ring[i], b_ring[i]
                ss, bss = new_stat()
                act(junk, src, AF.Square, [bsrc], [b_junk, bss], accum=ss)
                rstd_of(ss, D, [bss])
                ts("dve", ring[i], src, ss, None, ALU.mult, None, [bsrc, bss], [b_ring[i]])
                return i

            def back(i, avec, shvec, dst_fn, bdst):
                for hf in range(2):
                    if fixed_banks is None:
                        ps, bps = nextbank()
                    else:
                        ps, bps = banks[fixed_banks[hf]], bank_b[fixed_banks[hf]]
                    for j in range(4):
                        k = hf * 4 + j
                        tp(ps[:, j * 128:(j + 1) * 128], ring[i][:, k * 128:(k + 1) * 128], [b_ring[i]], [bps],
                           inc=(j == 3))
                    for j in range(4):
                        k = hf * 4 + j
                        bd = bdst[k] if isinstance(bdst, list) else bdst
                        if hf == 0:
                            act(dst_fn(k), ps[:, j * 128:(j + 1) * 128], AF.Identity, [bps, b_mod], [bd],
                                bias=shvec[:, k:k + 1], scale=avec[:, k:k + 1])
                        else:
                            ts("dve", dst_fn(k), ps[:, j * 128:(j + 1) * 128], avec[:, k:k + 1], shvec[:, k:k + 1],
                               ALU.mult, ALU.add, [bps, b_mod], [bd])

            return front, back

        p1_front, p1_back = make_nt([xt[0], xt[1], xn[0], xn[1]], [b_xt[0], b_xt[1], b_xn[0], b_xn[1]], fixed_banks=(0, 1))
        p1_slot = {}
        p1_ps = {}

        def p1_A(t):
            p1_slot[t] = p1_front(None, None, load=x_d[t * 128:(t + 1) * 128, :])

        def p1_B(t):
            tl = t % 4
            p1_back(p1_slot.pop(t), a1, sh1, lambda k, tl=tl: hT[:, k, tl * 128:(tl + 1) * 128], b_hT[tl])
            ia = 2 + 2 * (t % 2)
            psA, bA, psB, bB = banks[ia], bank_b[ia], banks[ia + 1], bank_b[ia + 1]
            for k in range(8):
                mm(psA[:, 0:384], hT[:, k, tl * 128:(tl + 1) * 128], w704[:, k, 0:384], k == 0, k == 7,
                   [b_hT[tl], b_w704], [bA], inc=(k == 7))
            for k in range(8):
                mm(psB[:, 0:320], hT[:, k, tl * 128:(tl + 1) * 128], w704[:, k, 384:704], k == 0, k == 7,
                   [b_hT[tl], b_w704], [bB], inc=(k == 7))
            p1_ps[t] = (psA, bA, psB, bB)

        def p1_C(t):
            sl = t % 2
            psA, bA, psB, bB = p1_ps.pop(t)
            ssq, bq = new_stat()
            act(junk[:, 0:384], psA[:, 0:384], AF.Square, [bA], [b_junk, bq], accum=ssq)
            rstd_of(ssq, 384, [bq])
            sskv, bkv = new_stat()
            act(junk[:, 0:256], psB[:, 0:256], AF.Square, [bB], [b_junk, bkv], accum=sskv)
            rstd_of(sskv, 256, [bkv])
            ts("dve", cqkv[sl][:, 0:384], psA[:, 0:384], ssq, None, ALU.mult, None, [bA, bq], [b_cqkv[sl]])
            ts("dve", cqkv[sl][:, 384:640], psB[:, 0:256], sskv, None, ALU.mult, None, [bB, bkv], [b_cqkv[sl]])
            x1_ = psB[:, 256:288]; x2_ = psB[:, 288:320]
            ct = cos_t[:, t, :]; sn = sin_t[:, t, :]
            tt("dve", rtmp[0], x1_, ct, ALU.mult, [bB, b_trig], [b_rtmp])
            tt("dve", rtmp[1], x2_, sn, ALU.mult, [bB, b_trig], [b_rtmp])
            tt("dve", rtmp[2], x2_, ct, ALU.mult, [bB, b_trig], [b_rtmp])
            tt("dve", rtmp[3], x1_, sn, ALU.mult, [bB, b_trig], [b_rtmp])
            tt("dve", krr[sl][:, 0:32], rtmp[0], rtmp[1], ALU.subtract, [b_rtmp], [b_krr[sl]])
            tt("dve", krr[sl][:, 32:64], rtmp[2], rtmp[3], ALU.add, [b_rtmp], [b_krr[sl]])
            cp("dve", krr[sl][:, 64:128], krr[sl][:, 0:64], [b_krr[sl]], [b_krr[sl]])
            psT, bT = banks[6], bank_b[6]
            for j in range(3):
                tp(psT[:, j * 128:(j + 1) * 128], cqkv[sl][:, j * 128:(j + 1) * 128], [b_cqkv[sl]], [bT], inc=(j == 2))
            psU, bU = banks[7], bank_b[7]
            for j in range(2):
                tp(psU[:, j * 128:(j + 1) * 128], cqkv[sl][:, 384 + j * 128:384 + (j + 1) * 128], [b_cqkv[sl]], [bU],
                   inc=False)
            tp(psU[:, 256:384], krr[sl], [b_krr[sl]], [bU], inc=True)
            tok = slice(t * 128, (t + 1) * 128)
            for j in range(3):
                ts("dve", cqnT[:, j, tok], psT[:, j * 128:(j + 1) * 128], qn[:, j:j + 1], None, ALU.mult, None,
                   [bT, b_small], [b_cqn[t]])
            for j in range(2):
                act(ckvnT[:, j, tok], psU[:, j * 128:(j + 1) * 128], AF.Identity, [bU, b_small], [b_ckvn[t]],
                    scale=kvn[:, j:j + 1])
            cp("act", KrT[:, tok], psU[:, 256:384], [bU], [b_kr[t]])

        p1_A(0)
        p1_A(1)
        for n in range(1, NT + 2):
            if 0 <= n - 1 < NT:
                p1_B(n - 1)
            if 0 <= n - 2 < NT:
                p1_C(n - 2)
            if n + 1 < NT:
                p1_A(n + 1)
        dump("cqnT", cqnT[:, :, 0:512], b_cqn[0:4])
        dump("ckvnT", ckvnT[:, :, 0:512], b_ckvn[0:4])
        dump("KrT", KrT[:, 0:512], b_kr[0:4])
        S.barrier()

        KT = [V(TR + 8192 * i, BF16, [SEQ]) for i in range(2)]
        Vt = [V(TR + 16384 + 8192 * i, BF16, [NT, 128]) for i in range(2)]
        QTn = [V(TR + 32768 + 8192 * i, BF16, [SEQ]) for i in range(2)]
        QTr = V(TR + 49152, BF16, [SEQ])
        ropeT = [V(TR + 57344 + 2048 * i, F32, [4, 2, 32]) for i in range(4)]
        b_KT = [[Buf() for _ in range(NB)] for _ in range(2)]
        b_V = [[Buf() for _ in range(NB)] for _ in range(2)]
        b_QTn = [[Buf() for _ in range(NB)] for _ in range(2)]
        b_QTr = [[Buf() for _ in range(NB)] for _ in range(2)]
        b_ropeT = Buf("ropeT")
        wqn = [V(TR + 65536 + 1792 * i, BF16, [3, 128]) for i in range(2)]
        wk = [V(TR + 65536 + 1792 * i + 768, BF16, [2, 128]) for i in range(2)]
        wv = [V(TR + 65536 + 1792 * i + 1280, BF16, [2, 128]) for i in range(2)]
        b_wh = [Buf(), Buf()]
        wqr = V(TR + 69120, BF16, [3, 2, 64]); b_wqr = Buf("wqr")
        qrot = V(TR + 69120 + 768, F32, [4, 2, 2, 32])
        PT = [V(TR + 70656 + 1024 * i, BF16, [512]) for i in range(4)] + [V(TR + 57344 + 7168, BF16, [512])]
        b_PT = [Buf() for _ in range(5)]
        recs = [V(TR + 74752, F32, [512])] * 2
        b_recs = [Buf("rec")] * 2
        sc_rr = [0]
        ropeT = [V(TR + 57344 + 1024 * i, F32, [4, 2, 32]) for i in range(3)]
        qrot = V(TR + 57344 + 3072, F32, [4, 2, 2, 32])
        b_qrot = Buf("qrot")
        QTrz = [V(TR + 57344 + 5120 + 1024 * i, BF16, [512]) for i in range(2)]
        b_QTrz = [Buf(), Buf()]
        pt_rr = [0]
        ev_rr = [0]
        dacc = [V(EXTRA + 2048 * i, F32, [512]) for i in range(2)]; b_dacc = [Buf(), Buf()]
        dacc_rr = [0]

        def evac(out, in_, R, W):
            ev_rr[0] += 1
            cp("dve", out, in_, R, W)

        def load_head_weights(h):
            sl = h % 2
            dma("pool", wqn[sl], wuq_d[:, h * 192:h * 192 + 128].rearrange("(k p) n -> p k n", p=128), [], [b_wh[sl]])
            dma("pool", wk[sl], wukv_d[:, h * 256:h * 256 + 128].rearrange("(k p) n -> p k n", p=128), [], [b_wh[sl]])
            dma("pool", wv[sl], wukv_d[:, h * 256 + 128:h * 256 + 256].rearrange("(k p) n -> p k n", p=128), [],
                [b_wh[sl]])
            dma("pool", wqr[:, :, sl, :],
                wuq_d[:, h * 192 + 128:h * 192 + 192].rearrange("(k p) n -> p k n", p=128), [], [b_wh[sl]])

        def prod_steps(h, bank):
            sl = h % 2
            steps = []

            def getbank():
                if bank is None:
                    return nextbank()
                return banks[bank], bank_b[bank]

            def mk_q(g):
                def f():
                    cols = slice(g * 512, (g + 1) * 512)
                    ps, bps = getbank()
                    for kc in range(3):
                        mm(ps, wqn[sl][:, kc, :], cqnT[:, kc, cols], kc == 0, kc == 2,
                           [b_wh[sl]] + b_cqn[4 * g:4 * g + 4], [bps], inc=(kc == 2))
                    evac(QTn[sl][:, cols], ps, [bps], [b_QTn[sl][g]])
                return f

            def mk_k(g):
                def f():
                    cols = slice(g * 512, (g + 1) * 512)
                    ps, bps = getbank()
                    for kc in range(2):
                        mm(ps, wk[sl][:, kc, :], ckvnT[:, kc, cols], kc == 0, kc == 1,
                           [b_wh[sl]] + b_ckvn[4 * g:4 * g + 4], [bps], inc=(kc == 1))
                    evac(KT[sl][:, cols], ps, [bps], [b_KT[sl][g]])
                return f

            def mk_v(g):
                def f():
                    ps, bps = getbank()
                    for tl in range(4):
                        t = 4 * g + tl
                        for kc in range(2):
                            mm(ps[:, tl * 128:(tl + 1) * 128], ckvnT[:, kc, t * 128:(t + 1) * 128], wv[sl][:, kc, :],
                               kc == 0, kc == 1, [b_wh[sl], b_ckvn[t]], [bps], inc=(kc == 1 and tl == 3))
                    evac(Vt[sl][:, 4 * g:4 * g + 4, :].rearrange("p t d -> p (t d)"), ps, [bps], [b_V[sl][g]])
                return f

            e_ = h % 2

            def mk_ra(g):
                def f():
                    ps, bps = getbank()
                    for tl in range(4):
                        t = 4 * g + tl
                        for kc in range(3):
                            mm(ps[:, tl * 64:(tl + 1) * 64], cqnT[:, kc, t * 128:(t + 1) * 128], wqr[:, kc, e_, :],
                               kc == 0, kc == 2, [b_cqn[t], b_wh[sl]], [bps], inc=(kc == 2 and tl == 3))
                    psv = ps[:, 0:256].rearrange("p (t f i) -> p t f i", t=4, f=2)
                    cb = cos_t[:, 4 * g:4 * g + 4, :]
                    sb_ = sin_t[:, 4 * g:4 * g + 4, :]
                    x1_ = psv[:, :, 0, :]; x2_ = psv[:, :, 1, :]
                    r0 = ropeT[0][:, :, 0, :]; r1 = ropeT[1][:, :, 0, :]
                    tt("dve", r0, x1_, cb, ALU.mult, [bps, b_trig], [b_ropeT])
                    tt("dve", r1, x2_, sb_, ALU.mult, [bps, b_trig], [b_ropeT])
                    tt("dve", qrot[:, :, e_, 0, :], r0, r1, ALU.subtract, [b_ropeT], [b_qrot])
                    tt("dve", r0, x2_, cb, ALU.mult, [bps, b_trig], [b_ropeT])
                    tt("dve", r1, x1_, sb_, ALU.mult, [bps, b_trig], [b_ropeT])
                    tt("dve", qrot[:, :, e_, 1, :], r0, r1, ALU.add, [b_ropeT], [b_qrot])
                return f

            def mk_rb(g):
                def f():
                    ps2, bps2 = getbank()
                    qflat = qrot.rearrange("p t h f i -> p (t h f i)")
                    for tl in range(4):
                        tp(ps2[:, tl * 128:(tl + 1) * 128], qflat[:, tl * 128:(tl + 1) * 128], [b_qrot], [bps2],
                           inc=(tl == 3))
                    evac(QTr[e_ * 64:(e_ + 1) * 64, g * 512:(g + 1) * 512], ps2[e_ * 64:(e_ + 1) * 64, :], [bps2],
                         [b_QTr[e_][g]])
                return f

            for g in range(NB):
                steps += [mk_ra(g), mk_q(g), mk_rb(g), mk_k(g), mk_v(g)]
            return steps

        for hp in range(4):
            for e in range(2):
                h = 2 * hp + e
                sl = e
                if h == 0:
                    load_head_weights(0)
                    for st_ in prod_steps(0, None):
                        st_()
                nxt_prod = []
                if h + 1 < 8:
                    load_head_weights(h + 1)
                    nxt_prod = prod_steps(h + 1, 3)
                precast_some(11)
                for zi in range(2):
                    memset("pool", QTrz[zi][(1 - e) * 64:(2 - e) * 64, :], 0.0, [b_QTrz[zi]])
                LA = 3
                items = [(qb, kc) for qb in range(NB) for kc in range(NT)]
                accs = {}
                pend = {}
                dst = {}
                deferred = []

                def acc_of(qb):
                    if qb not in accs:
                        a = (qb % 2) * 2
                        accs[qb] = (banks[a], bank_b[a], banks[1], bank_b[1])
                    return accs[qb]

                def scores(qb, kc, sl=sl, e=e):
                    qc = slice(qb * 512, (qb + 1) * 512)
                    zi = qb % 2
                    if kc == 0:
                        cp("pool", QTrz[zi][e * 64:(e + 1) * 64, :], QTr[e * 64:(e + 1) * 64, qc], [b_QTr[e][qb]],
                           [b_QTrz[zi]])
                    si = 4 + sc_rr[0]
                    sc_rr[0] = (sc_rr[0] + 1) % 4
                    ps, bps = banks[si], bank_b[si]
                    kcs = slice(kc * 128, (kc + 1) * 128)
                    mm(ps, KT[sl][:, kcs], QTn[sl][:, qc], True, False,
                       [b_KT[sl][kc // 4], b_QTn[sl][qb]], [bps], inc=False)
                    mm(ps, KrT[:, kcs], QTrz[zi], False, True, [b_kr[kc], b_QTrz[zi]], [bps], inc=True)
                    i = pt_rr[0]
                    pt_rr[0] = (i + 1) % len(PT)
                    act(PT[i], ps, AF.Exp, [bps], [b_PT[i]], scale=MLA_SCALE)
                    pend[(qb, kc)] = i

                def pv(qb, kc, sl=sl):
                    accO, bO, accD, bD = acc_of(qb)
                    i = pend.pop((qb, kc))
                    on_pe = False
                    mm(accO, Vt[sl][:, kc, :], PT[i], kc == 0, kc == NT - 1, [b_V[sl][kc // 4], b_PT[i]], [bO],
                       inc=not on_pe)
                    if on_pe:
                        mm(accD, ones_bf, PT[i], kc == 7, False, [b_const, b_PT[i]], [bD], inc=True)
                    else:
                        d = dst.setdefault(qb, {"n": 0, "used": [False, False]})
                        j = d["n"] % 2
                        d["n"] += 1
                        if not d["used"][j]:
                            d["used"][j] = True
                            cp("dve", dacc[j], PT[i], [b_PT[i]], [b_dacc[j]])
                        else:
                            tt("dve", dacc[j], dacc[j], PT[i], ALU.add, [b_dacc[j], b_PT[i]], [b_dacc[j]])

                def epi_pe(qb):
                    accO, bO, accD, bD = acc_of(qb)
                    ri = qb % 2
                    mm(accD, ones_f, recs[ri], True, True, [b_const, b_recs[ri]], [bD], inc=True)

                def epi_dve(qb, h=h):
                    accO, bO, accD, bD = acc_of(qb)
                    ri = qb % 2
                    qc = slice(qb * 512, (qb + 1) * 512)
                    act(recs[ri], accD, AF.Ln, [bD], [b_recs[ri]])
                    act(recs[ri], recs[ri], AF.Exp, [b_recs[ri]], [b_recs[ri]], scale=-1.0)
                    tt("dve", OT[:, h, qc], accO, recs[ri], ALU.mult, [bO, b_recs[ri]], [b_OT[h][qb]])

                for n in range(min(LA, len(items))):
                    scores(*items[n])
                for n, (qb, kc) in enumerate(items):
                    pv(qb, kc)
                    if n + LA < len(items):
                        scores(*items[n + LA])
                    if kc == NT - 1:
                        ri = qb % 2
                        tt("dve", recs[ri], dacc[0], dacc[1], ALU.add, [b_dacc[0], b_dacc[1]], [b_recs[ri]])
                        deferred.append(qb)
                    if kc == 3 and deferred:
                        epi_pe(deferred[0])
                    if kc == 5 and deferred:
                        epi_dve(deferred.pop(0))
                    if n % 6 == 4 and nxt_prod:
                        nxt_prod.pop(0)()
                for qb in deferred:
                    epi_pe(qb)
                    epi_dve(qb)
                while nxt_prod:
                    nxt_prod.pop(0)()
        dump("OT", OT[:, :, 0:512].bitcast(BF16) if False else OT[:, :, 0:512], [b_OT[h][0] for h in range(8)])
        S.barrier()

        P3 = P1O
        G1b = V(P3, F32, [1024]); G2b = V(P3 + 4096, F32, [1024]); gFb = V(P3 + 8192, F32, [1024])
        Bm = V(P3 + 12288, F32, [3, 8, 128])
        b_G = Buf("G"); b_Bm = Buf("Bm")
        hTe = V(P3 + 24576, BF16, [8, 768]); b_hTe = [[Buf() for _ in range(8)] for _ in range(6)]
        U = P3 + 36864
        ksT = V(U, BF16, [2, 768]); b_ks = Buf("ks")
        vs = V(U + 3072, BF16, [6, 256]); b_vs = [Buf() for _ in range(6)]
        qsT = V(U + 6144, BF16, [8, 512]); b_qs = [Buf() for _ in range(8)]
        swaOT = V(U + 14336, BF16, [8, 512]); b_swaO = [[Buf() for _ in range(4)] for _ in range(2)]
        mergedT = V(U + 22528, BF16, [8, 512]); b_mg = [Buf() for _ in range(8)]
        uT = V(U, BF16, [32, 512]); b_uT = [Buf() for _ in range(32)]
        u_old = [b_ks] + b_vs + b_qs + b_swaO[0] + b_swaO[1] + b_mg
        XB = U + 32768
        xb = [V(XB + 4096 * i, F32, [1024]) for i in range(2)]; b_xb = [Buf(), Buf()]
        x1 = V(XB + 8192, F32, [4, 1024]); b_x1 = [Buf() for _ in range(4)]
        PT3 = [V(XB + 24576 + 1024 * i, BF16, [4, 128]) for i in range(3)]; b_PT3 = [Buf() for _ in range(3)]
        WR = XB + 27648
        b_wrh = [Buf() for _ in range(8)]
        b_wr = [[b_wrh[0], b_wrh[1]]]
        xn3 = V(WR + 16384, F32, [1024]); b_xn3 = Buf("xn3")
        sfp = [V(WR + 20480 + 2048 * i, F32, [512]) for i in range(3)]; b_sfp = [Buf() for _ in range(3)]
        junk3 = V(WR + 26624, BF16, [1024])
        dtmp = xn3[:, 0:512]; b_dtmp = b_xn3
        for i_ in range(4):
            PT3.append(V(EXTRA + 1024 * i_, BF16, [4, 128])); b_PT3.append(Buf())
        swb_rr = [0]; sfp_rr = [0]; pt3_rr = [0]
        assert WR + 26624 + 2048 <= ARENA_BYTES
        junk = junk3
        wr_rr = [0]

        def wpiece(*key):
            idx, kch, ncols = pspec[key]
            i = wr_rr[0]
            nh = 1 if kch * ncols <= 1024 else 2
            if nh == 2 and i % 2 == 1:
                i += 1
            i %= 8
            wr_rr[0] = (i + nh) % 8
            bw = b_wrh[i:i + nh]
            v = V(WR + 2048 * i, BF16, [kch, ncols])
            dma("pool", v, wsc[idx][:, 0:kch * ncols].rearrange("p (k n) -> p k n", k=kch), [b_wsc[idx]], bw)
            return v, bw

        dg = [V(WR + 20480 + 2048 * i, F32, [128]) for i in range(2)]
        for gi, (gvec, Gb, bsrc) in enumerate(((g1v, G1b, b_mod), (g2v, G2b, b_mod), (nfin, gFb, b_small))):
            for hf in range(2):
                ps, bps = nextbank()
                for j in range(4):
                    c = hf * 4 + j
                    d = dg[c % 2]
                    bd = b_sfp[c % 2]
                    ts("dve", d, ident_f, gvec[:, c:c + 1], None, ALU.mult, None, [b_const, bsrc], [bd])
                    mm(ps[:, j * 128:(j + 1) * 128], ones_f, d, True, True, [b_const, bd], [bps], inc=True)
                cp("dve", Gb[:, hf * 512:(hf + 1) * 512], ps, [bps], [b_G])
        ps, bps = nextbank()
        mm(ps[0:8, :], rb_aug, oht, True, True, [b_small], [bps], inc=True)
        b_tbl = Buf("tbl")
        cp("dve", tbl_sb, ps[0:8, :], [bps], [b_tbl])
        b_tbld = Buf("tbld")
        dma("sp", tbl_t.ap(), tbl_sb, [b_tbl], [b_tbld])
        hank = V(WR, F32, [8, 128])
        for dl in range(3):
            src = bass.AP(tensor=tbl_t, offset=dl * 128, ap=[[1, 128], [512, 8], [1, 128]])
            dma("sp", hank, src, [b_tbld], [b_wr[0]])
            for hh in range(2):
                ps, bps = nextbank()
                for j in range(4):
                    h = hh * 4 + j
                    mm(ps[:, j * 128:(j + 1) * 128], hank[:, h, :], jrev_f, True, True, [b_wr[0], b_const], [bps],
                       inc=(j == 3))
                cp("dve", Bm[:, dl, hh * 4:(hh + 1) * 4, :].rearrange("p h q -> p (h q)"), ps, [bps], [b_Bm])
        dump("Bm", Bm.rearrange("p a h q -> p (a h q)"), [b_Bm])
        dump("G1b", G1b, [b_G])

        nt_front, nt_back = make_nt([xb[0], xb[1], xn3], [b_xb[0], b_xb[1], b_xn3])

        def hext_steps(b):
            valid = [j for j in range(6) if 0 <= 4 * b - 1 + j < NT]
            slot = {}
            steps = []

            def mk_front(j):
                def f():
                    te = 4 * b - 1 + j
                    slot[j] = nt_front(None, None, load=x_d[te * 128:(te + 1) * 128, :])
                return f

            def mk_back(j):
                def f():
                    nt_back(slot[j], a1, sh1, lambda k, j=j: hTe[:, k, j * 128:(j + 1) * 128], b_hTe[j])
                return f

            for n, j in enumerate(valid):
                steps.append(mk_front(j))
                if n >= 1:
                    steps.append(mk_back(valid[n - 1]))
            steps.append(mk_back(valid[-1]))
            return steps

        def act_recip(buf, src, R, W):
            act(buf, src, AF.Ln, R, W)
            act(buf, buf, AF.Exp, W, W, scale=-1.0)

        for st_ in hext_steps(0):
            st_()
        for b in range(NB):
            S.alias(u_old, b_uT)
            valid = [j for j in range(6) if 0 <= 4 * b - 1 + j < NT]
            own = slice(128, 640)
            b_own = b_hTe[1:5]
            for tl in range(4):
                te = 4 * b + tl
                dma("sp", x1[:, tl, :], x_d[te * 128:(te + 1) * 128, :], [], [b_x1[tl]])
            wp, bwp = wpiece("ks")
            for kv in range(2):
                for (j0, j1) in ((0, 4), (4, 6)):
                    js = [j for j in valid if j0 <= j < j1]
                    if not js:
                        continue
                    cs = slice(js[0] * 128, (js[-1] + 1) * 128)
                    n = (js[-1] + 1 - js[0]) * 128
                    ps, bps = nextbank()
                    for k in range(8):
                        mm(ps[:, 0:n], wp[:, k, kv * 128:(kv + 1) * 128], hTe[:, k, cs], k == 0, k == 7,
                           [bwp] + [b_hTe[j] for j in js], [bps], inc=(k == 7))
                    cp("act", ksT[:, kv, cs], ps[:, 0:n], [bps], [b_ks])
            wp, bwp = wpiece("vs")
            for j in valid:
                ps, bps = nextbank()
                for k in range(8):
                    mm(ps[:, 0:256], hTe[:, k, j * 128:(j + 1) * 128], wp[:, k, :], k == 0, k == 7,
                       [bwp, b_hTe[j]], [bps], inc=(k == 7))
                cp("act", vs[:, j, :], ps[:, 0:256], [bps], [b_vs[j]])
            for pc in range(4):
                wp, bwp = wpiece("qs", pc)
                for hh in range(2):
                    h = pc * 2 + hh
                    ps, bps = nextbank()
                    for k in range(8):
                        mm(ps, wp[:, k, hh * 128:(hh + 1) * 128], hTe[:, k, own], k == 0, k == 7,
                           [bwp] + b_own, [bps], inc=(k == 7))
                    cp("act", qsT[:, h, :], ps, [bps], [b_qs[h]])
            units = [(qt, kv) for qt in range(4) for kv in range(2)]
            sw = {}

            def swa_scores(u):
                qt, kv = units[u]
                j = qt + 1
                hs = slice(kv * 4, (kv + 1) * 4)
                dls = [dl for dl in (-1, 0, 1) if (j + dl) in valid]
                pts = []
                for dl in dls:
                    jk = j + dl
                    bi = 4 + swb_rr[0]
                    swb_rr[0] = (swb_rr[0] + 1) % 4
                    ps, bps = banks[bi], bank_b[bi]
                    psv = ps.rearrange("p (h q) -> p h q", h=4)
                    mm(psv, ksT[:, kv, jk * 128:(jk + 1) * 128], qsT[:, hs, qt * 128:(qt + 1) * 128], True, True,
                       [b_ks] + b_qs[kv * 4:(kv + 1) * 4], [bps], inc=True)
                    si = sfp_rr[0]
                    sfp_rr[0] = (si + 1) % 3
                    pi = pt3_rr[0]
                    pt3_rr[0] = (pi + 1) % len(PT3)
                    sv_ = sfp[si].rearrange("p (h q) -> p h q", h=4)
                    stt("dve", sv_, psv, SWA_SCALE, Bm[:, dl + 1, hs, :], ALU.mult, ALU.add, [bps, b_Bm],
                        [b_sfp[si]])
                    act(PT3[pi], sv_, AF.Exp, [b_sfp[si]], [b_PT3[pi]])
                    pts.append((jk, pi))
                sw[u] = pts

            def swa_pv(u):
                qt, kv = units[u]
                hs = slice(kv * 4, (kv + 1) * 4)
                a = (u % 2) * 2
                accO, bO, accD, bD = banks[a], bank_b[a], banks[a + 1], bank_b[a + 1]
                pts = sw.pop(u)
                for n_i, (jk, pi) in enumerate(pts):
                    first = (n_i == 0)
                    last = (n_i == len(pts) - 1)
                    ptf = PT3[pi].rearrange("p h q -> p (h q)")
                    mm(accO, vs[:, jk, kv * 128:(kv + 1) * 128], ptf, first, last, [b_vs[jk], b_PT3[pi]], [bO],
                       inc=False)
                    mm(accD, ones_bf, ptf, first, last, [b_const, b_PT3[pi]], [bD], inc=True)
                dv = dtmp.rearrange("p (h q) -> p h q", h=4)
                tt("dve", dv, accD.rearrange("p (h q) -> p h q", h=4),
                   sinkexp[:, hs].unsqueeze(2).broadcast_to([128, 4, 128]), ALU.add, [bD, b_small], [b_dtmp])
                act_recip(dtmp, dtmp, [b_dtmp], [b_dtmp])
                tt("dve", swaOT[:, hs, qt * 128:(qt + 1) * 128], accO.rearrange("p (h q) -> p h q", h=4), dv,
                   ALU.mult, [bO, b_dtmp], [b_swaO[kv][qt]])

            reserved.update(range(4))
            swa_scores(0)
            for u in range(len(units)):
                if u + 1 < len(units):
                    swa_scores(u + 1)
                swa_pv(u)
            reserved.clear()
            if b == 0:
                dump("swaOT", swaOT, b_swaO[0] + b_swaO[1])
            for c in range(8):
                if True:
                    pg0, bpg0 = wpiece("g0", c)
                    pg1, bpg1 = wpiece("g1", c)
                    pa, bpa = wpiece("oa", c)
                    pb, bpb = wpiece("ob", c)
                    ccs = slice(0, 128)
                    ps_g0, bg0 = nextbank()
                    for k in range(8):
                        mm(ps_g0, pg0[:, k, ccs], hTe[:, k, own], k == 0, k == 7, [bpg0] + b_own, [bg0], inc=(k == 7))
                    ps_g1, bg1 = nextbank()
                    for k in range(8):
                        mm(ps_g1, pg1[:, k, ccs], hTe[:, k, own], k == 0, k == 7, [bpg1] + b_own, [bg1], inc=(k == 7))
                    ps_a, ba = nextbank()
                    for h in range(8):
                        mm(ps_a, pa[:, h, ccs], OT[:, h, b * 512:(b + 1) * 512], h == 0, h == 7, [bpa, b_OT[h][b]],
                           [ba], inc=(h == 7))
                    ps_b, bb = nextbank()
                    for h in range(8):
                        mm(ps_b, pb[:, h, ccs], swaOT[:, h, :], h == 0, h == 7,
                           [bpb] + [b_swaO[h // 4][q] for q in range(4)], [bb], inc=(h == 7))
                    act(sfp[0], ps_g0, AF.Sigmoid, [bg0], [b_sfp[0]])
                    act(sfp[1], ps_g1, AF.Sigmoid, [bg1], [b_sfp[1]])
                    tt("dve", sfp[0], sfp[0], ps_a, ALU.mult, [b_sfp[0], ba], [b_sfp[0]])
                    tt("dve", sfp[1], sfp[1], ps_b, ALU.mult, [b_sfp[1], bb], [b_sfp[1]])
                    tt("dve", mergedT[:, c, :], sfp[0], sfp[1], ALU.add, [b_sfp[0], b_sfp[1]], [b_mg[c]])
            if b == 0:
                dump("mergedT", mergedT, b_mg)
            slot2 = {}
            for tp2 in range(2):
                for cq in range(4):
                    po, bpo = wpiece("wo", cq)
                    cs = slice(cq * 256, (cq + 1) * 256)
                    ps, bps = nextbank()
                    for tl2 in range(2):
                        tl = tp2 * 2 + tl2
                        for k in range(8):
                            mm(ps[:, tl2 * 256:(tl2 + 1) * 256], mergedT[:, k, tl * 128:(tl + 1) * 128], po[:, k, :],
                               k == 0, k == 7, [bpo, b_mg[k]], [bps], inc=(k == 7 and tl2 == 1))
                    for tl2 in range(2):
                        tl = tp2 * 2 + tl2
                        si = tl2
                        tt("dve", sfp[si][:, 0:256], ps[:, tl2 * 256:(tl2 + 1) * 256], G1b[:, cs], ALU.mult,
                           [bps, b_G], [b_sfp[si]])
                        tt("dve", x1[:, tl, cs], sfp[si][:, 0:256], x1[:, tl, cs], ALU.add, [b_sfp[si], b_x1[tl]],
                           [b_x1[tl]])
                if tp2 == 1:
                    for tl in (0, 1):
                        nt_back(slot2[tl], a2, sh2, lambda k, tl=tl: hTe[:, k, tl * 128:(tl + 1) * 128], b_hTe[tl])
                for tl in (tp2 * 2, tp2 * 2 + 1):
                    slot2[tl] = nt_front(x1[:, tl, :], b_x1[tl])
            for tl in (2, 3):
                nt_back(slot2[tl], a2, sh2, lambda k, tl=tl: hTe[:, k, tl * 128:(tl + 1) * 128], b_hTe[tl])
            if b == 0:
                dump("x1", x1.rearrange("p t d -> p (t d)"), b_x1)
            S.alias(b_uT, u_old)
            h2 = slice(0, 512)
            b_h2 = b_hTe[0:4]
            for pc in range(16):
                p1, bp1 = wpiece("f1", pc)
                for cc in range(2):
                    ch = pc * 2 + cc
                    ps, bps = nextbank()
                    for k in range(8):
                        mm(ps, p1[:, k, cc * 128:(cc + 1) * 128], hTe[:, k, h2], k == 0, k == 7, [bp1] + b_h2, [bps],
                           inc=(k == 7))
                    si = ch % 3
                    act(sfp[si], ps, AF.Relu, [bps], [b_sfp[si]])
                    tt("dve", uT[:, ch, :], sfp[si], sfp[si], ALU.mult, [b_sfp[si]], [b_uT[ch]])
            nxt = hext_steps(b + 1) if b + 1 < NB else []
            for hf in range(2):
                cs = slice(hf * 512, (hf + 1) * 512)
                accs = [(banks[i], bank_b[i]) for i in range(4)]
                reserved.update(range(4))
                for g in range(8):
                    p2, bp2 = wpiece("f2", hf, g)
                    for kk in range(4):
                        kc = g * 4 + kk
                        for tl in range(4):
                            mm(accs[tl][0], uT[:, kc, tl * 128:(tl + 1) * 128], p2[:, kk, :], kc == 0, kc == 31,
                               [bp2, b_uT[kc]], [accs[tl][1]], inc=(kc == 31 or (kk == 3 and tl == 3)))
                    if nxt and not (hf == 1 and g == 7):
                        nxt.pop(0)()
                reserved.clear()
                for tl in range(4):
                    si = tl % 3
                    tt("dve", sfp[si], accs[tl][0], G2b[:, cs], ALU.mult, [accs[tl][1], b_G], [b_sfp[si]])
                    tt("dve", x1[:, tl, cs], sfp[si], x1[:, tl, cs], ALU.add, [b_sfp[si], b_x1[tl]], [b_x1[tl]])
            for tl in range(4):
                te = 4 * b + tl
                ss, bss = new_stat()
                act(junk, x1[:, tl, :], AF.Square, [b_x1[tl]], [b_junk, bss], accum=ss)
                rstd_of(ss, D, [bss])
                stt("dve", x1[:, tl, :], x1[:, tl, :], ss, gFb, ALU.mult, ALU.mult, [b_x1[tl], bss, b_G], [b_x1[tl]])
                dma("sp", out_d[te * 128:(te + 1) * 128, :], x1[:, tl, :], [b_x1[tl]], [])
            while nxt:
                nxt.pop(0)()
        S.barrier()
        S.emit()
    return nc


_PROGRAM = None


def _lay(v, k):
    return np.ascontiguousarray(np.asarray(v, dtype=np.float32).reshape(k, 128).T)


def kernel(x, c, positions, w_ada, b_ada, norm_mix, w_in, q_norm, w_uq, kv_norm, w_ukv,
           rel_bias, sink, w_o_mla, w_o_swa, w_out, norm_mlp, w_ff1, w_ff2, norm_final):
    global _PROGRAM
    if _PROGRAM is None:
        _PROGRAM = build_program()
    nc = _PROGRAM
    f = lambda a: np.ascontiguousarray(np.asarray(a, dtype=np.float32))
    consts = _host_consts()
    shared = {
        "w_ada": f(w_ada[0]), "b_ada_l": _lay(b_ada[0], 48), "norm_mix_l": _lay(norm_mix[0], 8),
        "w_in": f(w_in[0]), "q_norm_l": _lay(q_norm[0], 3), "w_uq": f(w_uq[0]),
        "kv_norm_l": _lay(kv_norm[0], 2), "w_ukv": f(w_ukv[0]), "rel_bias": f(rel_bias),
        "sink": f(sink[0]), "w_o_mla": f(w_o_mla[0]), "w_o_swa": f(w_o_swa[0]), "w_out": f(w_out[0]),
        "norm_mlp_l": _lay(norm_mlp[0], 8), "w_ff1": f(w_ff1[0]), "w_ff2": f(w_ff2[0]),
        "norm_final_l": _lay(norm_final, 8),
    }
    shared.update(consts)
    x = np.asarray(x, dtype=np.float32)
    c = np.asarray(c, dtype=np.float32)
    positions = np.asarray(positions, dtype=np.int32)
    in_maps = []
    for b in range(N_CORES):
        m = dict(shared)
        m["x"] = np.ascontiguousarray(x[b])
        m["c_l"] = _lay(c[b], 8)
        m["pos_l"] = np.ascontiguousarray(positions[b].reshape(NT, 128).T)
        in_maps.append(m)
    res = run_bass_kernel_spmd(nc, in_maps, core_ids=list(range(N_CORES)))
    kernel.last_results = res
    return np.stack([np.asarray(r["out"], dtype=np.float32) for r in res.results], axis=0)
```

```python
import math
import contextlib
import numpy as np
import concourse.bass as bass
import concourse.mybir as mybir
from concourse.bass_utils import run_bass_kernel_spmd

F32 = mybir.dt.float32
BF16 = mybir.dt.bfloat16
I32 = mybir.dt.int32
AF = mybir.ActivationFunctionType
ALU = mybir.AluOpType

D = 1024
SEQ = 4096
NT = SEQ // 128
NB = SEQ // 512
D_IN = 4288
EPS = 1e-6
MLA_SCALE = 192 ** -0.5
SWA_SCALE = 128 ** -0.5
NEG = -30000.0
TWO_PI = 2.0 * math.pi
C1 = 6.28125
C2 = TWO_PI - C1
PI_SAFE = 3.1415925
N_CORES = 8

DEBUG = {}


class Buf:
    __slots__ = ("name", "w", "r", "psum")

    def __init__(self, name="", psum=False):
        self.name = name
        self.w = None
        self.r = {}
        self.psum = psum


class Sched:
    ENGS = ("pe", "act", "dve", "pool", "sp")

    def __init__(self, nc, n_dma_sems=48):
        self.nc = nc
        self.ops = {e: [] for e in self.ENGS}
        self.cnt = {e: 0 for e in self.ENGS}
        self.known = {e: {} for e in self.ENGS}
        self.n_dma = n_dma_sems
        self.dma_cnt = [0] * n_dma_sems
        half = n_dma_sems // 2
        self.dma_pools = {"sp": list(range(0, half)), "pool": list(range(half, n_dma_sems))}
        self.dma_rr = {"sp": 0, "pool": 0}
        self.sems = {}

    def _need(self, X, waits, key, val):
        if self.known[X].get(key, 0) >= val:
            return
        if waits.get(key, 0) < val:
            waits[key] = val

    @staticmethod
    def _flat(bs):
        out = []
        for b in bs:
            if isinstance(b, (list, tuple)):
                out.extend(Sched._flat(b))
            else:
                out.append(b)
        return out

    def op(self, eng, fn, reads=(), writes=(), inc=True, dma=False):
        X = eng
        reads = self._flat(reads)
        writes = self._flat(writes)
        waits = {}
        for b in reads:
            if b.psum:
                for k, (v, e) in b.r.items():
                    if e != X:
                        self._need(X, waits, k, v)
            if b.w is not None:
                k, v, e = b.w
                if e == X and k == X and X == "pe":
                    continue
                self._need(X, waits, k, v)
        for b in writes:
            if b.w is not None:
                k, v, e = b.w
                if not (e == X and k == X):
                    self._need(X, waits, k, v)
            for k, (v, e) in b.r.items():
                if e == X and k == X:
                    continue
                self._need(X, waits, k, v)
        if dma:
            pl = "pool" if X == "pool" else "sp"
            lst = self.dma_pools[pl]
            i = lst[self.dma_rr[pl]]
            self.dma_rr[pl] = (self.dma_rr[pl] + 1) % len(lst)
            key = "d%d" % i
            if self.dma_cnt[i] > 0:
                self._need(X, waits, key, 16 * self.dma_cnt[i])
            self.dma_cnt[i] += 1
            tok = (key, 16 * self.dma_cnt[i], X)
            incspec = (key, 16)
        else:
            if inc:
                self.cnt[X] += 1
                tok = (X, self.cnt[X], X)
                incspec = (X, 1)
            else:
                tok = (X, self.cnt[X] + 1, X)
                incspec = None
        for k, v in waits.items():
            self.known[X][k] = v
        self.ops[X].append((tuple(waits.items()), fn, incspec))
        k, v, e = tok
        for b in reads:
            old = b.r.get(k)
            if old is None or old[0] < v:
                b.r[k] = (v, e)
        for b in writes:
            b.w = tok
            b.r = {}
        return tok

    def alias(self, news, olds):
        acc = {}
        for o in olds:
            for k, (v, e) in o.r.items():
                if k not in acc or acc[k][0] < v:
                    acc[k] = (v, "?")
            if o.w is not None:
                k, v, e = o.w
                if k not in acc or acc[k][0] < v:
                    acc[k] = (v, "?")
        for n in news:
            for k, (v, e) in acc.items():
                old = n.r.get(k)
                if old is None or old[0] < v:
                    n.r[k] = (v, e)

    def barrier(self):
        for X in self.ENGS:
            waits = {}
            for i in range(self.n_dma):
                if self.dma_cnt[i] > 0:
                    self._need(X, waits, "d%d" % i, 16 * self.dma_cnt[i])
            for e in self.ENGS:
                if e != X and self.cnt[e] > 0:
                    self._need(X, waits, e, self.cnt[e])
            for k, v in waits.items():
                self.known[X][k] = v
            self.ops[X].append((tuple(waits.items()), None, None))

    def emit(self):
        nc = self.nc
        with contextlib.ExitStack() as st:
            keys = list(self.ENGS) + ["d%d" % i for i in range(self.n_dma)]
            for k in keys:
                self.sems[k] = st.enter_context(nc.semaphore("s_" + k))
            block = st.enter_context(nc.Block())
            sems = self.sems

            def run(e, engobj):
                for waits, fn, incspec in self.ops[e]:
                    for k, v in waits:
                        engobj.wait_ge(sems[k], v)
                    if fn is None:
                        continue
                    ins = fn(engobj)
                    if incspec is not None:
                        ins.then_inc(sems[incspec[0]], incspec[1])

            @block.tensor
            def _(t):
                run("pe", t)

            @block.scalar
            def _(s):
                run("act", s)

            @block.vector
            def _(v):
                run("dve", v)

            @block.gpsimd
            def _(g):
                run("pool", g)

            @block.sync
            def _(s):
                run("sp", s)


def _t5_bucket_np(rel):
    rel = np.asarray(rel, dtype=np.int32)
    half, max_exact = 16, 8
    ret = np.where(rel > 0, half, 0)
    n = np.abs(rel)
    nf = np.maximum(n, 1).astype(np.float32)
    large = max_exact + (np.log(nf / np.float32(max_exact)) / np.float32(math.log(128 / max_exact))
                         * np.float32(half - max_exact)).astype(np.int32)
    large = np.minimum(large, half - 1)
    return ret + np.where(n < max_exact, n, large)


_BUCKET_FIX = {16: 10, 32: 12, 64: 14, 128: 15}


def _host_consts():
    ident = np.eye(128, dtype=np.float32)
    J = np.ascontiguousarray(ident[::-1])
    inv = (10000.0 ** (-np.arange(0, 64, 2, dtype=np.float32) / np.float32(64))).astype(np.float32)
    invt = np.ascontiguousarray(np.broadcast_to(inv[None, :], (128, 32))).astype(np.float32)
    oht = np.zeros((33, 512), dtype=np.float32)
    for m in range(512):
        rel = m - 255
        if abs(rel) <= 128:
            n = abs(rel)
            bk = int(_t5_bucket_np(rel))
            if n in _BUCKET_FIX:
                bk = _BUCKET_FIX[n] + (16 if rel > 0 else 0)
            oht[bk, m] = 1.0
        else:
            oht[32, m] = 1.0
    return {"ident": ident, "jrev": J, "invt": invt, "oht": oht}


ARENA_BYTES = 211968
EXTRA = 207872


def build_program():
    nc = bass.Bass("TRN2", target_bir_lowering=False)
    S = Sched(nc, n_dma_sems=48)

    def dram(name, shape, dt, kind="ExternalInput"):
        return nc.dram_tensor(name, shape, dt, kind=kind)

    x_d = dram("x", [SEQ, D], F32).ap()
    c_d = dram("c_l", [128, 8], F32).ap()
    pos_d = dram("pos_l", [128, NT], I32).ap()
    wada_d = dram("w_ada", [D, 6 * D], F32).ap()
    bada_d = dram("b_ada_l", [128, 48], F32).ap()
    nmix_d = dram("norm_mix_l", [128, 8], F32).ap()
    win_d = dram("w_in", [D, D_IN], F32).ap()
    qn_d = dram("q_norm_l", [128, 3], F32).ap()
    wuq_d = dram("w_uq", [384, 1536], F32).ap()
    kvn_d = dram("kv_norm_l", [128, 2], F32).ap()
    wukv_d = dram("w_ukv", [256, 2048], F32).ap()
    rb_d = dram("rel_bias", [32, 8], F32).ap()
    sink_d = dram("sink", [8], F32).ap()
    womla_d = dram("w_o_mla", [D, D], F32).ap()
    woswa_d = dram("w_o_swa", [D, D], F32).ap()
    wout_d = dram("w_out", [D, D], F32).ap()
    nmlp_d = dram("norm_mlp_l", [128, 8], F32).ap()
    wff1_d = dram("w_ff1", [D, 4 * D], F32).ap()
    wff2_d = dram("w_ff2", [4 * D, D], F32).ap()
    nfin_d = dram("norm_final_l", [128, 8], F32).ap()
    ident_d = dram("ident", [128, 128], F32).ap()
    jrev_d = dram("jrev", [128, 128], F32).ap()
    invt_d = dram("invt", [128, 32], F32).ap()
    oht_d = dram("oht", [33, 512], F32).ap()
    tbl_t = dram("tbl_scratch", [8, 512], F32, kind="Internal")
    NPIECE = 74
    wsc = dram("wsc", [NPIECE, 128, 2048], BF16, kind="Internal").ap()
    out_d = dram("out", [SEQ, D], F32, kind="ExternalOutput").ap()
    dbg_d = {}
    for name, shape in DEBUG.items():
        dbg_d[name] = dram("dbg_" + name, list(shape), F32, kind="ExternalOutput").ap()

    st = contextlib.ExitStack()
    with st:
        arena = st.enter_context(nc.sbuf_tensor("arena", [128, ARENA_BYTES // 2], BF16))
        banks = [st.enter_context(nc.psum_tensor("bank%d" % i, [128, 512], F32))[:, :] for i in range(8)]
        bank_b = [Buf("bank%d" % i, psum=True) for i in range(8)]

        def V(off, dt, shape, p0=0, p1=128):
            n = int(np.prod(shape))
            esz = 2 if dt == BF16 else 4
            assert off % 4 == 0 and off + n * esz <= ARENA_BYTES, (off, n, esz)
            v = arena[p0:p1, off // 2:(off + n * esz) // 2]
            if dt != BF16:
                v = v.bitcast(dt)
            if len(shape) == 2:
                v = v.rearrange("p (a b) -> p a b", a=shape[0])
            elif len(shape) == 3:
                v = v.rearrange("p (a b c) -> p a b c", a=shape[0], b=shape[1])
            elif len(shape) == 4:
                v = v.rearrange("p (a b c d) -> p a b c d", a=shape[0], b=shape[1], c=shape[2])
            return v

        def mm(out, lhsT, rhs, start, stop, R, W, inc):
            S.op("pe", lambda e: e.matmul(out, lhsT=lhsT, rhs=rhs, start=start, stop=stop),
                 reads=R, writes=W, inc=inc)

        def tp(out, in_, R, W, inc):
            S.op("pe", lambda e: e.transpose(out=out, in_=in_, identity=ident_f), reads=R + [b_const],
                 writes=W, inc=inc)

        def act(out, in_, func, R, W, bias=None, scale=None, accum=None):
            kw = {}
            if bias is not None:
                kw["bias"] = bias
            if scale is not None:
                kw["scale"] = scale
            if accum is not None:
                kw["accum_out"] = accum
            S.op("act", lambda e: e.activation(out=out, in_=in_, func=func, **kw), reads=R, writes=W)

        def ts(eng, out, in0, s1, s2, op0, op1, R, W):
            if op1 is None:
                S.op(eng, lambda e: e.tensor_scalar(out=out, in0=in0, scalar1=s1, scalar2=None, op0=op0),
                     reads=R, writes=W)
            else:
                S.op(eng, lambda e: e.tensor_scalar(out=out, in0=in0, scalar1=s1, scalar2=s2, op0=op0, op1=op1),
                     reads=R, writes=W)

        def tt(eng, out, in0, in1, op, R, W):
            S.op(eng, lambda e: e.tensor_tensor(out=out, in0=in0, in1=in1, op=op), reads=R, writes=W)

        def stt(eng, out, in0, scalar, in1, op0, op1, R, W):
            S.op(eng, lambda e: e.scalar_tensor_tensor(out=out, in0=in0, scalar=scalar, in1=in1, op0=op0, op1=op1),
                 reads=R, writes=W)

        def cp(eng, out, in_, R, W):
            if eng == "act":
                S.op("act", lambda e: e.copy(out=out, in_=in_), reads=R, writes=W)
            else:
                S.op(eng, lambda e: e.tensor_copy(out=out, in_=in_), reads=R, writes=W)

        def recip(out, in_, R, W):
            S.op("dve", lambda e: e.reciprocal(out=out, in_=in_), reads=R, writes=W)

        def dma(eng, out, in_, R, W):
            S.op(eng, lambda e: e.dma_start(out=out, in_=in_), reads=R, writes=W, dma=True)

        def memset(eng, ap, val, W):
            S.op(eng, lambda e: e.memset(ap, val), writes=W)

        bank_rr = [0]

        reserved = set()

        def nextbank():
            while True:
                i = bank_rr[0]
                bank_rr[0] = (i + 1) % 8
                if i not in reserved:
                    return banks[i], bank_b[i]

        def dump(name, ap, R):
            if name in dbg_d:
                dma("sp", dbg_d[name], ap, R, [])

        ident_f = V(0, F32, [128])
        ones_f = V(512, F32, [128])
        ones_bf = V(1024, BF16, [128])
        jrev_f = V(1280, F32, [128])
        b_const = Buf("const")
        SV = 2048

        def sv(i, n):
            return V(SV + 4 * i, F32, [n])

        cT = sv(0, 8); cexp = sv(8, 8); cact2 = V(SV + 64, F32, [8, 2])
        modT = sv(32, 48); badaT = sv(80, 48)
        nmix = sv(128, 8); nmlp = sv(136, 8); nfin = sv(144, 8)
        a1 = sv(152, 8); a2 = sv(160, 8)
        qn = sv(168, 3); kvn = sv(172, 2)
        sinkexp = sv(176, 8)
        pos_f = sv(184, 32)
        pos_i = V(SV + 4 * 216, I32, [32])
        stat = sv(248, 16)
        invt = sv(264, 32)
        cos_t = V(4096, F32, [32, 32])
        sin_t = V(8192, F32, [32, 32])
        rb_aug = V(3328, F32, [8], 0, 33)
        tbl_sb = V(12288, F32, [512], 0, 8)
        oht = V(14336, F32, [512], 0, 33)
        b_small = Buf("small")
        b_trig = Buf("trig")
        b_mod = Buf("mod")

        OT_OFF = 16384
        OT = V(OT_OFF, BF16, [8, SEQ])
        b_OT = [[Buf("OT%d_%d" % (h, q)) for q in range(NB)] for h in range(8)]
        P1O = 81920
        cqnT = V(P1O, BF16, [3, SEQ])
        ckvnT = V(P1O + 24576, BF16, [2, SEQ])
        KrT = V(P1O + 40960, BF16, [SEQ])
        b_cqn = [Buf("cqn%d" % t) for t in range(NT)]
        b_ckvn = [Buf("ckvn%d" % t) for t in range(NT)]
        b_kr = [Buf("kr%d" % t) for t in range(NT)]
        TR = 131072

        dma("sp", ident_f, ident_d, [], [b_const])
        dma("sp", jrev_f, jrev_d, [], [b_const])
        dma("sp", invt, invt_d, [], [b_small])
        dma("sp", oht, oht_d, [], [b_small])
        dma("sp", cT, c_d, [], [b_small])
        dma("sp", badaT, bada_d, [], [b_small])
        dma("sp", nmix, nmix_d, [], [b_small])
        dma("sp", nmlp, nmlp_d, [], [b_small])
        dma("sp", nfin, nfin_d, [], [b_small])
        dma("sp", qn, qn_d, [], [b_small])
        dma("sp", kvn, kvn_d, [], [b_small])
        dma("sp", pos_i, pos_d, [], [b_small])
        dma("sp", sinkexp, sink_d.partition_broadcast(128), [], [b_small])
        dma("sp", rb_aug[0:32, :], rb_d, [], [b_small])
        memset("dve", rb_aug[32:33, :], NEG, [b_small])
        memset("dve", ones_f, 1.0, [b_const])
        memset("dve", ones_bf, 1.0, [b_const])

        act(cexp, cT, AF.Exp, [b_small], [b_small], scale=-1.0)
        ts("dve", cexp, cexp, 1.0, None, ALU.add, None, [b_small], [b_small])
        recip(cexp, cexp, [b_small], [b_small])
        tt("dve", cact2[:, :, 0], cT, cexp, ALU.mult, [b_small], [b_small])
        tt("dve", cact2[:, :, 1], cT, cexp, ALU.mult, [b_small], [b_small])
        act(sinkexp, sinkexp, AF.Exp, [b_small], [b_small])

        tg = [V(TR + 32768 + 4096 * i, F32, [32, 32]) for i in range(4)]
        tgi = V(TR + 32768 + 4096 * 4, I32, [32, 32])
        b_tg = Buf("tg")
        cp("dve", pos_f, pos_i, [b_small], [b_small])
        ang, nf, rr, mk = tg
        tt("dve", ang, pos_f.unsqueeze(2).broadcast_to([128, 32, 32]),
           invt.unsqueeze(1).broadcast_to([128, 32, 32]), ALU.mult, [b_small], [b_tg])
        ts("dve", tgi, ang, 1.0 / TWO_PI, None, ALU.mult, None, [b_tg], [b_tg])
        cp("dve", nf, tgi, [b_tg], [b_tg])
        stt("dve", rr, nf, -C1, ang, ALU.mult, ALU.add, [b_tg], [b_tg])
        stt("dve", rr, nf, -C2, rr, ALU.mult, ALU.add, [b_tg], [b_tg])

        def wrap(r):
            ts("dve", mk, r, math.pi, TWO_PI, ALU.is_gt, ALU.mult, [b_tg], [b_tg])
            tt("dve", r, r, mk, ALU.subtract, [b_tg], [b_tg])
            ts("dve", mk, r, -math.pi, TWO_PI, ALU.is_lt, ALU.mult, [b_tg], [b_tg])
            tt("dve", r, r, mk, ALU.add, [b_tg], [b_tg])
            ts("dve", r, r, -PI_SAFE, PI_SAFE, ALU.max, ALU.min, [b_tg], [b_tg])

        wrap(rr)
        act(sin_t, rr, AF.Sin, [b_tg], [b_trig])
        ts("dve", rr, rr, math.pi / 2, None, ALU.add, None, [b_tg], [b_tg])
        wrap(rr)
        act(cos_t, rr, AF.Sin, [b_tg], [b_trig])

        stg = [V(TR + 16384 * i, F32, [8, 512]) for i in range(2)]
        b_stg = [Buf("stg0"), Buf("stg1")]
        cact128 = V(P1O, F32, [8, 128]); b_c128 = Buf("c128")
        dtmpm = [V(P1O + 4096 + 512 * i, F32, [128]) for i in range(2)]; b_dtm = [Buf(), Buf()]
        modraw = V(P1O + 5120, F32, [48]); b_mraw = Buf("modraw")
        cp("dve", cact128, cact2[:, :, 0:1].broadcast_to([128, 8, 128]), [b_small], [b_c128])
        for pc in range(12):
            sl = pc % 2
            dma("sp", stg[sl], wada_d[:, pc * 512:(pc + 1) * 512].rearrange("(k p) n -> p k n", p=128),
                [], [b_stg[sl]])
            ps, bps = nextbank()
            for k in range(8):
                mm(ps, cact128[:, k, :], stg[sl][:, k, :], k == 0, k == 7, [b_stg[sl], b_c128], [bps], inc=(k == 7))
            for j in range(4):
                cc = pc * 4 + j
                di = cc % 2
                tt("dve", dtmpm[di], ps[:, j * 128:(j + 1) * 128], ident_f, ALU.mult, [bps, b_const], [b_dtm[di]])
                S.op("dve", lambda e, cc=cc, di=di: e.reduce_sum(out=modraw[:, cc:cc + 1], in_=dtmpm[di],
                                                                  axis=mybir.AxisListType.X),
                     reads=[b_dtm[di]], writes=[b_mraw])
        tt("dve", modT, modraw, badaT, ALU.add, [b_mraw, b_small], [b_mod])
        stt("dve", a1, modT[:, 8:16], 1.0, nmix, ALU.add, ALU.mult, [b_mod, b_small], [b_mod])
        stt("dve", a2, modT[:, 32:40], 1.0, nmlp, ALU.add, ALU.mult, [b_mod, b_small], [b_mod])
        sh1 = modT[:, 0:8]; g1v = modT[:, 16:24]; sh2 = modT[:, 24:32]; g2v = modT[:, 40:48]
        S.barrier()

        stat_rr = [0]

        def rstd_of(ss_ap, n, R):
            act(ss_ap, ss_ap, AF.Ln, R, R, bias=EPS, scale=1.0 / n)
            act(ss_ap, ss_ap, AF.Exp, R, R, scale=-0.5)
            return ss_ap

        stat_bufs = [Buf("stat%d" % i) for i in range(16)]

        def new_stat():
            i = stat_rr[0]
            stat_rr[0] = (i + 1) % 16
            return stat[:, i:i + 1], stat_bufs[i]

        w704 = V(TR, BF16, [8, 704]); b_w704 = Buf("w704")
        xt = [V(TR + 11264 + 4096 * i, F32, [1024]) for i in range(2)]; b_xt = [Buf(), Buf()]
        xn = [V(TR + 19456 + 4096 * i, F32, [1024]) for i in range(2)]; b_xn = [Buf(), Buf()]
        hT = V(TR + 27648, BF16, [8, 512]); b_hT = [[Buf() for _ in range(8)] for _ in range(4)]
        junk = V(TR + 35840, BF16, [1024]); b_junk = Buf("junk")
        cqkv = [V(TR + 37888 + 2560 * i, F32, [640]) for i in range(2)]; b_cqkv = [Buf(), Buf()]
        krr = [V(TR + 43008 + 512 * i, F32, [128]) for i in range(2)]; b_krr = [Buf(), Buf()]
        rtmp = [V(TR + 44032 + 128 * i, F32, [32]) for i in range(4)]; b_rtmp = Buf("rtmp")

        dma("pool", w704, win_d[:, 0:704].rearrange("(k p) n -> p k n", p=128), [], [b_w704])

        pspec = {}
        plist = []

        def addp(key, src2d, kch, ncols):
            pspec[key] = (len(plist), kch, ncols)
            plist.append((key, src2d, kch, ncols))

        addp(("ks",), win_d[:, 1728:1984], 8, 256)
        addp(("vs",), win_d[:, 1984:2240], 8, 256)
        for pc in range(4):
            addp(("qs", pc), win_d[:, 704 + pc * 256:704 + (pc + 1) * 256], 8, 256)
        for c in range(8):
            addp(("g0", c), win_d[:, 2240 + c * 128:2240 + (c + 1) * 128], 8, 128)
            addp(("g1", c), win_d[:, 3264 + c * 128:3264 + (c + 1) * 128], 8, 128)
            addp(("oa", c), womla_d[:, c * 128:(c + 1) * 128], 8, 128)
            addp(("ob", c), woswa_d[:, c * 128:(c + 1) * 128], 8, 128)
        for cq in range(4):
            addp(("wo", cq), wout_d[:, cq * 256:(cq + 1) * 256], 8, 256)
        for pc in range(16):
            addp(("f1", pc), wff1_d[:, pc * 256:(pc + 1) * 256], 8, 256)
        for hf in range(2):
            for g in range(8):
                addp(("f2", hf, g), wff2_d[g * 512:(g + 1) * 512, hf * 512:(hf + 1) * 512], 4, 512)
        assert len(plist) == NPIECE
        b_wsc = [Buf("wsc%d" % i) for i in range(NPIECE)]
        pre_rr = [0]

        def precast_some(n):
            for _ in range(n):
                i = pre_rr[0]
                if i >= NPIECE:
                    return
                pre_rr[0] = i + 1
                key, src2d, kch, ncols = plist[i]
                dma("pool", wsc[i][:, 0:kch * ncols].rearrange("p (k n) -> p k n", k=kch),
                    src2d.rearrange("(k p) n -> p k n", p=128), [], [b_wsc[i]])

        def make_nt(ring, b_ring, fixed_banks=None):
            rr = [0]

            def front(src, bsrc, load=None):
                i = rr[0]
                rr[0] = (i + 1) % len(ring)
                if load is not None:
                    dma("sp", ring[i], load, [], [b_ring[i]])
                    src, bsrc = ring[i], b_ring[i]
                ss, bss = new_stat()
                act(junk, src, AF.Square, [bsrc], [b_junk, bss], accum=ss)
                rstd_of(ss, D, [bss])
                ts("dve", ring[i], src, ss, None, ALU.mult, None, [bsrc, bss], [b_ring[i]])
                return i

            def back(i, avec, shvec, dst_fn, bdst):
                for hf in range(2):
                    if fixed_banks is None:
                        ps, bps = nextbank()
                    else:
                        ps, bps = banks[fixed_banks[hf]], bank_b[fixed_banks[hf]]
                    for j in range(4):
                        k = hf * 4 + j
                        tp(ps[:, j * 128:(j + 1) * 128], ring[i][:, k * 128:(k + 1) * 128], [b_ring[i]], [bps],
                           inc=(j == 3))
                    for j in range(4):
                        k = hf * 4 + j
                        bd = bdst[k] if isinstance(bdst, list) else bdst
                        if hf == 0:
                            act(dst_fn(k), ps[:, j * 128:(j + 1) * 128], AF.Identity, [bps, b_mod], [bd],
                                bias=shvec[:, k:k + 1], scale=avec[:, k:k + 1])
                        else:
                            ts("dve", dst_fn(k), ps[:, j * 128:(j + 1) * 128], avec[:, k:k + 1], shvec[:, k:k + 1],
                               ALU.mult, ALU.add, [bps, b_mod], [bd])

            return front, back

        p1_front, p1_back = make_nt([xt[0], xt[1], xn[0], xn[1]], [b_xt[0], b_xt[1], b_xn[0], b_xn[1]], fixed_banks=(0, 1))
        p1_slot = {}
        p1_ps = {}

        def p1_A(t):
            p1_slot[t] = p1_front(None, None, load=x_d[t * 128:(t + 1) * 128, :])

        def p1_B(t):
            tl = t % 4
            p1_back(p1_slot.pop(t), a1, sh1, lambda k, tl=tl: hT[:, k, tl * 128:(tl + 1) * 128], b_hT[tl])
            ia = 2 + 2 * (t % 2)
            psA, bA, psB, bB = banks[ia], bank_b[ia], banks[ia + 1], bank_b[ia + 1]
            for k in range(8):
                mm(psA[:, 0:384], hT[:, k, tl * 128:(tl + 1) * 128], w704[:, k, 0:384], k == 0, k == 7,
                   [b_hT[tl], b_w704], [bA], inc=(k == 7))
            for k in range(8):
                mm(psB[:, 0:320], hT[:, k, tl * 128:(tl + 1) * 128], w704[:, k, 384:704], k == 0, k == 7,
                   [b_hT[tl], b_w704], [bB], inc=(k == 7))
            p1_ps[t] = (psA, bA, psB, bB)

        def p1_C(t):
            sl = t % 2
            psA, bA, psB, bB = p1_ps.pop(t)
            ssq, bq = new_stat()
            act(junk[:, 0:384], psA[:, 0:384], AF.Square, [bA], [b_junk, bq], accum=ssq)
            rstd_of(ssq, 384, [bq])
            sskv, bkv = new_stat()
            act(junk[:, 0:256], psB[:, 0:256], AF.Square, [bB], [b_junk, bkv], accum=sskv)
            rstd_of(sskv, 256, [bkv])
            ts("dve", cqkv[sl][:, 0:384], psA[:, 0:384], ssq, None, ALU.mult, None, [bA, bq], [b_cqkv[sl]])
            ts("dve", cqkv[sl][:, 384:640], psB[:, 0:256], sskv, None, ALU.mult, None, [bB, bkv], [b_cqkv[sl]])
            x1_ = psB[:, 256:288]; x2_ = psB[:, 288:320]
            ct = cos_t[:, t, :]; sn = sin_t[:, t, :]
            tt("dve", rtmp[0], x1_, ct, ALU.mult, [bB, b_trig], [b_rtmp])
            tt("dve", rtmp[1], x2_, sn, ALU.mult, [bB, b_trig], [b_rtmp])
            tt("dve", rtmp[2], x2_, ct, ALU.mult, [bB, b_trig], [b_rtmp])
            tt("dve", rtmp[3], x1_, sn, ALU.mult, [bB, b_trig], [b_rtmp])
            tt("dve", krr[sl][:, 0:32], rtmp[0], rtmp[1], ALU.subtract, [b_rtmp], [b_krr[sl]])
            tt("dve", krr[sl][:, 32:64], rtmp[2], rtmp[3], ALU.add, [b_rtmp], [b_krr[sl]])
            cp("dve", krr[sl][:, 64:128], krr[sl][:, 0:64], [b_krr[sl]], [b_krr[sl]])
            psT, bT = banks[6], bank_b[6]
            for j in range(3):
                tp(psT[:, j * 128:(j + 1) * 128], cqkv[sl][:, j * 128:(j + 1) * 128], [b_cqkv[sl]], [bT], inc=(j == 2))
            psU, bU = banks[7], bank_b[7]
            for j in range(2):
                tp(psU[:, j * 128:(j + 1) * 128], cqkv[sl][:, 384 + j * 128:384 + (j + 1) * 128], [b_cqkv[sl]], [bU],
                   inc=False)
            tp(psU[:, 256:384], krr[sl], [b_krr[sl]], [bU], inc=True)
            tok = slice(t * 128, (t + 1) * 128)
            for j in range(3):
                ts("dve", cqnT[:, j, tok], psT[:, j * 128:(j + 1) * 128], qn[:, j:j + 1], None, ALU.mult, None,
                   [bT, b_small], [b_cqn[t]])
            for j in range(2):
                act(ckvnT[:, j, tok], psU[:, j * 128:(j + 1) * 128], AF.Identity, [bU, b_small], [b_ckvn[t]],
                    scale=kvn[:, j:j + 1])
            cp("act", KrT[:, tok], psU[:, 256:384], [bU], [b_kr[t]])

        p1_A(0)
        p1_A(1)
        for n in range(1, NT + 2):
            if 0 <= n - 1 < NT:
                p1_B(n - 1)
            if 0 <= n - 2 < NT:
                p1_C(n - 2)
            if n + 1 < NT:
                p1_A(n + 1)
        dump("cqnT", cqnT[:, :, 0:512], b_cqn[0:4])
        dump("ckvnT", ckvnT[:, :, 0:512], b_ckvn[0:4])
        dump("KrT", KrT[:, 0:512], b_kr[0:4])
        S.barrier()

        KT = [V(TR + 8192 * i, BF16, [SEQ]) for i in range(2)]
        Vt = [V(TR + 16384 + 8192 * i, BF16, [NT, 128]) for i in range(2)]
        QTn = [V(TR + 32768 + 8192 * i, BF16, [SEQ]) for i in range(2)]
        QTr = V(TR + 49152, BF16, [SEQ])
        ropeT = [V(TR + 57344 + 2048 * i, F32, [4, 2, 32]) for i in range(4)]
        b_KT = [[Buf() for _ in range(NB)] for _ in range(2)]
        b_V = [[Buf() for _ in range(NB)] for _ in range(2)]
        b_QTn = [[Buf() for _ in range(NB)] for _ in range(2)]
        b_QTr = [[Buf() for _ in range(NB)] for _ in range(2)]
        b_ropeT = Buf("ropeT")
        wqn = [V(TR + 65536 + 1792 * i, BF16, [3, 128]) for i in range(2)]
        wk = [V(TR + 65536 + 1792 * i + 768, BF16, [2, 128]) for i in range(2)]
        wv = [V(TR + 65536 + 1792 * i + 1280, BF16, [2, 128]) for i in range(2)]
        b_wh = [Buf(), Buf()]
        wqr = V(TR + 69120, BF16, [3, 2, 64]); b_wqr = Buf("wqr")
        qrot = V(TR + 69120 + 768, F32, [4, 2, 2, 32])
        PT = [V(TR + 70656 + 1024 * i, BF16, [512]) for i in range(4)] + [V(TR + 57344 + 7168, BF16, [512])]
        b_PT = [Buf() for _ in range(5)]
        recs = [V(TR + 74752, F32, [512])] * 2
        b_recs = [Buf("rec")] * 2
        sc_rr = [0]
        ropeT = [V(TR + 57344 + 1024 * i, F32, [4, 2, 32]) for i in range(3)]
        qrot = V(TR + 57344 + 3072, F32, [4, 2, 2, 32])
        b_qrot = Buf("qrot")
        QTrz = [V(TR + 57344 + 5120 + 1024 * i, BF16, [512]) for i in range(2)]
        b_QTrz = [Buf(), Buf()]
        pt_rr = [0]
        ev_rr = [0]
        dacc = [V(EXTRA + 2048 * i, F32, [512]) for i in range(2)]; b_dacc = [Buf(), Buf()]
        dacc_rr = [0]

        def evac(out, in_, R, W):
            ev_rr[0] += 1
            cp("dve", out, in_, R, W)

        def load_head_weights(h):
            sl = h % 2
            dma("pool", wqn[sl], wuq_d[:, h * 192:h * 192 + 128].rearrange("(k p) n -> p k n", p=128), [], [b_wh[sl]])
            dma("pool", wk[sl], wukv_d[:, h * 256:h * 256 + 128].rearrange("(k p) n -> p k n", p=128), [], [b_wh[sl]])
            dma("pool", wv[sl], wukv_d[:, h * 256 + 128:h * 256 + 256].rearrange("(k p) n -> p k n", p=128), [],
                [b_wh[sl]])
            dma("pool", wqr[:, :, sl, :],
                wuq_d[:, h * 192 + 128:h * 192 + 192].rearrange("(k p) n -> p k n", p=128), [], [b_wh[sl]])

        def prod_steps(h, bank):
            sl = h % 2
            steps = []

            def getbank():
                if bank is None:
                    return nextbank()
                return banks[bank], bank_b[bank]

            def mk_q(g):
                def f():
                    cols = slice(g * 512, (g + 1) * 512)
                    ps, bps = getbank()
                    for kc in range(3):
                        mm(ps, wqn[sl][:, kc, :], cqnT[:, kc, cols], kc == 0, kc == 2,
                           [b_wh[sl]] + b_cqn[4 * g:4 * g + 4], [bps], inc=(kc == 2))
                    evac(QTn[sl][:, cols], ps, [bps], [b_QTn[sl][g]])
                return f

            def mk_k(g):
                def f():
                    cols = slice(g * 512, (g + 1) * 512)
                    ps, bps = getbank()
                    for kc in range(2):
                        mm(ps, wk[sl][:, kc, :], ckvnT[:, kc, cols], kc == 0, kc == 1,
                           [b_wh[sl]] + b_ckvn[4 * g:4 * g + 4], [bps], inc=(kc == 1))
                    evac(KT[sl][:, cols], ps, [bps], [b_KT[sl][g]])
                return f

            def mk_v(g):
                def f():
                    ps, bps = getbank()
                    for tl in range(4):
                        t = 4 * g + tl
                        for kc in range(2):
                            mm(ps[:, tl * 128:(tl + 1) * 128], ckvnT[:, kc, t * 128:(t + 1) * 128], wv[sl][:, kc, :],
                               kc == 0, kc == 1, [b_wh[sl], b_ckvn[t]], [bps], inc=(kc == 1 and tl == 3))
                    evac(Vt[sl][:, 4 * g:4 * g + 4, :].rearrange("p t d -> p (t d)"), ps, [bps], [b_V[sl][g]])
                return f

            e_ = h % 2

            def mk_ra(g):
                def f():
                    ps, bps = getbank()
                    for tl in range(4):
                        t = 4 * g + tl
                        for kc in range(3):
                            mm(ps[:, tl * 64:(tl + 1) * 64], cqnT[:, kc, t * 128:(t + 1) * 128], wqr[:, kc, e_, :],
                               kc == 0, kc == 2, [b_cqn[t], b_wh[sl]], [bps], inc=(kc == 2 and tl == 3))
                    psv = ps[:, 0:256].rearrange("p (t f i) -> p t f i", t=4, f=2)
                    cb = cos_t[:, 4 * g:4 * g + 4, :]
                    sb_ = sin_t[:, 4 * g:4 * g + 4, :]
                    x1_ = psv[:, :, 0, :]; x2_ = psv[:, :, 1, :]
                    r0 = ropeT[0][:, :, 0, :]; r1 = ropeT[1][:, :, 0, :]
                    tt("dve", r0, x1_, cb, ALU.mult, [bps, b_trig], [b_ropeT])
                    tt("dve", r1, x2_, sb_, ALU.mult, [bps, b_trig], [b_ropeT])
                    tt("dve", qrot[:, :, e_, 0, :], r0, r1, ALU.subtract, [b_ropeT], [b_qrot])
                    tt("dve", r0, x2_, cb, ALU.mult, [bps, b_trig], [b_ropeT])
                    tt("dve", r1, x1_, sb_, ALU.mult, [bps, b_trig], [b_ropeT])
                    tt("dve", qrot[:, :, e_, 1, :], r0, r1, ALU.add, [b_ropeT], [b_qrot])
                return f

            def mk_rb(g):
                def f():
                    ps2, bps2 = getbank()
                    qflat = qrot.rearrange("p t h f i -> p (t h f i)")
                    for tl in range(4):
                        tp(ps2[:, tl * 128:(tl + 1) * 128], qflat[:, tl * 128:(tl + 1) * 128], [b_qrot], [bps2],
                           inc=(tl == 3))
                    evac(QTr[e_ * 64:(e_ + 1) * 64, g * 512:(g + 1) * 512], ps2[e_ * 64:(e_ + 1) * 64, :], [bps2],
                         [b_QTr[e_][g]])
                return f

            for g in range(NB):
                steps += [mk_ra(g), mk_q(g), mk_rb(g), mk_k(g), mk_v(g)]
            return steps

        for hp in range(4):
            for e in range(2):
                h = 2 * hp + e
                sl = e
                if h == 0:
                    load_head_weights(0)
                    for st_ in prod_steps(0, None):
                        st_()
                nxt_prod = []
                if h + 1 < 8:
                    load_head_weights(h + 1)
                    nxt_prod = prod_steps(h + 1, 3)
                precast_some(11)
                for zi in range(2):
                    memset("pool", QTrz[zi][(1 - e) * 64:(2 - e) * 64, :], 0.0, [b_QTrz[zi]])
                LA = 3
                items = [(qb, kc) for qb in range(NB) for kc in range(NT)]
                accs = {}
                pend = {}
                dst = {}
                deferred = []

                def acc_of(qb):
                    if qb not in accs:
                        a = (qb % 2) * 2
                        accs[qb] = (banks[a], bank_b[a], banks[1], bank_b[1])
                    return accs[qb]

                def scores(qb, kc, sl=sl, e=e):
                    qc = slice(qb * 512, (qb + 1) * 512)
                    zi = qb % 2
                    if kc == 0:
                        cp("pool", QTrz[zi][e * 64:(e + 1) * 64, :], QTr[e * 64:(e + 1) * 64, qc], [b_QTr[e][qb]],
                           [b_QTrz[zi]])
                    si = 4 + sc_rr[0]
                    sc_rr[0] = (sc_rr[0] + 1) % 4
                    ps, bps = banks[si], bank_b[si]
                    kcs = slice(kc * 128, (kc + 1) * 128)
                    mm(ps, KT[sl][:, kcs], QTn[sl][:, qc], True, False,
                       [b_KT[sl][kc // 4], b_QTn[sl][qb]], [bps], inc=False)
                    mm(ps, KrT[:, kcs], QTrz[zi], False, True, [b_kr[kc], b_QTrz[zi]], [bps], inc=True)
                    i = pt_rr[0]
                    pt_rr[0] = (i + 1) % len(PT)
                    act(PT[i], ps, AF.Exp, [bps], [b_PT[i]], scale=MLA_SCALE)
                    pend[(qb, kc)] = i

                def pv(qb, kc, sl=sl):
                    accO, bO, accD, bD = acc_of(qb)
                    i = pend.pop((qb, kc))
                    on_pe = False
                    mm(accO, Vt[sl][:, kc, :], PT[i], kc == 0, kc == NT - 1, [b_V[sl][kc // 4], b_PT[i]], [bO],
                       inc=not on_pe)
                    if on_pe:
                        mm(accD, ones_bf, PT[i], kc == 7, False, [b_const, b_PT[i]], [bD], inc=True)
                    else:
                        d = dst.setdefault(qb, {"n": 0, "used": [False, False]})
                        j = d["n"] % 2
                        d["n"] += 1
                        if not d["used"][j]:
                            d["used"][j] = True
                            cp("dve", dacc[j], PT[i], [b_PT[i]], [b_dacc[j]])
                        else:
                            tt("dve", dacc[j], dacc[j], PT[i], ALU.add, [b_dacc[j], b_PT[i]], [b_dacc[j]])

                def epi_pe(qb):
                    accO, bO, accD, bD = acc_of(qb)
                    ri = qb % 2
                    mm(accD, ones_f, recs[ri], True, True, [b_const, b_recs[ri]], [bD], inc=True)

                def epi_dve(qb, h=h):
                    accO, bO, accD, bD = acc_of(qb)
                    ri = qb % 2
                    qc = slice(qb * 512, (qb + 1) * 512)
                    act(recs[ri], accD, AF.Ln, [bD], [b_recs[ri]])
                    act(recs[ri], recs[ri], AF.Exp, [b_recs[ri]], [b_recs[ri]], scale=-1.0)
                    tt("dve", OT[:, h, qc], accO, recs[ri], ALU.mult, [bO, b_recs[ri]], [b_OT[h][qb]])

                for n in range(min(LA, len(items))):
                    scores(*items[n])
                for n, (qb, kc) in enumerate(items):
                    pv(qb, kc)
                    if n + LA < len(items):
                        scores(*items[n + LA])
                    if kc == NT - 1:
                        ri = qb % 2
                        tt("dve", recs[ri], dacc[0], dacc[1], ALU.add, [b_dacc[0], b_dacc[1]], [b_recs[ri]])
                        deferred.append(qb)
                    if kc == 3 and deferred:
                        epi_pe(deferred[0])
                    if kc == 5 and deferred:
                        epi_dve(deferred.pop(0))
                    if n % 6 == 4 and nxt_prod:
                        nxt_prod.pop(0)()
                for qb in deferred:
                    epi_pe(qb)
                    epi_dve(qb)
                while nxt_prod:
                    nxt_prod.pop(0)()
        dump("OT", OT[:, :, 0:512].bitcast(BF16) if False else OT[:, :, 0:512], [b_OT[h][0] for h in range(8)])
        S.barrier()

        P3 = P1O
        G1b = V(P3, F32, [1024]); G2b = V(P3 + 4096, F32, [1024]); gFb = V(P3 + 8192, F32, [1024])
        Bm = V(P3 + 12288, F32, [3, 8, 128])
        b_G = Buf("G"); b_Bm = Buf("Bm")
        hTe = V(P3 + 24576, BF16, [8, 768]); b_hTe = [[Buf() for _ in range(8)] for _ in range(6)]
        U = P3 + 36864
        ksT = V(U, BF16, [2, 768]); b_ks = Buf("ks")
        vs = V(U + 3072, BF16, [6, 256]); b_vs = [Buf() for _ in range(6)]
        qsT = V(U + 6144, BF16, [8, 512]); b_qs = [Buf() for _ in range(8)]
        swaOT = V(U + 14336, BF16, [8, 512]); b_swaO = [[Buf() for _ in range(4)] for _ in range(2)]
        mergedT = V(U + 22528, BF16, [8, 512]); b_mg = [Buf() for _ in range(8)]
        uT = V(U, BF16, [32, 512]); b_uT = [Buf() for _ in range(32)]
        u_old = [b_ks] + b_vs + b_qs + b_swaO[0] + b_swaO[1] + b_mg
        XB = U + 32768
        xb = [V(XB + 4096 * i, F32, [1024]) for i in range(2)]; b_xb = [Buf(), Buf()]
        x1 = V(XB + 8192, F32, [4, 1024]); b_x1 = [Buf() for _ in range(4)]
        PT3 = [V(XB + 24576 + 1024 * i, BF16, [4, 128]) for i in range(3)]; b_PT3 = [Buf() for _ in range(3)]
        WR = XB + 27648
        b_wrh = [Buf() for _ in range(8)]
        b_wr = [[b_wrh[0], b_wrh[1]]]
        xn3 = V(WR + 16384, F32, [1024]); b_xn3 = Buf("xn3")
        sfp = [V(WR + 20480 + 2048 * i, F32, [512]) for i in range(3)]; b_sfp = [Buf() for _ in range(3)]
        junk3 = V(WR + 26624, BF16, [1024])
        dtmp = xn3[:, 0:512]; b_dtmp = b_xn3
        for i_ in range(4):
            PT3.append(V(EXTRA + 1024 * i_, BF16, [4, 128])); b_PT3.append(Buf())
        swb_rr = [0]; sfp_rr = [0]; pt3_rr = [0]
        assert WR + 26624 + 2048 <= ARENA_BYTES
        junk = junk3
        wr_rr = [0]

        def wpiece(*key):
            idx, kch, ncols = pspec[key]
            i = wr_rr[0]
            nh = 1 if kch * ncols <= 1024 else 2
            if nh == 2 and i % 2 == 1:
                i += 1
            i %= 8
            wr_rr[0] = (i + nh) % 8
            bw = b_wrh[i:i + nh]
            v = V(WR + 2048 * i, BF16, [kch, ncols])
            dma("pool", v, wsc[idx][:, 0:kch * ncols].rearrange("p (k n) -> p k n", k=kch), [b_wsc[idx]], bw)
            return v, bw

        dg = [V(WR + 20480 + 2048 * i, F32, [128]) for i in range(2)]
        for gi, (gvec, Gb, bsrc) in enumerate(((g1v, G1b, b_mod), (g2v, G2b, b_mod), (nfin, gFb, b_small))):
            for hf in range(2):
                ps, bps = nextbank()
                for j in range(4):
                    c = hf * 4 + j
                    d = dg[c % 2]
                    bd = b_sfp[c % 2]
                    ts("dve", d, ident_f, gvec[:, c:c + 1], None, ALU.mult, None, [b_const, bsrc], [bd])
                    mm(ps[:, j * 128:(j + 1) * 128], ones_f, d, True, True, [b_const, bd], [bps], inc=True)
                cp("dve", Gb[:, hf * 512:(hf + 1) * 512], ps, [bps], [b_G])
        ps, bps = nextbank()
        mm(ps[0:8, :], rb_aug, oht, True, True, [b_small], [bps], inc=True)
        b_tbl = Buf("tbl")
        cp("dve", tbl_sb, ps[0:8, :], [bps], [b_tbl])
        b_tbld = Buf("tbld")
        dma("sp", tbl_t.ap(), tbl_sb, [b_tbl], [b_tbld])
        hank = V(WR, F32, [8, 128])
        for dl in range(3):
            src = bass.AP(tensor=tbl_t, offset=dl * 128, ap=[[1, 128], [512, 8], [1, 128]])
            dma("sp", hank, src, [b_tbld], [b_wr[0]])
            for hh in range(2):
                ps, bps = nextbank()
                for j in range(4):
                    h = hh * 4 + j
                    mm(ps[:, j * 128:(j + 1) * 128], hank[:, h, :], jrev_f, True, True, [b_wr[0], b_const], [bps],
                       inc=(j == 3))
                cp("dve", Bm[:, dl, hh * 4:(hh + 1) * 4, :].rearrange("p h q -> p (h q)"), ps, [bps], [b_Bm])
        dump("Bm", Bm.rearrange("p a h q -> p (a h q)"), [b_Bm])
        dump("G1b", G1b, [b_G])

        nt_front, nt_back = make_nt([xb[0], xb[1], xn3], [b_xb[0], b_xb[1], b_xn3])

        def hext_steps(b):
            valid = [j for j in range(6) if 0 <= 4 * b - 1 + j < NT]
            slot = {}
            steps = []

            def mk_front(j):
                def f():
                    te = 4 * b - 1 + j
                    slot[j] = nt_front(None, None, load=x_d[te * 128:(te + 1) * 128, :])
                return f

            def mk_back(j):
                def f():
                    nt_back(slot[j], a1, sh1, lambda k, j=j: hTe[:, k, j * 128:(j + 1) * 128], b_hTe[j])
                return f

            for n, j in enumerate(valid):
                steps.append(mk_front(j))
                if n >= 1:
                    steps.append(mk_back(valid[n - 1]))
            steps.append(mk_back(valid[-1]))
            return steps

        def act_recip(buf, src, R, W):
            act(buf, src, AF.Ln, R, W)
            act(buf, buf, AF.Exp, W, W, scale=-1.0)

        for st_ in hext_steps(0):
            st_()
        for b in range(NB):
            S.alias(u_old, b_uT)
            valid = [j for j in range(6) if 0 <= 4 * b - 1 + j < NT]
            own = slice(128, 640)
            b_own = b_hTe[1:5]
            for tl in range(4):
                te = 4 * b + tl
                dma("sp", x1[:, tl, :], x_d[te * 128:(te + 1) * 128, :], [], [b_x1[tl]])
            wp, bwp = wpiece("ks")
            for kv in range(2):
                for (j0, j1) in ((0, 4), (4, 6)):
                    js = [j for j in valid if j0 <= j < j1]
                    if not js:
                        continue
                    cs = slice(js[0] * 128, (js[-1] + 1) * 128)
                    n = (js[-1] + 1 - js[0]) * 128
                    ps, bps = nextbank()
                    for k in range(8):
                        mm(ps[:, 0:n], wp[:, k, kv * 128:(kv + 1) * 128], hTe[:, k, cs], k == 0, k == 7,
                           [bwp] + [b_hTe[j] for j in js], [bps], inc=(k == 7))
                    cp("act", ksT[:, kv, cs], ps[:, 0:n], [bps], [b_ks])
            wp, bwp = wpiece("vs")
            for j in valid:
                ps, bps = nextbank()
                for k in range(8):
                    mm(ps[:, 0:256], hTe[:, k, j * 128:(j + 1) * 128], wp[:, k, :], k == 0, k == 7,
                       [bwp, b_hTe[j]], [bps], inc=(k == 7))
                cp("act", vs[:, j, :], ps[:, 0:256], [bps], [b_vs[j]])
            for pc in range(4):
                wp, bwp = wpiece("qs", pc)
                for hh in range(2):
                    h = pc * 2 + hh
                    ps, bps = nextbank()
                    for k in range(8):
                        mm(ps, wp[:, k, hh * 128:(hh + 1) * 128], hTe[:, k, own], k == 0, k == 7,
                           [bwp] + b_own, [bps], inc=(k == 7))
                    cp("act", qsT[:, h, :], ps, [bps], [b_qs[h]])
            units = [(qt, kv) for qt in range(4) for kv in range(2)]
            sw = {}

            def swa_scores(u):
                qt, kv = units[u]
                j = qt + 1
                hs = slice(kv * 4, (kv + 1) * 4)
                dls = [dl for dl in (-1, 0, 1) if (j + dl) in valid]
                pts = []
                for dl in dls:
                    jk = j + dl
                    bi = 4 + swb_rr[0]
                    swb_rr[0] = (swb_rr[0] + 1) % 4
                    ps, bps = banks[bi], bank_b[bi]
                    psv = ps.rearrange("p (h q) -> p h q", h=4)
                    mm(psv, ksT[:, kv, jk * 128:(jk + 1) * 128], qsT[:, hs, qt * 128:(qt + 1) * 128], True, True,
                       [b_ks] + b_qs[kv * 4:(kv + 1) * 4], [bps], inc=True)
                    si = sfp_rr[0]
                    sfp_rr[0] = (si + 1) % 3
                    pi = pt3_rr[0]
                    pt3_rr[0] = (pi + 1) % len(PT3)
                    sv_ = sfp[si].rearrange("p (h q) -> p h q", h=4)
                    stt("dve", sv_, psv, SWA_SCALE, Bm[:, dl + 1, hs, :], ALU.mult, ALU.add, [bps, b_Bm],
                        [b_sfp[si]])
                    act(PT3[pi], sv_, AF.Exp, [b_sfp[si]], [b_PT3[pi]])
                    pts.append((jk, pi))
                sw[u] = pts

            def swa_pv(u):
                qt, kv = units[u]
                hs = slice(kv * 4, (kv + 1) * 4)
                a = (u % 2) * 2
                accO, bO, accD, bD = banks[a], bank_b[a], banks[a + 1], bank_b[a + 1]
                pts = sw.pop(u)
                for n_i, (jk, pi) in enumerate(pts):
                    first = (n_i == 0)
                    last = (n_i == len(pts) - 1)
                    ptf = PT3[pi].rearrange("p h q -> p (h q)")
                    mm(accO, vs[:, jk, kv * 128:(kv + 1) * 128], ptf, first, last, [b_vs[jk], b_PT3[pi]], [bO],
                       inc=False)
                    mm(accD, ones_bf, ptf, first, last, [b_const, b_PT3[pi]], [bD], inc=True)
                dv = dtmp.rearrange("p (h q) -> p h q", h=4)
                tt("dve", dv, accD.rearrange("p (h q) -> p h q", h=4),
                   sinkexp[:, hs].unsqueeze(2).broadcast_to([128, 4, 128]), ALU.add, [bD, b_small], [b_dtmp])
                act_recip(dtmp, dtmp, [b_dtmp], [b_dtmp])
                tt("dve", swaOT[:, hs, qt * 128:(qt + 1) * 128], accO.rearrange("p (h q) -> p h q", h=4), dv,
                   ALU.mult, [bO, b_dtmp], [b_swaO[kv][qt]])

            reserved.update(range(4))
            swa_scores(0)
            for u in range(len(units)):
                if u + 1 < len(units):
                    swa_scores(u + 1)
                swa_pv(u)
            reserved.clear()
            if b == 0:
                dump("swaOT", swaOT, b_swaO[0] + b_swaO[1])
            for c in range(8):
                if True:
                    pg0, bpg0 = wpiece("g0", c)
                    pg1, bpg1 = wpiece("g1", c)
                    pa, bpa = wpiece("oa", c)
                    pb, bpb = wpiece("ob", c)
                    ccs = slice(0, 128)
                    ps_g0, bg0 = nextbank()
                    for k in range(8):
                        mm(ps_g0, pg0[:, k, ccs], hTe[:, k, own], k == 0, k == 7, [bpg0] + b_own, [bg0], inc=(k == 7))
                    ps_g1, bg1 = nextbank()
                    for k in range(8):
                        mm(ps_g1, pg1[:, k, ccs], hTe[:, k, own], k == 0, k == 7, [bpg1] + b_own, [bg1], inc=(k == 7))
                    ps_a, ba = nextbank()
                    for h in range(8):
                        mm(ps_a, pa[:, h, ccs], OT[:, h, b * 512:(b + 1) * 512], h == 0, h == 7, [bpa, b_OT[h][b]],
                           [ba], inc=(h == 7))
                    ps_b, bb = nextbank()
                    for h in range(8):
                        mm(ps_b, pb[:, h, ccs], swaOT[:, h, :], h == 0, h == 7,
                           [bpb] + [b_swaO[h // 4][q] for q in range(4)], [bb], inc=(h == 7))
                    act(sfp[0], ps_g0, AF.Sigmoid, [bg0], [b_sfp[0]])
                    act(sfp[1], ps_g1, AF.Sigmoid, [bg1], [b_sfp[1]])
                    tt("dve", sfp[0], sfp[0], ps_a, ALU.mult, [b_sfp[0], ba], [b_sfp[0]])
                    tt("dve", sfp[1], sfp[1], ps_b, ALU.mult, [b_sfp[1], bb], [b_sfp[1]])
                    tt("dve", mergedT[:, c, :], sfp[0], sfp[1], ALU.add, [b_sfp[0], b_sfp[1]], [b_mg[c]])
            if b == 0:
                dump("mergedT", mergedT, b_mg)
            slot2 = {}
            for tp2 in range(2):
                for cq in range(4):
                    po, bpo = wpiece("wo", cq)
                    cs = slice(cq * 256, (cq + 1) * 256)
                    ps, bps = nextbank()
                    for tl2 in range(2):
                        tl = tp2 * 2 + tl2
                        for k in range(8):
                            mm(ps[:, tl2 * 256:(tl2 + 1) * 256], mergedT[:, k, tl * 128:(tl + 1) * 128], po[:, k, :],
                               k == 0, k == 7, [bpo, b_mg[k]], [bps], inc=(k == 7 and tl2 == 1))
                    for tl2 in range(2):
                        tl = tp2 * 2 + tl2
                        si = tl2
                        tt("dve", sfp[si][:, 0:256], ps[:, tl2 * 256:(tl2 + 1) * 256], G1b[:, cs], ALU.mult,
                           [bps, b_G], [b_sfp[si]])
                        tt("dve", x1[:, tl, cs], sfp[si][:, 0:256], x1[:, tl, cs], ALU.add, [b_sfp[si], b_x1[tl]],
                           [b_x1[tl]])
                if tp2 == 1:
                    for tl in (0, 1):
                        nt_back(slot2[tl], a2, sh2, lambda k, tl=tl: hTe[:, k, tl * 128:(tl + 1) * 128], b_hTe[tl])
                for tl in (tp2 * 2, tp2 * 2 + 1):
                    slot2[tl] = nt_front(x1[:, tl, :], b_x1[tl])
            for tl in (2, 3):
                nt_back(slot2[tl], a2, sh2, lambda k, tl=tl: hTe[:, k, tl * 128:(tl + 1) * 128], b_hTe[tl])
            if b == 0:
                dump("x1", x1.rearrange("p t d -> p (t d)"), b_x1)
            S.alias(b_uT, u_old)
            h2 = slice(0, 512)
            b_h2 = b_hTe[0:4]
            for pc in range(16):
                p1, bp1 = wpiece("f1", pc)
                for cc in range(2):
                    ch = pc * 2 + cc
                    ps, bps = nextbank()
                    for k in range(8):
                        mm(ps, p1[:, k, cc * 128:(cc + 1) * 128], hTe[:, k, h2], k == 0, k == 7, [bp1] + b_h2, [bps],
                           inc=(k == 7))
                    si = ch % 3
                    act(sfp[si], ps, AF.Relu, [bps], [b_sfp[si]])
                    tt("dve", uT[:, ch, :], sfp[si], sfp[si], ALU.mult, [b_sfp[si]], [b_uT[ch]])
            nxt = hext_steps(b + 1) if b + 1 < NB else []
            for hf in range(2):
                cs = slice(hf * 512, (hf + 1) * 512)
                accs = [(banks[i], bank_b[i]) for i in range(4)]
                reserved.update(range(4))
                for g in range(8):
                    p2, bp2 = wpiece("f2", hf, g)
                    for kk in range(4):
                        kc = g * 4 + kk
                        for tl in range(4):
                            mm(accs[tl][0], uT[:, kc, tl * 128:(tl + 1) * 128], p2[:, kk, :], kc == 0, kc == 31,
                               [bp2, b_uT[kc]], [accs[tl][1]], inc=(kc == 31 or (kk == 3 and tl == 3)))
                    if nxt and not (hf == 1 and g == 7):
                        nxt.pop(0)()
                reserved.clear()
                for tl in range(4):
                    si = tl % 3
                    tt("dve", sfp[si], accs[tl][0], G2b[:, cs], ALU.mult, [accs[tl][1], b_G], [b_sfp[si]])
                    tt("dve", x1[:, tl, cs], sfp[si], x1[:, tl, cs], ALU.add, [b_sfp[si], b_x1[tl]], [b_x1[tl]])
            for tl in range(4):
                te = 4 * b + tl
                ss, bss = new_stat()
                act(junk, x1[:, tl, :], AF.Square, [b_x1[tl]], [b_junk, bss], accum=ss)
                rstd_of(ss, D, [bss])
                stt("dve", x1[:, tl, :], x1[:, tl, :], ss, gFb, ALU.mult, ALU.mult, [b_x1[tl], bss, b_G], [b_x1[tl]])
                dma("sp", out_d[te * 128:(te + 1) * 128, :], x1[:, tl, :], [b_x1[tl]], [])
            while nxt:
                nxt.pop(0)()
        S.barrier()
        S.emit()
    return nc


_PROGRAM = None


def _lay(v, k):
    return np.ascontiguousarray(np.asarray(v, dtype=np.float32).reshape(k, 128).T)


def kernel(x, c, positions, w_ada, b_ada, norm_mix, w_in, q_norm, w_uq, kv_norm, w_ukv,
           rel_bias, sink, w_o_mla, w_o_swa, w_out, norm_mlp, w_ff1, w_ff2, norm_final):
    global _PROGRAM
    if _PROGRAM is None:
        _PROGRAM = build_program()
    nc = _PROGRAM
    f = lambda a: np.ascontiguousarray(np.asarray(a, dtype=np.float32))
    consts = _host_consts()
    shared = {
        "w_ada": f(w_ada[0]), "b_ada_l": _lay(b_ada[0], 48), "norm_mix_l": _lay(norm_mix[0], 8),
        "w_in": f(w_in[0]), "q_norm_l": _lay(q_norm[0], 3), "w_uq": f(w_uq[0]),
        "kv_norm_l": _lay(kv_norm[0], 2), "w_ukv": f(w_ukv[0]), "rel_bias": f(rel_bias),
        "sink": f(sink[0]), "w_o_mla": f(w_o_mla[0]), "w_o_swa": f(w_o_swa[0]), "w_out": f(w_out[0]),
        "norm_mlp_l": _lay(norm_mlp[0], 8), "w_ff1": f(w_ff1[0]), "w_ff2": f(w_ff2[0]),
        "norm_final_l": _lay(norm_final, 8),
    }
    shared.update(consts)
    x = np.asarray(x, dtype=np.float32)
    c = np.asarray(c, dtype=np.float32)
    positions = np.asarray(positions, dtype=np.int32)
    in_maps = []
    for b in range(N_CORES):
        m = dict(shared)
        m["x"] = np.ascontiguousarray(x[b])
        m["c_l"] = _lay(c[b], 8)
        m["pos_l"] = np.ascontiguousarray(positions[b].reshape(NT, 128).T)
        in_maps.append(m)
    res = run_bass_kernel_spmd(nc, in_maps, core_ids=list(range(N_CORES)))
    kernel.last_results = res
    return np.stack([np.asarray(r["out"], dtype=np.float32) for r in res.results], axis=0)
```
